# Optimizing a Trainium2 kernel written in Bass

```python
import jax, jax.numpy as jnp
from jax import lax
import numpy as np

D_MODEL = 1024
BATCH = 8
SEQ = 2048
DEPTH = 1
DEC_BATCH = 128
DEC_SEQ = 4
PAST_LEN = 16384
PAGE_SIZE = 128

N_META = 16
EPS = 1e-6
H_A = 8
DK_A = 128
DV_A = 128
D_QK_A = H_A * DK_A
D_A = H_A * DV_A
CONV_W = 4
GDN_CHUNK = 64
GDN_QKV = 2 * D_QK_A + D_A
H_B = 16
N_B = 64
D_B = H_B * N_B
LORA_W = 64
LORA_A = 64
GN_EPS = N_B * 1e-5
RWKV_SHIFT = 3 * D_B + LORA_W + LORA_A + D_B
OFF_GDN_QKV = 0
OFF_GDN_A = OFF_GDN_QKV + GDN_QKV
OFF_GDN_B = OFF_GDN_A + H_A
OFF_GDN_Z = OFF_GDN_B + H_A
OFF_RWKV = OFF_GDN_Z + D_A
OFF_GATE = OFF_RWKV + RWKV_SHIFT
D_IN = OFF_GATE + 2 * D_MODEL

kernel_name = "hybrid_gdn_rwkv7_gated_merge_step"


def rmsnorm(x, w):
    xf = x.astype(jnp.float32)
    xf = xf * lax.rsqrt(jnp.mean(xf * xf, axis=-1, keepdims=True) + EPS)
    return (xf * w.astype(jnp.float32)).astype(x.dtype)


def l2norm(x):
    return x * lax.rsqrt(jnp.sum(x * x, axis=-1, keepdims=True) + 1e-6)


def gdn_chunked(q, k, v, g, beta, s0, chunk):
    B, T, H, DK = q.shape
    DV = v.shape[-1]
    n = T // chunk

    def to_chunks(t):
        return jnp.moveaxis(t.reshape((B, n, chunk) + t.shape[2:]), 3, 2)

    q, k, v, g, beta = (to_chunks(t) for t in (q, k, v, g, beta))
    gc = jnp.cumsum(g, axis=-1)
    tril = jnp.tril(jnp.ones((chunk, chunk), dtype=bool))
    strict = jnp.tril(jnp.ones((chunk, chunk), dtype=bool), k=-1)
    diff = gc[..., :, None] - gc[..., None, :]
    decay = jnp.where(tril, jnp.exp(jnp.where(tril, diff, 0.0)), 0.0)
    k_beta = k * beta[..., None]
    v_beta = v * beta[..., None]
    lower = jnp.where(strict, jnp.einsum('bnhik,bnhjk->bnhij', k_beta, k) * decay, 0.0)
    eye = jnp.eye(chunk, dtype=q.dtype)
    t_inv = lax.linalg.triangular_solve(eye + lower, jnp.broadcast_to(eye, lower.shape),
                                        left_side=True, lower=True)
    u = jnp.einsum('bnhij,bnhjv->bnhiv', t_inv, v_beta)
    w = jnp.einsum('bnhij,bnhjk->bnhik', t_inv, k_beta * jnp.exp(gc)[..., None])
    qk = jnp.where(tril, jnp.einsum('bnhik,bnhjk->bnhij', q, k) * decay, 0.0)
    g_last = gc[..., -1]
    k_tail = k * jnp.exp(g_last[..., None] - gc)[..., None]

    def step(s, inp):
        q_c, gc_c, u_c, w_c, qk_c, gl_c, kt_c = inp
        v_new = u_c - jnp.einsum('bhck,bhkv->bhcv', w_c, s)
        o = (jnp.einsum('bhck,bhkv->bhcv', q_c * jnp.exp(gc_c)[..., None], s)
             + jnp.einsum('bhij,bhjv->bhiv', qk_c, v_new))
        s = s * jnp.exp(gl_c)[..., None, None] + jnp.einsum('bhck,bhcv->bhkv', kt_c, v_new)
        return s, o

    xs = tuple(jnp.moveaxis(t, 1, 0) for t in (q, gc, u, w, qk, g_last, k_tail))
    s, o = lax.scan(step, s0, xs)
    o = jnp.transpose(o, (1, 0, 3, 2, 4)).reshape(B, T, H, DV)
    return o, s


def rwkv7_scan(r, w_log, k, v, kk, a, s0):
    def step(s, inp):
        r_t, w_t, k_t, v_t, kk_t, a_t = inp
        sa = jnp.einsum('bhvk,bhk->bhv', s, kk_t)
        s = (s * jnp.exp(w_t)[:, :, None, :] - sa[..., None] * (kk_t * a_t)[:, :, None, :]
             + v_t[..., None] * k_t[:, :, None, :])
        y = jnp.einsum('bhvk,bhk->bhv', s, r_t)
        return s, y

    xs = tuple(jnp.moveaxis(t, 1, 0) for t in (r, w_log, k, v, kk, a))
    s, ys = lax.scan(step, s0, xs)
    return jnp.moveaxis(ys, 0, 1), s


def hybrid_layer(x, x_prev, conv_buf, s_gdn, s_rwkv, front_pad, chunk,
                 ln1_w, w_in, gdn_conv_w, gdn_a_log, gdn_dt_bias, gdn_norm_w, w_out_a,
                 rwkv_mu, rwkv_w0, rwkv_w2, rwkv_a0, rwkv_a2, rwkv_k_k, rwkv_k_a, rwkv_r_k,
                 rwkv_gn_w, rwkv_gn_b, w_out_b, w_out):
    f32 = jnp.float32
    B, T, _ = x.shape
    h = rmsnorm(x, ln1_w)
    p = jnp.einsum('btd,de->bte', h, w_in)

    qkv_raw = p[..., OFF_GDN_QKV:OFF_GDN_A]
    conv_in = jnp.concatenate([conv_buf.astype(p.dtype), qkv_raw], axis=1)
    qkv = conv_in[:, 0:T] * gdn_conv_w[0]
    for i in range(1, CONV_W):
        qkv = qkv + conv_in[:, i:i + T] * gdn_conv_w[i]
    qkv = jax.nn.silu(qkv.astype(f32))
    q = l2norm(qkv[..., :D_QK_A].reshape(B, T, H_A, DK_A)) * (DK_A ** -0.5)
    k = l2norm(qkv[..., D_QK_A:2 * D_QK_A].reshape(B, T, H_A, DK_A))
    v = qkv[..., 2 * D_QK_A:].reshape(B, T, H_A, DV_A)
    g = -jnp.exp(gdn_a_log.astype(f32)) * jax.nn.softplus(
        p[..., OFF_GDN_A:OFF_GDN_B].astype(f32) + gdn_dt_bias.astype(f32))
    beta = jax.nn.sigmoid(p[..., OFF_GDN_B:OFF_GDN_Z].astype(f32))
    pad4 = ((0, 0), (front_pad, 0), (0, 0), (0, 0))
    pad3 = ((0, 0), (front_pad, 0), (0, 0))
    o_a, s_gdn_new = gdn_chunked(jnp.pad(q, pad4), jnp.pad(k, pad4), jnp.pad(v, pad4),
                                 jnp.pad(g, pad3), jnp.pad(beta, pad3), s_gdn.astype(f32), chunk)
    o_a = o_a[:, front_pad:]
    z_a = p[..., OFF_GDN_Z:OFF_RWKV].astype(f32).reshape(B, T, H_A, DV_A)
    o_a = rmsnorm(o_a, gdn_norm_w) * jax.nn.silu(z_a)
    branch_a = jnp.einsum('bte,ed->btd', o_a.reshape(B, T, D_A).astype(x.dtype), w_out_a)

    w_b = w_in[:, OFF_RWKV:OFF_GATE]
    pb = p[..., OFF_RWKV:OFF_GATE]
    pb_first = jnp.einsum('bd,de->be', x_prev.astype(h.dtype), w_b)
    pb_prev = jnp.concatenate([pb_first[:, None].astype(pb.dtype), pb[:, :-1]], axis=1)
    mix = (pb + rwkv_mu * (pb_prev - pb)).astype(f32)
    r = mix[..., 0:D_B]
    kb = mix[..., D_B:2 * D_B]
    vb = mix[..., 2 * D_B:3 * D_B]
    wd = mix[..., 3 * D_B:3 * D_B + LORA_W]
    ad = mix[..., 3 * D_B + LORA_W:3 * D_B + LORA_W + LORA_A]
    z_b = mix[..., 3 * D_B + LORA_W + LORA_A:]
    w_raw = rwkv_w0.astype(f32) + jnp.tanh(wd) @ rwkv_w2.astype(f32)
    w_log = -jnp.exp(-jax.nn.softplus(-w_raw) - 0.5)
    a = jax.nn.sigmoid(rwkv_a0.astype(f32) + ad @ rwkv_a2.astype(f32))
    kk = l2norm((kb * rwkv_k_k.astype(f32)).reshape(B, T, H_B, N_B))
    kb = kb * (1.0 + (a - 1.0) * rwkv_k_a.astype(f32))
    r4 = r.reshape(B, T, H_B, N_B)
    k4 = kb.reshape(B, T, H_B, N_B)
    v4 = vb.reshape(B, T, H_B, N_B)
    y, s_rwkv_new = rwkv7_scan(r4, w_log.reshape(B, T, H_B, N_B), k4, v4, kk,
                               a.reshape(B, T, H_B, N_B), s_rwkv.astype(f32))
    mu = jnp.mean(y, axis=-1, keepdims=True)
    var = jnp.mean(jnp.square(y - mu), axis=-1, keepdims=True)
    yn = (y - mu) * lax.rsqrt(var + GN_EPS)
    yn = yn.reshape(B, T, D_B) * rwkv_gn_w.astype(f32) + rwkv_gn_b.astype(f32)
    bonus = jnp.sum(r4 * k4 * rwkv_r_k.astype(f32), axis=-1, keepdims=True) * v4
    o_b = (yn + bonus.reshape(B, T, D_B)) * jax.nn.silu(z_b)
    branch_b = jnp.einsum('bte,ed->btd', o_b.astype(x.dtype), w_out_b)

    gates = jax.nn.sigmoid(p[..., OFF_GATE:].astype(f32))
    merged = gates[..., :D_MODEL] * branch_a.astype(f32) + gates[..., D_MODEL:] * branch_b.astype(f32)
    x_new = x + jnp.einsum('btd,de->bte', merged.astype(x.dtype), w_out)
    new_state = (s_gdn_new.astype(x.dtype), conv_in[:, T:].astype(x.dtype),
                 s_rwkv_new.astype(x.dtype), h[:, -1])
    return x_new, new_state


def setup_inputs(seed: int = 0) -> dict:
    key = jax.random.key(seed)
    ks = jax.random.split(key, 32)
    f32 = jnp.float32
    nrm = lambda i, shape, s: (jax.random.normal(ks[i], shape, f32) * s)
    dt = jnp.exp(jax.random.uniform(ks[10], (DEPTH, H_A), f32, np.log(1e-3), np.log(1e-1)))
    return {
        "x_prompt": nrm(0, (BATCH, SEQ, D_MODEL), 1.0),
        "x_sample": nrm(1, (DEC_BATCH, DEC_SEQ, D_MODEL), 1.0),
        "state_gdn": nrm(2, (DEPTH, DEC_BATCH, H_A, DK_A, DV_A), 0.1),
        "state_gdn_conv": nrm(3, (DEPTH, DEC_BATCH, CONV_W - 1, GDN_QKV), 1.0),
        "state_rwkv": nrm(4, (DEPTH, DEC_BATCH, H_B, N_B, N_B), 0.1),
        "state_shift": nrm(5, (DEPTH, DEC_BATCH, D_MODEL), 1.0),
        "meta_tokens": nrm(6, (N_META, D_MODEL), 1.0),
        "ln1_w": 1.0 + nrm(7, (DEPTH, D_MODEL), 0.02),
        "w_in": nrm(8, (DEPTH, D_MODEL, D_IN), D_MODEL ** -0.5),
        "gdn_conv_w": nrm(9, (DEPTH, CONV_W, GDN_QKV), CONV_W ** -0.5),
        "gdn_a_log": jnp.log(jax.random.uniform(ks[11], (DEPTH, H_A), f32, 1.0, 16.0)),
        "gdn_dt_bias": dt + jnp.log(-jnp.expm1(-dt)),
        "gdn_norm_w": 1.0 + nrm(12, (DEPTH, DV_A), 0.02),
        "w_out_a": nrm(13, (DEPTH, D_A, D_MODEL), D_A ** -0.5),
        "rwkv_mu": jax.random.uniform(ks[14], (DEPTH, RWKV_SHIFT), f32, 0.0, 1.0),
        "rwkv_w0": jax.random.uniform(ks[15], (DEPTH, D_B), f32, -5.0, 0.5),
        "rwkv_w2": nrm(16, (DEPTH, LORA_W, D_B), 0.1),
        "rwkv_a0": nrm(17, (DEPTH, D_B), 0.1),
        "rwkv_a2": nrm(18, (DEPTH, LORA_A, D_B), 0.1),
        "rwkv_k_k": 0.85 + nrm(19, (DEPTH, D_B), 0.02),
        "rwkv_k_a": 1.0 + nrm(20, (DEPTH, D_B), 0.02),
        "rwkv_r_k": nrm(21, (DEPTH, H_B, N_B), 0.1),
        "rwkv_gn_w": 1.0 + nrm(22, (DEPTH, D_B), 0.02),
        "rwkv_gn_b": nrm(23, (DEPTH, D_B), 0.01),
        "w_out_b": nrm(24, (DEPTH, D_B, D_MODEL), D_B ** -0.5),
        "w_out": nrm(25, (DEPTH, D_MODEL, D_MODEL), D_MODEL ** -0.5),
        "lnf_w": 1.0 + nrm(26, (D_MODEL,), 0.02),
    }


def reference(x_prompt, x_sample, state_gdn, state_gdn_conv, state_rwkv, state_shift,
              meta_tokens, ln1_w, w_in, gdn_conv_w, gdn_a_log, gdn_dt_bias, gdn_norm_w, w_out_a,
              rwkv_mu, rwkv_w0, rwkv_w2, rwkv_a0, rwkv_a2, rwkv_k_k, rwkv_k_a, rwkv_r_k,
              rwkv_gn_w, rwkv_gn_b, w_out_b, w_out, lnf_w):
    dt = x_prompt.dtype
    bp = x_prompt.shape[0]
    xp = jnp.concatenate([jnp.broadcast_to(meta_tokens.astype(dt)[None], (bp, N_META, D_MODEL)),
                          x_prompt], axis=1)
    xs = x_sample
    front_pad = (-N_META) % GDN_CHUNK
    prompt_states = []
    sample_states = []
    for l in range(DEPTH):
        lw = (ln1_w[l], w_in[l], gdn_conv_w[l], gdn_a_log[l], gdn_dt_bias[l], gdn_norm_w[l], w_out_a[l],
              rwkv_mu[l], rwkv_w0[l], rwkv_w2[l], rwkv_a0[l], rwkv_a2[l], rwkv_k_k[l], rwkv_k_a[l],
              rwkv_r_k[l], rwkv_gn_w[l], rwkv_gn_b[l], w_out_b[l], w_out[l])
        xp, st_p = hybrid_layer(xp, jnp.zeros((bp, D_MODEL), dt),
                                jnp.zeros((bp, CONV_W - 1, GDN_QKV), dt),
                                jnp.zeros((bp, H_A, DK_A, DV_A), dt),
                                jnp.zeros((bp, H_B, N_B, N_B), dt),
                                front_pad, GDN_CHUNK, *lw)
        xs, st_s = hybrid_layer(xs, state_shift[l], state_gdn_conv[l], state_gdn[l], state_rwkv[l],
                                0, xs.shape[1], *lw)
        prompt_states.append(st_p)
        sample_states.append(st_s)
    y_prompt = rmsnorm(xp[:, N_META:], lnf_w)
    y_sample = rmsnorm(xs, lnf_w)
    new_gdn_prompt = jnp.stack([s[0] for s in prompt_states])
    new_conv_prompt = jnp.stack([s[1] for s in prompt_states])
    new_rwkv_prompt = jnp.stack([s[2] for s in prompt_states])
    new_shift_prompt = jnp.stack([s[3] for s in prompt_states])
    new_gdn_sample = jnp.stack([s[0] for s in sample_states])
    new_conv_sample = jnp.stack([s[1] for s in sample_states])
    new_rwkv_sample = jnp.stack([s[2] for s in sample_states])
    new_shift_sample = jnp.stack([s[3] for s in sample_states])
    return (y_prompt, y_sample, new_gdn_prompt, new_conv_prompt, new_rwkv_prompt, new_shift_prompt,
            new_gdn_sample, new_conv_sample, new_rwkv_sample, new_shift_sample)
```

```python
import numpy as np
from contextlib import ExitStack
import concourse.bass as bass
import concourse.mybir as mybir
from concourse.bass_utils import run_bass_kernel_spmd

F32 = mybir.dt.float32
BF16 = mybir.dt.bfloat16
AF = mybir.ActivationFunctionType
ALU = mybir.AluOpType
AX = mybir.AxisListType

ENGS = ['sp', 'act', 'dve', 'pe', 'pool']


class Op:
    __slots__ = ('eng', 'fn', 'idx', 'pos', 'deps', 'dma_sem', 'dma_ord', 'waits', 'signal', 'semval', 'vc')


class Sched:
    DMA_POOL = {'sp': 32, 'pool': 8, 'act': 8}

    def __init__(self, nc):
        self.nc = nc
        self.ops = []
        self.eng_ops = {e: [] for e in ENGS}
        self.acc = {}
        self.dma_count = {}
        self.dma_keys = []
        self.dma_rr = {}
        self.dma_last = {}

    @staticmethod
    def _box(a):
        t = a.tensor
        name = t.name
        shape = list(t.shape)
        isdram = type(t).__name__.startswith('DRam')
        off = int(a.offset)
        if isdram:
            lo = off
            hi = off
            for st, cnt in a.ap:
                if cnt > 1:
                    if st >= 0:
                        hi += st * (cnt - 1)
                    else:
                        lo += st * (cnt - 1)
            return name, (0, 0, lo, hi)
        row = 1
        for s in shape[1:]:
            row *= s
        p0 = off // row
        f0 = off % row
        pe = 0
        fe = 0
        for st, cnt in a.ap:
            if cnt <= 1 or st == 0:
                continue
            if st % row == 0:
                pe += (st // row) * (cnt - 1)
            else:
                fe += st * (cnt - 1)
        p1 = p0 + pe
        f1 = f0 + fe
        if type(t).__name__.startswith('PSum'):
            eb = 1024 if 'bfloat16' in str(t.dtype) else 512
            f0 = (f0 // eb) * eb
            f1 = (f1 // eb) * eb + eb - 1
            p0 = (p0 // 32) * 32
            p1 = (p1 // 32) * 32 + 31
        return name, (p0, p1, f0, f1)

    @staticmethod
    def _ovl(a, b):
        return not (a[1] < b[0] or b[1] < a[0] or a[3] < b[2] or b[3] < a[2])

    @staticmethod
    def _covers(a, b):
        return a[0] <= b[0] and a[1] >= b[1] and a[2] <= b[2] and a[3] >= b[3]

    def add(self, eng, fn, reads=(), writes=(), dma=None):
        op = Op()
        op.eng = eng
        op.fn = fn
        op.idx = len(self.ops)
        op.deps = set()
        op.dma_sem = dma
        op.dma_ord = None
        op.signal = False
        for a in reads:
            name, box = self._box(a)
            lst = self.acc.setdefault(name, [])
            for rec in lst:
                if rec[2] and self._ovl(rec[0], box):
                    op.deps.add(rec[1])
            lst.append([box, op.idx, False])
        for a in writes:
            name, box = self._box(a)
            lst = self.acc.setdefault(name, [])
            keep = []
            for rec in lst:
                if rec[1] == op.idx:
                    keep.append(rec)
                    continue
                if self._ovl(rec[0], box):
                    op.deps.add(rec[1])
                    if self._covers(box, rec[0]):
                        continue
                keep.append(rec)
            keep.append([box, op.idx, True])
            self.acc[name] = keep
        op.deps.discard(op.idx)
        if eng == 'pe' and dma is None:
            op.deps = set(d for d in op.deps if not (self.ops[d].eng == 'pe' and self.ops[d].dma_sem is None))
        if dma is not None:
            npool = self.DMA_POOL.get(eng, 4)
            self.dma_rr[eng] = self.dma_rr.get(eng, 0) + 1
            dma = '%s%d' % (eng, self.dma_rr[eng] % npool)
            op.dma_sem = dma
            if dma not in self.dma_count:
                self.dma_count[dma] = 0
                self.dma_keys.append(dma)
            else:
                op.deps.add(self.dma_last[dma])
            self.dma_count[dma] += 1
            op.dma_ord = self.dma_count[dma]
            self.dma_last[dma] = op.idx
        op.pos = len(self.eng_ops[eng]) + 1
        self.eng_ops[eng].append(op)
        self.ops.append(op)
        return op

    def finalize(self):
        ops = self.ops
        K = {e: {} for e in ENGS}
        dma_seen = {k: 0 for k in self.dma_keys}
        for op in ops:
            e = op.eng
            need = {}
            for d in op.deps:
                dop = ops[d]
                if dop.dma_sem is not None:
                    ch = 'dma:' + dop.dma_sem
                    cnt = dop.dma_ord
                else:
                    ch = dop.eng
                    cnt = dop.pos
                if ch not in need or need[ch][0] < cnt:
                    need[ch] = (cnt, dop)
            waits = []
            Ke = K[e]
            for ch, (cnt, dop) in sorted(need.items(), key=lambda kv: -kv[1][1].idx):
                if Ke.get(ch, 0) >= cnt:
                    continue
                waits.append((ch, cnt))
                Ke[ch] = cnt
                for c2, v2 in dop.vc.items():
                    if Ke.get(c2, 0) < v2:
                        Ke[c2] = v2
            op.waits = waits
            vc = dict(Ke)
            if op.dma_sem is not None:
                dma_seen[op.dma_sem] += 1
                vc['dma:' + op.dma_sem] = op.dma_ord
            else:
                vc[e] = op.pos
            op.vc = vc
        for op in ops:
            for ch, cnt in op.waits:
                if not ch.startswith('dma:'):
                    self.eng_ops[ch][cnt - 1].signal = True
        for e in ENGS:
            last = [o for o in self.eng_ops[e] if o.dma_sem is None]
            if last:
                last[-1].signal = True
        self.final_counts = {}
        for e in ENGS:
            c = 0
            for o in self.eng_ops[e]:
                if o.dma_sem is None and o.signal:
                    c += 1
                o.semval = c
            self.final_counts[e] = c

    def emit(self, stack):
        nc = self.nc
        self.finalize()
        sems = {}
        for e in ENGS:
            sems[e] = stack.enter_context(nc.semaphore('s_' + e))
        for k in self.dma_keys:
            sems['dma:' + k] = stack.enter_context(nc.semaphore('d_' + k))
        block = stack.enter_context(nc.Block())
        sched = self

        def run(engname, eng):
            for op in sched.eng_ops[engname]:
                for ch, cnt in op.waits:
                    if ch.startswith('dma:'):
                        eng.wait_ge(sems[ch], 16 * cnt)
                    else:
                        eng.wait_ge(sems[ch], sched.eng_ops[ch][cnt - 1].semval)
                ins = op.fn(eng)
                if op.dma_sem is not None:
                    ins.then_inc(sems['dma:' + op.dma_sem], 16)
                elif op.signal:
                    ins.then_inc(sems[engname], 1)
            if engname == 'sp':
                for k in sched.dma_keys:
                    eng.wait_ge(sems['dma:' + k], 16 * sched.dma_count[k])
                for e2 in ENGS:
                    if e2 != 'sp' and sched.final_counts[e2] > 0:
                        eng.wait_ge(sems[e2], sched.final_counts[e2])

        @block.sync
        def _(eng):
            run('sp', eng)

        @block.scalar
        def _(eng):
            run('act', eng)

        @block.vector
        def _(eng):
            run('dve', eng)

        @block.tensor
        def _(eng):
            run('pe', eng)

        @block.gpsimd
        def _(eng):
            run('pool', eng)


D = 1024
DIN = 10384
NMETA = 16
EPS = 1e-6
GN_EPS = 64 * 1e-5
OFF_A = 3072
NEG = -30000.0
NCMAX = 272
WCOLS = 10368
C_Q, C_K, C_V, C_Z, C_WDAD, C_RW, C_ZB, C_GA, C_GB = 0, 1024, 2048, 3072, 4096, 4224, 7296, 8320, 9344


class Cfg:
    def __init__(self, npb=8, ns=16):
        self.npb = npb
        self.ns = ns
        self.seq = 256 * npb


def build(cfg):
    nc = bass.Bass("TRN2", target_bir_lowering=False)
    st = ExitStack()
    S = Sched(nc)
    NS = cfg.ns
    SEQ = cfg.seq

    def din(name, shape):
        return nc.dram_tensor(name, shape, F32, kind="ExternalInput").ap()

    def dout(name, shape):
        return nc.dram_tensor(name, shape, F32, kind="ExternalOutput").ap()

    xp = din("xp", [SEQ, D])
    xs = din("xs", [NS * 4, D])
    sg_in = din("sg", [NS, 8, 128, 128])
    sc_in = din("sc", [NS * 3, 3072])
    sr_in = din("sr", [NS, 16, 64, 64])
    ss_in = din("ss", [NS, D])
    meta = din("meta", [NMETA, D])
    ln1_w = din("ln1_w", [D])
    w_in = din("w_in", [D, DIN])
    conv_w = din("conv_w", [4, 3072])
    a_log = din("a_log", [8, 1])
    dt_bias = din("dt_bias", [8, 1])
    gnorm_w = din("gnorm_w", [128, 1])
    w_out_a = din("w_out_a", [D, D])
    mu_in = din("mu", [33, 128])
    w0_in = din("w0", [8, 128])
    w2_in = din("w2", [64, D])
    a0_in = din("a0", [8, 128])
    a2_in = din("a2", [64, D])
    kk_in = din("k_k", [8, 128])
    ka_in = din("k_a", [8, 128])
    rk_in = din("r_k", [8, 128])
    gnw_in = din("gn_w", [8, 128])
    gnb_in = din("gn_b", [8, 128])
    w_out_b = din("w_out_b", [D, D])
    w_out = din("w_out", [D, D])
    lnf_w = din("lnf_w", [D])

    yp = dout("yp", [SEQ, D])
    ys = dout("ys", [NS * 4, D])
    ngp = dout("ngp", [8, 128, 128])
    ncp = dout("ncp", [3, 3072])
    nrp = dout("nrp", [16, 64, 64])
    nsp = dout("nsp", [1, D])
    ngs = dout("ngs", [NS, 8, 128, 128])
    ncs = dout("ncs", [NS * 3, 3072])
    nrs = dout("nrs", [NS, 16, 64, 64])
    nss = dout("nss", [NS, D])

    Wb = nc.dram_tensor("Wb", [128, 8, WCOLS], BF16, kind="Internal").ap()
    Wab = nc.dram_tensor("Wab", [128, 8, 16], BF16, kind="Internal").ap()
    Wo = nc.dram_tensor("Wo", [3, 128, 8, D], BF16, kind="Internal").ap()

    def sb(name, shape, dt=F32):
        return st.enter_context(nc.sbuf_tensor("s_" + name, shape, dt))

    def psum(name, shape, dt=F32):
        return st.enter_context(nc.psum_tensor("p_" + name, shape, dt))

    def rw(aps):
        return [a for a in aps if a is not None and not isinstance(a, (int, float))]

    def mm(out, lhsT, rhs, start=True, stop=True):
        S.add('pe', lambda e: e.matmul(out, lhsT=lhsT, rhs=rhs, start=start, stop=stop),
              reads=[lhsT, rhs], writes=[out])

    def tr(out, in_, idn):
        S.add('pe', lambda e: e.transpose(out=out, in_=in_, identity=idn), reads=[in_, idn], writes=[out])

    def act(out, in_, func, bias=None, scale=None, accum=None):
        kw = {}
        if bias is not None:
            kw['bias'] = bias
        if scale is not None:
            kw['scale'] = scale
        if accum is not None:
            kw['accum_out'] = accum
        S.add('act', lambda e: e.activation(out=out, in_=in_, func=func, **kw),
              reads=rw([in_, bias, scale]), writes=rw([out, accum]))

    def tt(eng, out, in0, in1, op):
        S.add(eng, lambda e: e.tensor_tensor(out=out, in0=in0, in1=in1, op=op), reads=[in0, in1], writes=[out])

    def ts(eng, out, in0, s1, s2=None, op0=ALU.mult, op1=None):
        kw = {}
        if op1 is not None:
            kw['op1'] = op1
        S.add(eng, lambda e: e.tensor_scalar(out=out, in0=in0, scalar1=s1, scalar2=s2, op0=op0, **kw),
              reads=rw([in0, s1, s2]), writes=[out])

    def stt(eng, out, in0, scalar, in1, op0, op1):
        S.add(eng, lambda e: e.scalar_tensor_tensor(out=out, in0=in0, scalar=scalar, in1=in1, op0=op0, op1=op1),
              reads=rw([in0, scalar, in1]), writes=[out])

    def cp(eng, out, in_):
        if eng == 'act':
            S.add('act', lambda e: e.copy(out=out, in_=in_), reads=[in_], writes=[out])
        else:
            S.add(eng, lambda e: e.tensor_copy(out=out, in_=in_), reads=[in_], writes=[out])

    def memset(eng, out, val):
        S.add(eng, lambda e: e.memset(out, val), writes=[out])

    def dma(eng, out, in_, key):
        S.add(eng, lambda e: e.dma_start(out=out, in_=in_), reads=[in_], writes=[out], dma=key)

    def recip(out, in_):
        S.add('dve', lambda e: e.reciprocal(out=out, in_=in_), reads=[in_], writes=[out])

    def rsum(out, in_):
        S.add('dve', lambda e: e.reduce_sum(out=out, in_=in_, axis=AX.X), reads=[in_], writes=[out])

    def scan(out, d0, d1):
        S.add('dve', lambda e: e.tensor_tensor_scan(out=out, data0=d0, data1=d1, initial=0.0,
                                                    op0=ALU.mult, op1=ALU.add), reads=[d0, d1], writes=[out])

    def asel(out, in_, pattern, cmp, fill, base, cm):
        S.add('pool', lambda e: e.affine_select(out=out, in_=in_, pattern=pattern, compare_op=cmp, fill=fill,
                                                base=base, channel_multiplier=cm), reads=[in_], writes=[out])

    PP = psum("PP", [128, 2, 512])
    PB = [psum("PB%d" % i, [128, 1024], BF16) for i in range(2)]
    PG = psum("PG", [128, 4, 512])
    ctr = {'pp': 0, 'pb': 0, 'pg': 0}

    def ppbank():
        i = ctr['pp'] % 2
        ctr['pp'] += 1
        return PP[:, i, :]

    def pbbank():
        i = ctr['pb'] % 2
        ctr['pb'] += 1
        return PB[i]

    def pgbank():
        i = ctr['pg'] % 4
        ctr['pg'] += 1
        return PG[:, i, :]

    ident = sb("ident", [128, 128])
    identb = sb("identb", [128, 128], BF16)
    ones = sb("ones", [128, 128])
    bones = sb("bones", [128, 128])
    nmS = sb("nmS", [64, 64])
    nmTI = sb("nmTI", [64, 64])
    m_st = sb("m_st", [64, 64])
    m_stT = sb("m_stT", [64, 64])
    m_inT = sb("m_inT", [64, 64])
    Esel = sb("Esel", [8, 8, 64])
    memset('pool', ident[:], 0.0)
    asel(ident[:], ident[:], [[-1, 128]], ALU.not_equal, 1.0, 0, 1)
    cp('pool', identb[:], ident[:])
    memset('pool', ones[:], 1.0)
    memset('pool', bones[:], 0.0)
    memset('pool', bones[0:64, 0:64], 1.0)
    memset('pool', bones[64:128, 64:128], 1.0)
    memset('pool', nmS[:], 0.0)
    asel(nmS[:], nmS[:], [[-1, 64]], ALU.is_gt, NEG, 0, 1)
    memset('pool', nmTI[:], 0.0)
    asel(nmTI[:], nmTI[:], [[1, 64]], ALU.is_ge, NEG, 0, -1)
    memset('pool', m_st[:], 1.0)
    asel(m_st[:], m_st[:], [[-1, 64]], ALU.is_gt, 0.0, 0, 1)
    memset('pool', m_stT[:], 1.0)
    asel(m_stT[:], m_stT[:], [[1, 64]], ALU.is_gt, 0.0, 0, -1)
    memset('pool', m_inT[:], 1.0)
    asel(m_inT[:], m_inT[:], [[1, 64]], ALU.is_ge, 0.0, 0, -1)
    memset('pool', Esel[:], 0.0)
    asel(Esel[:], Esel[:], [[-1, 8], [0, 64]], ALU.not_equal, 1.0, 0, 1)

    hrow = sb("hrow", [128, D])
    xrow = sb("xrow", [128, D])
    rstat = sb("rstat", [128, 4])
    memset('pool', hrow[:], 0.0)

    pstage = xrow[0:33, 0:128]

    def load_cols(name, src, ntile):
        t = sb(name, [128, ntile])
        dma('sp', pstage[0:ntile, :], src, 'c')
        pt = pgbank()
        tr(pt[:, 0:ntile], pstage[0:ntile, :], ident[0:ntile, 0:ntile])
        cp('act', t[:], pt[:, 0:ntile])
        return t

    mu = load_cols("mu", mu_in, 33)
    w0c = load_cols("w0c", w0_in, 8)
    a0c = load_cols("a0c", a0_in, 8)
    kkc = load_cols("kkc", kk_in, 8)
    kac = load_cols("kac", ka_in, 8)
    rkc = load_cols("rkc", rk_in, 8)
    gnwc = load_cols("gnwc", gnw_in, 8)
    gnbc = load_cols("gnbc", gnb_in, 8)
    omka = sb("omka", [128, 8])
    ts('dve', omka[:], kac[:], -1.0, 1.0, op0=ALU.mult, op1=ALU.add)
    convw = sb("convw", [128, 24, 4])
    for q in range(3):
        dma('sp', xrow[0:4, :], conv_w[:, q * 1024:(q + 1) * 1024], 'c')
        for j in range(8):
            ft = q * 8 + j
            pt = pgbank()
            tr(pt[:, 0:4], xrow[0:4, j * 128:(j + 1) * 128], ident[0:4, 0:4])
            cp('act', convw[:, ft, :], pt[:, 0:4])
    ln1bc = sb("ln1bc", [128, D])
    lnfbc = sb("lnfbc", [128, D])
    dma('sp', ln1bc[:], ln1_w.partition_broadcast(128), 'c')
    dma('sp', lnfbc[:], lnf_w.partition_broadcast(128), 'c')
    alog = sb("alog", [8, 1])
    dtb = sb("dtb", [8, 1])
    nA = sb("nA", [8, 1])
    dma('sp', alog[:], a_log, 'c')
    dma('sp', dtb[:], dt_bias, 'c')
    act(nA[:], alog[:], AF.Exp)
    ts('dve', nA[:], nA[:], -1.0)
    gnw = sb("gnw", [128, 1])
    dma('sp', gnw[:], gnorm_w, 'c')
    wa2 = sb("wa2", [128, D])
    dma('sp', wa2[0:64, :], w2_in, 'c')
    dma('sp', wa2[64:128, :], a2_in, 'c')

    import os as _os2
    _KS = float(_os2.environ.get('KSTOP', '999'))
    w_v = w_in.rearrange("(dh dl) c -> dl dh c", dl=128)
    for dh in range(8 if _KS >= 1 else 0):
        dma('pool', Wb[:, dh, C_Q:C_Q + 3072], w_v[:, dh, 0:3072], 'wcast')
    for dh in range(8 if _KS >= 1 else 0):
        dma('pool', Wab[:, dh, :], w_v[:, dh, OFF_A:OFF_A + 16], 'wcast')
        dma('pool', Wb[:, dh, C_Z:C_Z + 1024], w_v[:, dh, 3088:4112], 'wcast')
        dma('pool', Wb[:, dh, C_WDAD:C_WDAD + 128], w_v[:, dh, 7184:7312], 'wcast')
        for g in range(3):
            dma('pool', Wb[:, dh, C_RW:C_RW + 3072].rearrange("d (p g f) -> d p g f", p=8, g=3)[:, :, g, :],
                w_v[:, dh, 4112 + g * 1024:4112 + (g + 1) * 1024].rearrange("d (p f) -> d p f", p=8), 'wcast')
        dma('pool', Wb[:, dh, C_ZB:C_ZB + 1024], w_v[:, dh, 7312:8336], 'wcast')
    for dh in range(8 if _KS >= 1 else 0):
        dma('pool', Wb[:, dh, C_GA:C_GA + 2048], w_v[:, dh, 8336:10384], 'wcast')
    for i, wsrc in enumerate((w_out_a, w_out_b, w_out)):
        wv = wsrc.rearrange("(dh dl) c -> dl dh c", dl=128)
        for dh in range(8 if _KS >= 1 else 0):
            dma('pool', Wo[i, :, dh, :], wv[:, dh, :], 'wcast')
    wab = sb("wab", [128, 8, 16], BF16)
    dma('sp', wab[:], Wab, 'c')

    NWB = 2
    wbuf = [sb("wbuf%d" % i, [128, 8, 1024], BF16) for i in range(NWB)]
    wctr = [0]

    def wload(src):
        b = wbuf[wctr[0] % NWB]
        key = 'w%d' % (wctr[0] % NWB)
        wctr[0] += 1
        dma('sp', b[:, :, 0:src.shape[2]], src, key)
        return b

    hT = sb("hT", [128, 8, NCMAX], BF16)
    BT = [sb("BT%d" % i, [128, 8, NCMAX], BF16) for i in range(10)]
    oaT = sb("oaT", [128, 8, NCMAX], BF16)
    obT = BT[0]
    bg2 = sb("bg2", [128, 16, NCMAX], BF16)
    rg2 = sb("rg2", [128, 16, NCMAX], BF16)
    memset('pool', bg2[:], 0.0)
    memset('pool', rg2[:], 0.0)
    mgT = BT[9]
    MG = BT[7]

    Sg = sb("Sg", [128, 8, 128])
    Sgb = sb("Sgb", [128, 8, 128], BF16)
    Mr = sb("Mr", [128, 8, 64])
    Mrb = sb("Mrb", [128, 8, 64], BF16)
    NSQ = max(NS, 1)
    hist = sb("hist", [128, 24, 3 * NSQ])
    pbh = sb("pbh", [128, 33])
    scanm = sb("scanm", [128, NCMAX])

    cin = sb("cin", [128, NCMAX + 4])
    T = [sb("T%d" % i, [128, NCMAX]) for i in range(15)]
    g8 = {n: sb("g8" + n, [8, NCMAX]) for n in ('g', 'gc', 'ngc', 'egc', 'beta', 'beg', 'etail')}
    NCHMAX = max(5, NS)
    eglD = sb("eglD", [8, NCHMAX, 8])
    eglbc = sb("eglbc", [128, NCHMAX, 8])
    eglR = sb("eglR", [128, 8, NCHMAX])

    def ctile(name, shape, dt=F32):
        return sb(name, [64] + shape, dt)
    sc24 = ctile("sc24", [24])
    kbg = ctile("kbg", [8, 128], BF16)
    ktl = ctile("ktl", [8, 128], BF16)
    vbt = ctile("vbt", [8, 128], BF16)
    Vtm, atl, ktlR = kbg, ktl, vbt
    tmpA = ctile("tmpA", [8, 64])
    DmS = ctile("DmS", [8, 64])
    DmT = ctile("DmT", [8, 64])
    RHS = tmpA
    Nm = [ctile("Nm%d" % i, [8, 64]) for i in range(2)]
    NTm = [ctile("NTm%d" % i, [8, 64]) for i in range(2)]
    RT = [ctile("RT%d" % i, [8, 64]) for i in range(2)]
    RTb = ctile("RTb", [8, 64], BF16)
    AqkT = ctile("AqkT", [8, 64], BF16)
    AakT = ctile("AakT", [8, 64], BF16)
    AqbT = ctile("AqbT", [8, 64], BF16)
    nwT = sb("nwT", [128, 8, 64], BF16)
    vnew = ctile("vnew", [8, 128], BF16)
    osq = ctile("osq", [4, 128])
    ysq = osq
    ost = ctile("ost", [8, 4])
    yst = ost
    onb = ctile("onb", [8, 128], BF16)
    ynb = onb
    Urw = ctile("Urw", [8, 64], BF16)
    strw = sb("strw", [64, 16, 64])
    cvt = strw[:].rearrange("p a b -> p (a b)")[:, 0:768]

    class Blk:
        pass

    blocks = []
    for b in range(cfg.npb):
        B = Blk()
        B.kind = 'p'
        B.idx = b
        B.ntok = 256 + (NMETA if b == 0 else 0)
        B.ncols = B.ntok
        B.nseg = 1
        B.L = B.ntok
        B.first = (b == 0)
        B.last = (b == cfg.npb - 1)
        chunks = []
        c = 0
        if b == 0:
            chunks.append((0, NMETA))
            c = NMETA
        while c < B.ntok:
            chunks.append((c, 64))
            c += 64
        B.segs = [dict(sid=0, chunks=chunks)]
        B.chunks = chunks
        tiles = []
        r = 0
        while r < B.ntok:
            n = min(128, B.ntok - r)
            tiles.append((r, n))
            r += n
        B.tiles = tiles
        blocks.append(B)
    if NS > 0:
        B = Blk()
        B.kind = 's'
        B.idx = 0
        B.ntok = 4 * NS
        B.ncols = 4 * NS + NS
        B.nseg = NS
        B.L = 4
        B.first = True
        B.last = True
        B.segs = [dict(sid=s, chunks=[(4 * s, 4)]) for s in range(NS)]
        B.chunks = [(4 * s, 4) for s in range(NS)]
        B.tiles = [(0, B.ntok)]
        blocks.append(B)

    def load_x_tile(B, r0, n, dst):
        if B.kind == 'p':
            if B.idx == 0 and r0 == 0:
                dma('sp', dst[0:NMETA, :], meta, 'x')
                dma('sp', dst[NMETA:n, :], xp[0:n - NMETA, :], 'x')
            else:
                s0 = r0 + 256 * B.idx - (NMETA if B.idx == 0 else 0)
                dma('sp', dst[0:n, :], xp[s0:s0 + n, :], 'x')
        else:
            dma('sp', dst[0:n, :], xs[0:n, :], 'x')

    def rms_rows(src, n, dst, wbc):
        act(dst[0:n, :], src[0:n, :], AF.Square, accum=rstat[0:n, 0:1])
        ts('dve', rstat[0:n, 1:2], rstat[0:n, 0:1], 1.0 / D, EPS, op0=ALU.mult, op1=ALU.add)
        act(rstat[0:n, 2:3], rstat[0:n, 1:2], AF.Sqrt)
        recip(rstat[0:n, 3:4], rstat[0:n, 2:3])
        stt('dve', dst[0:n, :], src[0:n, :], rstat[0:n, 3:4], wbc[0:n, :], ALU.mult, ALU.mult)

    def front(B):
        for ti, (r0, n) in enumerate(B.tiles):
            load_x_tile(B, r0, n, xrow)
            rms_rows(xrow, n, hrow, ln1bc)
            if B.kind == 's':
                dma('sp', hrow[64:64 + NS, :], ss_in, 'x')
                for s in range(NS):
                    dma('sp', nss[s:s + 1, :], hrow[4 * s + 3:4 * s + 4, :], 'out')
            elif B.last and ti == len(B.tiles) - 1:
                dma('sp', nsp, hrow[n - 1:n, :], 'out')
            for dc in range(8):
                pt = ppbank()
                if B.kind == 's':
                    tr(pt[:, 0:64 + NS], hrow[0:64 + NS, dc * 128:(dc + 1) * 128], ident[0:64 + NS, 0:64 + NS])
                    cp('act', hT[:, dc, 0:B.ntok], pt[:, 0:B.ntok])
                    cp('act', hT[:, dc, B.ntok:B.ntok + NS], pt[:, 64:64 + NS])
                else:
                    tr(pt[:, 0:n], hrow[0:n, dc * 128:(dc + 1) * 128], ident[0:n, 0:n])
                    cp('act' if dc % 2 == 0 else 'dve', hT[:, dc, r0:r0 + n], pt[:, 0:n])

    def proj(wt, ti, ncols, M=128, wcol0=None):
        ps = ppbank()
        c0 = ti * 128 if wcol0 is None else wcol0
        for dc in range(8):
            mm(ps[0:M, 0:ncols], wt[:, dc, c0:c0 + M], hT[:, dc, 0:ncols], start=(dc == 0), stop=(dc == 7))
        return ps

    def tokview(B, ap2d):
        return ap2d.rearrange("p (s l) -> p s l", l=B.L)

    CHK = [None]

    def gdn_prep(B):
        nt = B.ntok
        qT, kT, qgT, vT, zsT = BT[0], BT[1], BT[2], BT[3], BT[4]
        memset('pool', scanm[:, 0:nt], 1.0)
        for (c0, C) in B.chunks:
            memset('pool', scanm[:, c0:c0 + 1], 0.0)
        psa = proj(wab, 0, nt, M=8, wcol0=0)
        act(T[0][0:8, 0:nt], psa[0:8, 0:nt], AF.Exp, bias=dtb[:])
        act(T[0][0:8, 0:nt], T[0][0:8, 0:nt], AF.Ln, bias=1.0)
        ts('dve', g8['g'][:, 0:nt], T[0][0:8, 0:nt], nA[:])
        psb = proj(wab, 0, nt, M=8, wcol0=8)
        act(g8['beta'][:, 0:nt], psb[0:8, 0:nt], AF.Sigmoid)
        scan(g8['gc'][:, 0:nt], scanm[0:8, 0:nt], g8['g'][:, 0:nt])
        ts('dve', g8['ngc'][:, 0:nt], g8['gc'][:, 0:nt], -1.0)
        act(g8['egc'][:, 0:nt], g8['gc'][:, 0:nt], AF.Exp)
        tt('dve', g8['beg'][:, 0:nt], g8['beta'][:, 0:nt], g8['egc'][:, 0:nt], ALU.mult)
        for ci, (c0, C) in enumerate(B.chunks):
            last = c0 + C - 1
            ts('dve', g8['etail'][:, c0:c0 + C], g8['ngc'][:, c0:c0 + C], g8['gc'][:, last:last + 1], op0=ALU.add)
            ts('dve', eglD[:, ci, :], ident[0:8, 0:8], g8['egc'][:, last:last + 1])
        act(g8['etail'][:, 0:nt], g8['etail'][:, 0:nt], AF.Exp)
        nch = len(B.chunks)
        pe_ = pgbank()
        mm(pe_[:, 0:nch * 8], ones[0:8, :], eglD[:, 0:nch, :].rearrange("p c h -> p (c h)"))
        cp('act', eglbc[:, 0:nch, :].rearrange("p c h -> p (c h)"), pe_[:, 0:nch * 8])
        for gi, gname in enumerate(('q', 'k', 'v')):
            wt = wload(Wb[:, :, gi * 1024:(gi + 1) * 1024])
            for ti in range(8):
                ft = gi * 8 + ti
                ps = proj(wt, ti, nt)
                cv = cin[:, 0:B.nseg * (B.L + 3)].rearrange("p (s l) -> p s l", l=B.L + 3)
                cp('act', cv[:, :, 3:3 + B.L], tokview(B, ps[:, 0:nt]))
                hv = hist[:, ft, 0:3 * B.nseg].rearrange("p (s l) -> p s l", l=3)
                if B.first and B.kind == 'p':
                    memset('pool', cv[:, :, 0:3], 0.0)
                else:
                    cp('pool', cv[:, :, 0:3], hv)
                acc = tokview(B, T[0][:, 0:nt])
                ts('dve', acc, cv[:, :, 0:B.L], convw[:, ft, 0:1])
                for i in range(1, 4):
                    stt('dve', acc, cv[:, :, i:i + B.L], convw[:, ft, i:i + 1], acc, ALU.mult, ALU.add)
                cp('pool', hv, cv[:, :, B.L:B.L + 3])
                if gname == 'v':
                    act(vT[:, ti, 0:nt], T[0][:, 0:nt], AF.Silu)
                else:
                    act(T[1][:, 0:nt], T[0][:, 0:nt], AF.Silu)
                    act(T[2][:, 0:nt], T[1][:, 0:nt], AF.Square)
                    pn = pgbank()
                    mm(pn[:, 0:nt], ones[:], T[2][:, 0:nt])
                    act(T[3][:, 0:nt], pn[:, 0:nt], AF.Ln, bias=1e-6)
                    act(T[3][:, 0:nt], T[3][:, 0:nt], AF.Exp, scale=-0.5)
                    if gname == 'k':
                        tt('dve', kT[:, ti, 0:nt], T[1][:, 0:nt], T[3][:, 0:nt], ALU.mult)
                    else:
                        stt('dve', qT[:, ti, 0:nt], T[1][:, 0:nt], 128.0 ** -0.5, T[3][:, 0:nt], ALU.mult, ALU.mult)
                        ts('dve', T[3][0:8, 0:nt], g8['egc'][:, 0:nt], ident[0:8, ti:ti + 1])
                        pq = pgbank()
                        mm(pq[:, 0:nt], ones[0:8, :], T[3][0:8, 0:nt])
                        tt('dve', qgT[:, ti, 0:nt], qT[:, ti, 0:nt], pq[:, 0:nt], ALU.mult)
        wt = wload(Wb[:, :, C_Z:C_Z + 1024])
        for ti in range(8):
            ps = proj(wt, ti, nt)
            act(zsT[:, ti, 0:nt], ps[:, 0:nt], AF.Silu)

    def tri_inverse(N0, NT0, C, nh=8):
        cur = 0
        Icb = ident[0:C, 0:C].unsqueeze(1).to_broadcast([C, nh, C])
        tt('dve', RT[0][0:C, 0:nh, 0:C], NT0, Icb, ALU.add)
        P, PT = N0, NT0
        nlev = 0
        while (1 << (nlev + 1)) < C:
            nlev += 1
        for m in range(1, nlev + 1):
            pP = pgbank()
            pPv = pP[0:C, 0:nh * C].rearrange("p (h c) -> p h c", c=C)
            for h in range(nh):
                mm(pPv[:, h, :], PT[:, h, :], P[:, h, :])
            newP = Nm[m % 2][0:C, 0:nh, 0:C]
            cp('act', newP, pPv)
            newPT = None
            if m < nlev:
                pT = pgbank()
                pTv = pT[0:C, 0:nh * C].rearrange("p (h c) -> p h c", c=C)
                for h in range(nh):
                    mm(pTv[:, h, :], P[:, h, :], PT[:, h, :])
                newPT = NTm[m % 2][0:C, 0:nh, 0:C]
                cp('act', newPT, pTv)
            pR = pgbank()
            pRv = pR[0:C, 0:nh * C].rearrange("p (h c) -> p h c", c=C)
            old = RT[cur][0:C, 0:nh, 0:C]
            for h in range(nh):
                mm(pRv[:, h, :], newP[:, h, :], old[:, h, :])
            cur ^= 1
            tt('dve', RT[cur][0:C, 0:nh, 0:C], pRv, old, ALU.add)
            P = newP
            PT = newPT
        return RT[cur][0:C, 0:nh, 0:C]

    def gdn_chunk(B, ci, c0, C):
        qT, kT, qgT, vT, zsT = BT[0], BT[1], BT[2], BT[3], BT[4]
        cs = slice(c0, c0 + C)
        p1 = pgbank()
        for i, nm in enumerate(('beta', 'beg', 'etail')):
            tr(p1[0:C, 8 * i:8 * i + 8], g8[nm][:, cs], ident[0:8, 0:8])
        cp('act', sc24[0:C, :], p1[0:C, 0:24])

        def scb(i, w):
            return sc24[0:C, 8 * i:8 * i + 8].unsqueeze(2).to_broadcast([C, 8, w])

        def hv3(ap2, w):
            return ap2.rearrange("p (h c) -> p h c", c=w)
        pk = pbbank()
        pkv = hv3(pk[0:C, :], 128)
        for h in range(8):
            tr(pkv[:, h, :], kT[:, h, cs], identb[:])
        tt('dve', kbg[0:C], pkv, scb(1, 128), ALU.mult)
        tt('dve', ktl[0:C], pkv, scb(2, 128), ALU.mult)
        pv = pbbank()
        pvv = hv3(pv[0:C, :], 128)
        for h in range(8):
            tr(pvv[:, h, :], vT[:, h, cs], identb[:])
        tt('dve', vbt[0:C], pvv, scb(0, 128), ALU.mult)
        pd = pgbank()
        pdv = hv3(pd[0:C, 0:8 * C], C)
        for h in range(8):
            mm(pdv[:, h, :], g8['gc'][:, cs], Esel[:, h, 0:C], start=True, stop=False)
            mm(pdv[:, h, :], Esel[:, h, 0:C], g8['ngc'][:, cs], start=False, stop=True)
        tt('dve', DmS[0:C, :, 0:C], pdv, nmS[0:C, 0:C].unsqueeze(1).to_broadcast([C, 8, C]), ALU.add)
        act(DmS[0:C, :, 0:C], DmS[0:C, :, 0:C], AF.Exp)
        pdT = pgbank()
        pdTv = hv3(pdT[0:C, 0:8 * C], C)
        for h in range(8):
            mm(pdTv[:, h, :], Esel[:, h, 0:C], g8['gc'][:, cs], start=True, stop=False)
            mm(pdTv[:, h, :], g8['ngc'][:, cs], Esel[:, h, 0:C], start=False, stop=True)
        tt('dve', DmT[0:C, :, 0:C], pdTv, nmTI[0:C, 0:C].unsqueeze(1).to_broadcast([C, 8, C]), ALU.add)
        act(DmT[0:C, :, 0:C], DmT[0:C, :, 0:C], AF.Exp)
        pg_ = pgbank()
        pgv = hv3(pg_[0:C, 0:8 * C], C)
        for h in range(8):
            mm(pgv[:, h, :], kT[:, h, cs], kT[:, h, cs])
        stt('dve', tmpA[0:C, :, 0:C], pgv, -1.0, DmS[0:C, :, 0:C], ALU.mult, ALU.mult)
        N0 = Nm[0][0:C, :, 0:C]
        tt('dve', N0, tmpA[0:C, :, 0:C], scb(0, C), ALU.mult)
        pq = pgbank()
        pqv = hv3(pq[0:C, 0:8 * C], C)
        for h in range(8):
            mm(pqv[:, h, :], kT[:, h, cs], qT[:, h, cs])
        tt('dve', AqkT[0:C, :, 0:C], pqv, DmT[0:C, :, 0:C], ALU.mult)
        pn = pgbank()
        pnv = hv3(pn[0:C, 0:8 * C], C)
        for h in range(8):
            tr(pnv[:, h, :], N0[:, h, :], ident[0:C, 0:C])
        NT0 = NTm[0][0:C, :, 0:C]
        cp('act', NT0, pnv)
        RTf = tri_inverse(N0, NT0, C)
        cp('act', RTb[0:C, :, 0:C], RTf)
        pw = pgbank()
        pwv = hv3(pw[:, 0:8 * C], C)
        for h in range(8):
            mm(pwv[:, h, :], kbg[0:C, h, :], RTb[0:C, h, 0:C])
        ts('dve', nwT[:, :, 0:C], pwv, -1.0)
        for half in range(2):
            p2 = pgbank()
            p2v = hv3(p2[0:C, :], 128)
            for hh in range(4):
                h = half * 4 + hh
                mm(p2v[:, hh, :], RTb[0:C, h, 0:C], vbt[0:C, h, :], start=True, stop=False)
                mm(p2v[:, hh, :], nwT[:, h, 0:C], Sgb[:, h, :], start=False, stop=True)
            cp('act', vnew[0:C, half * 4:half * 4 + 4, :], p2v)
        for half in range(2):
            p3 = pgbank()
            p3v = hv3(p3[0:C, :], 128)
            for hh in range(4):
                h = half * 4 + hh
                mm(p3v[:, hh, :], qgT[:, h, cs], Sgb[:, h, :], start=True, stop=False)
                mm(p3v[:, hh, :], AqkT[0:C, h, 0:C], vnew[0:C, h, :], start=False, stop=True)
            hs = slice(half * 4, half * 4 + 4)
            act(osq[0:C], p3v, AF.Square)
            rsum(ost[0:C, hs, 0:1], osq[0:C])
            ts('dve', ost[0:C, hs, 1:2], ost[0:C, hs, 0:1], 1.0 / 128, EPS, op0=ALU.mult, op1=ALU.add)
            act(ost[0:C, hs, 2:3], ost[0:C, hs, 1:2], AF.Sqrt)
            recip(ost[0:C, hs, 3:4], ost[0:C, hs, 2:3])
            tt('dve', onb[0:C, hs, :], p3v, ost[0:C, hs, 3:4].to_broadcast([C, 4, 128]), ALU.mult)
        po = pbbank()
        pov = hv3(po[:, 0:8 * C], C)
        for h in range(8):
            tr(pov[:, h, :], onb[0:C, h, :], identb[0:C, 0:C])
        stt('dve', oaT[:, :, cs], pov, gnw[:, 0:1], zsT[:, :, cs], ALU.mult, ALU.mult)
        for half in range(2):
            p4 = pgbank()
            p4v = hv3(p4[:, :], 128)
            for hh in range(4):
                h = half * 4 + hh
                mm(p4v[:, hh, :], ktl[0:C, h, :], vnew[0:C, h, :])
            for hh in range(4):
                h = half * 4 + hh
                stt('dve', Sg[:, h, :], Sg[:, h, :], eglbc[:, ci, h:h + 1], p4v[:, hh, :], ALU.mult, ALU.add)
        cp('pool', Sgb[:], Sg[:])

    def rwkv_prep(B):
        nt = B.ntok
        ncols = B.ncols
        bgT, agT, kgT, rgT, atT, ktT, rvT, bonT, zbT = BT[0], BT[1], BT[2], BT[3], BT[4], BT[5], BT[6], BT[7], BT[8]

        def mixed(ps, fidx, dst):
            raw = cin[:, 0:B.nseg * (B.L + 1)].rearrange("p (s l) -> p s l", l=B.L + 1)
            cp('act', raw[:, :, 1:1 + B.L], tokview(B, ps[:, 0:nt]))
            if B.kind == 'p':
                if B.first:
                    memset('pool', raw[:, :, 0:1], 0.0)
                else:
                    cp('pool', raw[:, 0, 0:1], pbh[:, fidx:fidx + 1])
                cp('pool', pbh[:, fidx:fidx + 1], raw[:, 0, B.L:B.L + 1])
            else:
                cp('act', raw[:, :, 0:1], ps[:, nt:nt + B.nseg].unsqueeze(2))
            dv = tokview(B, dst)
            tt('dve', dv, raw[:, :, 0:B.L], raw[:, :, 1:1 + B.L], ALU.subtract)
            stt('dve', dv, dv, mu[:, fidx:fidx + 1], raw[:, :, 1:1 + B.L], ALU.mult, ALU.add)

        wt = wload(Wb[:, :, C_WDAD:C_WDAD + 128])
        ps = proj(wt, 0, ncols)
        mixed(ps, 24, T[0][:, 0:nt])
        act(T[1][0:64, 0:nt], T[0][0:64, 0:nt], AF.Tanh)
        Tr, Tk, Tv = T[2], T[3], T[4]
        for g in range(4):
            wt = wload(Wb[:, :, C_RW + g * 768:C_RW + (g + 1) * 768])
            for pp in range(2):
                p = 2 * g + pp
                for j, dst in enumerate((Tr, Tk, Tv)):
                    ps = proj(wt, pp * 3 + j, ncols)
                    mixed(ps, 8 * j + p, dst[:, 0:nt])
                r_p, k_p, v_p = Tr[:, 0:nt], Tk[:, 0:nt], Tv[:, 0:nt]
                wlog, gc, eg, eng, egp = (T[i][:, 0:nt] for i in (5, 6, 7, 8, 9))
                a_, kk, rn, kb2, kg32 = (T[i][:, 0:nt] for i in (10, 11, 12, 13, 14))
                psw = pgbank()
                mm(psw[:, 0:nt], wa2[0:64, p * 128:(p + 1) * 128], T[1][0:64, 0:nt])
                act(wlog, psw[:, 0:nt], AF.Sigmoid, bias=w0c[:, p:p + 1])
                ts('pool', wlog, wlog, -float(np.exp(-0.5)))
                scan(gc, scanm[:, 0:nt], wlog)
                act(eg, gc, AF.Exp)
                act(eng, gc, AF.Exp, scale=-1.0)
                tt('pool', egp, gc, wlog, ALU.subtract)
                act(egp, egp, AF.Exp)
                for ci, (c0, C) in enumerate(B.chunks):
                    cp('pool', eglR[:, p, ci:ci + 1], eg[:, c0 + C - 1:c0 + C])
                psa = pgbank()
                mm(psa[:, 0:nt], wa2[64:128, p * 128:(p + 1) * 128], T[0][64:128, 0:nt])
                act(a_, psa[:, 0:nt], AF.Sigmoid, bias=a0c[:, p:p + 1])
                ts('pool', kk, k_p, kkc[:, p:p + 1])
                act(rn, kk, AF.Square)
                pss = pgbank()
                mm(pss[:, 0:nt], bones[:], rn)
                act(rn, pss[:, 0:nt], AF.Ln, bias=1e-6)
                act(rn, rn, AF.Exp, scale=-0.5)
                tt('dve', kk, kk, rn, ALU.mult)
                ts('dve', kb2, a_, kac[:, p:p + 1], omka[:, p:p + 1], op0=ALU.mult, op1=ALU.add)
                tt('dve', kb2, kb2, k_p, ALU.mult)
                for e_ in range(2):
                    hsl = slice(64 * e_, 64 * e_ + 64)
                    tt('dve', bg2[hsl, 2 * p + e_, 0:nt], kk[hsl], egp[hsl], ALU.mult)
                ka = wlog
                tt('pool', ka, kk, a_, ALU.mult)
                ag32 = gc
                stt('dve', ag32, ka, -1.0, eng, ALU.mult, ALU.mult)
                cp('pool', agT[:, p, 0:nt], ag32)
                tt('dve', kg32, kb2, eng, ALU.mult)
                cp('pool', kgT[:, p, 0:nt], kg32)
                for e_ in range(2):
                    hsl = slice(64 * e_, 64 * e_ + 64)
                    tt('dve', rg2[hsl, 2 * p + e_, 0:nt], r_p[hsl], eg[hsl], ALU.mult)
                for ci, (c0, C) in enumerate(B.chunks):
                    ts('pool', atT[:, p, c0:c0 + C], ag32[:, c0:c0 + C], eglR[:, p, ci:ci + 1])
                    ts('pool', ktT[:, p, c0:c0 + C], kg32[:, c0:c0 + C], eglR[:, p, ci:ci + 1])
                cp('act', rvT[:, p, 0:nt], v_p)
                prod = egp
                stt('dve', prod, r_p, rkc[:, p:p + 1], kb2, ALU.mult, ALU.mult)
                psb_ = pgbank()
                mm(psb_[:, 0:nt], bones[:], prod)
                tt('dve', bonT[:, p, 0:nt], psb_[:, 0:nt], v_p, ALU.mult)
        wt = wload(Wb[:, :, C_ZB:C_ZB + 1024])
        for ti in range(8):
            ps = proj(wt, ti, ncols)
            mixed(ps, 25 + ti, T[14][:, 0:nt])
            act(zbT[:, ti, 0:nt], T[14][:, 0:nt], AF.Silu)

    def rwkv_chunk(B, ci, c0, C):
        chk = CHK[0]
        bgT, agT, kgT, rgT, atT, ktT, rvT, bonT, zbT, ynT = BT
        cs = slice(c0, c0 + C)

        def hv3(ap2, w):
            return ap2.rearrange("p (h c) -> p h c", c=w)
        import os as _os3
        _kv = int(_os3.environ.get('KVAR', '3'))
        for k_, (src, dst) in enumerate(((rvT, Vtm), (atT, atl), (ktT, ktlR))[:_kv]):
            pk = pbbank()
            pkv = hv3(pk[0:C, :], 128)
            for p in range(8):
                tr(pkv[:, p, :], src[:, p, cs], identb[:])
            cp('act' if k_ == 0 else 'dve', dst[0:C], pkv)
        Vv = Vtm[0:C].rearrange("p a (e v) -> p (a e) v", e=2)
        chk(6.1)
        for half in range(2):
            heads = list(range(half * 8, half * 8 + 8))

            def hop(T_, h):
                if T_ is bg2 or T_ is rg2:
                    return T_[:, h, cs]
                return T_[:, h // 2, cs]

            import os as _os4
            _ksc = [int(_os4.environ.get('KSC', '99'))]

            def score(lt, rt, mask, dst):
                _ksc[0] -= 1
                if _ksc[0] < 0:
                    return
                ps_ = pgbank()
                psv = hv3(ps_[0:C, 0:8 * C], C)
                for hi, h in enumerate(heads):
                    mm(psv[:, hi, :], hop(lt, h), hop(rt, h))
                tt('dve', dst, psv, mask[0:C, 0:C].unsqueeze(1).to_broadcast([C, 8, C]), ALU.mult)
            N0 = Nm[0][0:C, :, 0:C]
            NT0 = NTm[0][0:C, :, 0:C]
            score(bg2, agT, m_st, N0)
            score(agT, bg2, m_stT, NT0)
            score(kgT, bg2, m_stT, AakT[0:C, :, 0:C])
            score(agT, rg2, m_inT, AqbT[0:C, :, 0:C])
            score(kgT, rg2, m_inT, AqkT[0:C, :, 0:C])
            chk(6.2)
            RTf = tri_inverse(N0, NT0, C)
            chk(6.3)
            p1 = pgbank()
            p1v = hv3(p1[0:C, 0:8 * 64], 64)
            for hi, h in enumerate(heads):
                p, e = h // 2, h % 2
                mm(p1v[:, hi, :], hop(bg2, h), Mrb[:, p, :], start=True, stop=False)
                mm(p1v[:, hi, :], AakT[0:C, hi, 0:C], Vv[:, h, :], start=False, stop=True)
            cp('act', RHS[0:C], p1v)
            p2 = pgbank()
            p2v = hv3(p2[0:C, 0:8 * 64], 64)
            for hi, h in enumerate(heads):
                mm(p2v[:, hi, :], RTf[:, hi, :], RHS[0:C, hi, :])
            cp('act', Urw[0:C], p2v)
            p3 = pgbank()
            p3v = hv3(p3[0:C, 0:8 * 64], 64)
            for hi, h in enumerate(heads):
                p, e = h // 2, h % 2
                mm(p3v[:, hi, :], hop(rg2, h), Mrb[:, p, :], start=True, stop=False)
                mm(p3v[:, hi, :], AqbT[0:C, hi, 0:C], Urw[0:C, hi, :], start=False, stop=False)
                mm(p3v[:, hi, :], AqkT[0:C, hi, 0:C], Vv[:, h, :], start=False, stop=True)
            chk(6.4)
            ysv = ysq[0:C].rearrange("p a (b v) -> p (a b) v", b=2)
            Ysb = tmpA[0:C]
            cp('act', Ysb, p3v)
            rsum(yst[0:C, :, 0:1], Ysb)
            chk(6.41)
            act(ysv, Ysb, AF.Square)
            rsum(yst[0:C, :, 1:2], ysv)
            chk(6.42)
            ts('dve', yst[0:C, :, 0:1], yst[0:C, :, 0:1], 1.0 / 64)
            tt('dve', yst[0:C, :, 2:3], yst[0:C, :, 0:1], yst[0:C, :, 0:1], ALU.mult)
            stt('dve', yst[0:C, :, 1:2], yst[0:C, :, 1:2], 1.0 / 64, yst[0:C, :, 2:3], ALU.mult, ALU.subtract)
            ts('dve', yst[0:C, :, 1:2], yst[0:C, :, 1:2], GN_EPS, op0=ALU.add)
            act(yst[0:C, :, 2:3], yst[0:C, :, 1:2], AF.Sqrt)
            recip(yst[0:C, :, 3:4], yst[0:C, :, 2:3])
            chk(6.43)
            tt('dve', ysv, Ysb, yst[0:C, :, 0:1].to_broadcast([C, 8, 64]), ALU.subtract)
            ynv8 = ynb[0:C, 0:4, :].rearrange("p a (b v) -> p (a b) v", b=2)
            tt('dve', ynv8, ysv, yst[0:C, :, 3:4].to_broadcast([C, 8, 64]), ALU.mult)
            chk(6.44)
            po = pbbank()
            pov = hv3(po[:, 0:4 * C], C)
            for a in range(4):
                tr(pov[:, a, :], ynb[0:C, a, :], identb[0:C, 0:C])
            cp('act', ynT[:, half * 4:half * 4 + 4, cs], pov)
            chk(6.5)
            pA = pgbank()
            pAv = hv3(pA[:, 0:4 * 64], 64)
            pBk = pgbank()
            pBv = hv3(pBk[:, 0:4 * 64], 64)
            for a in range(4):
                p = half * 4 + a
                for e, pv in ((0, pAv), (1, pBv)):
                    hi = 2 * a + e
                    h = 2 * p + e
                    mm(pv[:, a, :], atl[0:C, p, :], Urw[0:C, hi, :], start=True, stop=False)
                    mm(pv[:, a, :], ktlR[0:C, p, :], Vv[:, h, :], start=False, stop=True)
            ps_ = slice(half * 4, half * 4 + 4)
            for e, pv in ((0, pAv), (1, pBv)):
                rows = slice(64 * e, 64 * e + 64)
                tt('dve', Mr[rows, ps_, :], Mr[rows, ps_, :],
                   eglR[rows, ps_, ci:ci + 1].to_broadcast([64, 4, 64]), ALU.mult)
                tt('dve', Mr[rows, ps_, :], Mr[rows, ps_, :], pv[rows], ALU.add)
        cp('pool', Mrb[:], Mr[:])

    def out_stage(B):
        nt = B.ntok
        bonT, zbT, ynT = BT[7], BT[8], BT[9]
        for p in range(8):
            ts('dve', T[0][:, 0:nt], ynT[:, p, 0:nt], gnwc[:, p:p + 1], gnbc[:, p:p + 1], op0=ALU.mult, op1=ALU.add)
            tt('dve', T[0][:, 0:nt], T[0][:, 0:nt], bonT[:, p, 0:nt], ALU.add)
            tt('dve', obT[:, p, 0:nt], T[0][:, 0:nt], zbT[:, p, 0:nt], ALU.mult)
        for bi, (src, gcol) in enumerate(((oaT, C_GA), (obT, C_GB))):
            wg = wload(Wb[:, :, gcol:gcol + 1024])
            wo = wload(Wo[bi])
            for ti in range(8):
                psg = proj(wg, ti, nt)
                act(T[1][:, 0:nt], psg[:, 0:nt], AF.Sigmoid)
                pbr = ppbank()
                for ec in range(8):
                    mm(pbr[:, 0:nt], wo[:, ec, ti * 128:(ti + 1) * 128], src[:, ec, 0:nt], start=(ec == 0), stop=(ec == 7))
                if bi == 0:
                    tt('dve', MG[:, ti, 0:nt], pbr[:, 0:nt], T[1][:, 0:nt], ALU.mult)
                else:
                    tt('dve', T[2][:, 0:nt], pbr[:, 0:nt], T[1][:, 0:nt], ALU.mult)
                    tt('dve', mgT[:, ti, 0:nt], T[2][:, 0:nt], MG[:, ti, 0:nt], ALU.add)
        wo = wload(Wo[2])
        for ti, (r0, n) in enumerate(B.tiles):
            load_x_tile(B, r0, n, hrow)
            for half in range(2):
                px = ppbank()
                for dc in range(8):
                    mm(px[0:n, :], mgT[:, dc, r0:r0 + n], wo[:, dc, half * 512:(half + 1) * 512], start=(dc == 0), stop=(dc == 7))
                tt('dve', xrow[0:n, half * 512:(half + 1) * 512], px[0:n, :], hrow[0:n, half * 512:(half + 1) * 512], ALU.add)
            rms_rows(xrow, n, hrow, lnfbc)
            if B.kind == 'p':
                if B.idx == 0 and r0 == 0:
                    dma('sp', yp[0:n - NMETA, :], hrow[NMETA:n, :], 'out')
                else:
                    s0 = r0 + 256 * B.idx - (NMETA if B.idx == 0 else 0)
                    dma('sp', yp[s0:s0 + n, :], hrow[0:n, :], 'out')
            else:
                dma('sp', ys[0:n, :], hrow[0:n, :], 'out')

    def load_gdn_state(s):
        dma('sp', Sg[:], sg_in[s].rearrange("h k v -> k h v"), 'st')
        cp('pool', Sgb[:], Sg[:])

    def store_gdn_state(dst):
        dma('sp', dst.rearrange("h k v -> k h v"), Sg[:], 'out')

    def load_rwkv_state(s):
        dma('sp', strw[:], sr_in[s].rearrange("h v k -> v h k"), 'st')
        for p in range(8):
            pt = pgbank()
            tr(pt[:, 0:64], strw[:, 2 * p:2 * p + 2, :].rearrange("p a b -> p (a b)"), ident[0:64, 0:64])
            cp('act', Mr[:, p, :], pt[:, 0:64])
        cp('pool', Mrb[:], Mr[:])

    def store_rwkv_state(dst):
        for p in range(8):
            pt = pgbank()
            tr(pt[0:64, 0:128], Mr[:, p, :], ident[:])
            cp('act', strw[:, 2 * p:2 * p + 2, :].rearrange("p a b -> p (a b)"), pt[0:64, 0:128])
        dma('sp', dst.rearrange("h v k -> v h k"), strw[:], 'out')

    def store_conv(B, dst):
        n3 = 3 * B.nseg
        for q in range(4):
            for j in range(6):
                ft = q * 6 + j
                pt = pgbank()
                tr(pt[0:n3, 0:128], hist[:, ft, 0:n3], ident[:])
                cp('act' if j % 2 == 0 else 'dve', cvt[0:n3, j * 128:(j + 1) * 128], pt[0:n3, 0:128])
            dma('sp', dst[:, q * 768:(q + 1) * 768], cvt[0:n3, :], 'out')

    def load_conv_hist(B):
        n3 = 3 * B.nseg
        for q in range(4):
            dma('sp', cvt[0:n3, :], sc_in[:, q * 768:(q + 1) * 768], 'st')
            for j in range(6):
                ft = q * 6 + j
                pt = pgbank()
                tr(pt[:, 0:n3], cvt[0:n3, j * 128:(j + 1) * 128], ident[0:n3, 0:n3])
                cp('act' if j % 2 == 0 else 'dve', hist[:, ft, 0:n3], pt[:, 0:n3])

    import os as _os
    KSTOP = float(_os.environ.get('KSTOP', '999'))

    class _Stop(Exception):
        pass

    def chk(n):
        if n > KSTOP:
            raise _Stop()

    CHK[0] = chk

    def main_prog():
      memset('pool', Sg[:], 0.0)
      memset('pool', Sgb[:], 0.0)
      memset('pool', Mr[:], 0.0)
      memset('pool', Mrb[:], 0.0)
      for B in blocks:
        chk(2)
        front(B)
        if B.kind == 's':
            load_conv_hist(B)
        chk(3)
        gdn_prep(B)
        ci = 0
        for seg in B.segs:
            if B.kind == 's':
                load_gdn_state(seg['sid'])
            for (c0, C) in seg['chunks']:
                chk(4)
                gdn_chunk(B, ci, c0, C)
                ci += 1
            if B.kind == 's':
                store_gdn_state(ngs[seg['sid']])
        if B.kind == 'p' and B.last:
            store_gdn_state(ngp)
        if B.last:
            store_conv(B, ncp if B.kind == 'p' else ncs)
        chk(5)
        rwkv_prep(B)
        ci = 0
        for seg in B.segs:
            if B.kind == 's':
                load_rwkv_state(seg['sid'])
            for (c0, C) in seg['chunks']:
                chk(6)
                rwkv_chunk(B, ci, c0, C)
                ci += 1
            if B.kind == 's':
                store_rwkv_state(nrs[seg['sid']])
        if B.kind == 'p' and B.last:
            store_rwkv_state(nrp)
        chk(7)
        out_stage(B)

    try:
        main_prog()
    except _Stop:
        pass
    if _os.environ.get('KDBG'):
        dbg = dout("dbg", [10, 128, 8, NCMAX])
        for i in range(9):
            for p in range(8):
                cp('dve', T[0][:, 0:NCMAX], BT[i][:, p, :])
                dma('sp', dbg[i, :, p, :], T[0][:, 0:NCMAX], 'out')

    S.emit(st)
    st.close()
    return nc


_NC_CACHE = {}


def make_in_maps(cfg, inputs, ncores):
    f = lambda a: np.ascontiguousarray(np.asarray(a, dtype=np.float32))
    ns = cfg.ns
    shared = {
        "meta": f(inputs["meta_tokens"]),
        "ln1_w": f(inputs["ln1_w"]).reshape(D),
        "w_in": f(inputs["w_in"]).reshape(D, DIN),
        "conv_w": f(inputs["gdn_conv_w"]).reshape(4, 3072),
        "a_log": f(inputs["gdn_a_log"]).reshape(8, 1),
        "dt_bias": f(inputs["gdn_dt_bias"]).reshape(8, 1),
        "gnorm_w": f(inputs["gdn_norm_w"]).reshape(128, 1),
        "w_out_a": f(inputs["w_out_a"]).reshape(D, D),
        "mu": f(inputs["rwkv_mu"]).reshape(33, 128),
        "w0": f(inputs["rwkv_w0"]).reshape(8, 128),
        "w2": f(inputs["rwkv_w2"]).reshape(64, D),
        "a0": f(inputs["rwkv_a0"]).reshape(8, 128),
        "a2": f(inputs["rwkv_a2"]).reshape(64, D),
        "k_k": f(inputs["rwkv_k_k"]).reshape(8, 128),
        "k_a": f(inputs["rwkv_k_a"]).reshape(8, 128),
        "r_k": f(inputs["rwkv_r_k"]).reshape(8, 128),
        "gn_w": f(inputs["rwkv_gn_w"]).reshape(8, 128),
        "gn_b": f(inputs["rwkv_gn_b"]).reshape(8, 128),
        "w_out_b": f(inputs["w_out_b"]).reshape(D, D),
        "w_out": f(inputs["w_out"]).reshape(D, D),
        "lnf_w": f(inputs["lnf_w"]).reshape(D),
    }
    maps = []
    for c in range(ncores):
        m = dict(shared)
        m["xp"] = f(inputs["x_prompt"][c])
        if ns > 0:
            sl = slice(c * ns, (c + 1) * ns)
            m["xs"] = f(inputs["x_sample"][sl]).reshape(ns * 4, D)
            m["sg"] = f(inputs["state_gdn"][0, sl])
            m["sc"] = f(inputs["state_gdn_conv"][0, sl]).reshape(ns * 3, 3072)
            m["sr"] = f(inputs["state_rwkv"][0, sl])
            m["ss"] = f(inputs["state_shift"][0, sl])
        maps.append(m)
    return maps


def gather(cfg, results, ncores):
    ns = cfg.ns
    cat = lambda k, shp: np.concatenate([np.asarray(r[k], dtype=np.float32).reshape(shp) for r in results], axis=0)
    y_prompt = cat("yp", (1, cfg.seq, D))
    ngp = cat("ngp", (1, 8, 128, 128))[None]
    ncp = cat("ncp", (1, 3, 3072))[None]
    nrp = cat("nrp", (1, 16, 64, 64))[None]
    nsp = cat("nsp", (1, D))[None]
    y_sample = cat("ys", (ns, 4, D))
    ngs = cat("ngs", (ns, 8, 128, 128))[None]
    ncs = cat("ncs", (ns, 3, 3072))[None]
    nrs = cat("nrs", (ns, 16, 64, 64))[None]
    nss = cat("nss", (ns, D))[None]
    return (y_prompt, y_sample, ngp, ncp, nrp, nsp, ngs, ncs, nrs, nss)


def kernel(**inputs):
    cfg = Cfg(8, 16)
    if 'nc' not in _NC_CACHE:
        _NC_CACHE['nc'] = build(cfg)
    nc = _NC_CACHE['nc']
    maps = make_in_maps(cfg, inputs, 8)
    res = run_bass_kernel_spmd(nc, maps, core_ids=list(range(8)))
    return gather(cfg, res.results, 8)
```

```python
import numpy as np
from contextlib import ExitStack
import concourse.bass as bass
import concourse.mybir as mybir
from concourse.bass_utils import run_bass_kernel_spmd

F32 = mybir.dt.float32
BF16 = mybir.dt.bfloat16
AF = mybir.ActivationFunctionType
ALU = mybir.AluOpType
AX = mybir.AxisListType

ENGS = ['sp', 'act', 'dve', 'pe', 'pool']


class Op:
    __slots__ = ('eng', 'fn', 'idx', 'pos', 'deps', 'dma_sem', 'dma_ord', 'waits', 'signal', 'semval', 'vc')


class Sched:
    DMA_POOL = {'sp': 32, 'pool': 8, 'act': 8}

    def __init__(self, nc):
        self.nc = nc
        self.ops = []
        self.eng_ops = {e: [] for e in ENGS}
        self.acc = {}
        self.dma_count = {}
        self.dma_keys = []
        self.dma_rr = {}
        self.dma_last = {}

    @staticmethod
    def _box(a):
        t = a.tensor
        name = t.name
        shape = list(t.shape)
        isdram = type(t).__name__.startswith('DRam')
        off = int(a.offset)
        if isdram:
            lo = off
            hi = off
            for st, cnt in a.ap:
                if cnt > 1:
                    if st >= 0:
                        hi += st * (cnt - 1)
                    else:
                        lo += st * (cnt - 1)
            return name, (0, 0, lo, hi)
        row = 1
        for s in shape[1:]:
            row *= s
        p0 = off // row
        f0 = off % row
        pe = 0
        fe = 0
        for st, cnt in a.ap:
            if cnt <= 1 or st == 0:
                continue
            if st % row == 0:
                pe += (st // row) * (cnt - 1)
            else:
                fe += st * (cnt - 1)
        p1 = p0 + pe
        f1 = f0 + fe
        if type(t).__name__.startswith('PSum'):
            eb = 1024 if 'bfloat16' in str(t.dtype) else 512
            f0 = (f0 // eb) * eb
            f1 = (f1 // eb) * eb + eb - 1
            p0 = (p0 // 32) * 32
            p1 = (p1 // 32) * 32 + 31
        return name, (p0, p1, f0, f1)

    @staticmethod
    def _ovl(a, b):
        return not (a[1] < b[0] or b[1] < a[0] or a[3] < b[2] or b[3] < a[2])

    @staticmethod
    def _covers(a, b):
        return a[0] <= b[0] and a[1] >= b[1] and a[2] <= b[2] and a[3] >= b[3]

    def add(self, eng, fn, reads=(), writes=(), dma=None):
        op = Op()
        op.eng = eng
        op.fn = fn
        op.idx = len(self.ops)
        op.deps = set()
        op.dma_sem = dma
        op.dma_ord = None
        op.signal = False
        for a in reads:
            name, box = self._box(a)
            lst = self.acc.setdefault(name, [])
            for rec in lst:
                if rec[2] and self._ovl(rec[0], box):
                    op.deps.add(rec[1])
            lst.append([box, op.idx, False])
        for a in writes:
            name, box = self._box(a)
            lst = self.acc.setdefault(name, [])
            keep = []
            for rec in lst:
                if rec[1] == op.idx:
                    keep.append(rec)
                    continue
                if self._ovl(rec[0], box):
                    op.deps.add(rec[1])
                    if self._covers(box, rec[0]):
                        continue
                keep.append(rec)
            keep.append([box, op.idx, True])
            self.acc[name] = keep
        op.deps.discard(op.idx)
        if eng == 'pe' and dma is None:
            op.deps = set(d for d in op.deps if not (self.ops[d].eng == 'pe' and self.ops[d].dma_sem is None))
        if dma is not None:
            npool = self.DMA_POOL.get(eng, 4)
            self.dma_rr[eng] = self.dma_rr.get(eng, 0) + 1
            dma = '%s%d' % (eng, self.dma_rr[eng] % npool)
            op.dma_sem = dma
            if dma not in self.dma_count:
                self.dma_count[dma] = 0
                self.dma_keys.append(dma)
            else:
                op.deps.add(self.dma_last[dma])
            self.dma_count[dma] += 1
            op.dma_ord = self.dma_count[dma]
            self.dma_last[dma] = op.idx
        op.pos = len(self.eng_ops[eng]) + 1
        self.eng_ops[eng].append(op)
        self.ops.append(op)
        return op

    def finalize(self):
        ops = self.ops
        K = {e: {} for e in ENGS}
        dma_seen = {k: 0 for k in self.dma_keys}
        for op in ops:
            e = op.eng
            need = {}
            for d in op.deps:
                dop = ops[d]
                if dop.dma_sem is not None:
                    ch = 'dma:' + dop.dma_sem
                    cnt = dop.dma_ord
                else:
                    ch = dop.eng
                    cnt = dop.pos
                if ch not in need or need[ch][0] < cnt:
                    need[ch] = (cnt, dop)
            waits = []
            Ke = K[e]
            for ch, (cnt, dop) in sorted(need.items(), key=lambda kv: -kv[1][1].idx):
                if Ke.get(ch, 0) >= cnt:
                    continue
                waits.append((ch, cnt))
                Ke[ch] = cnt
                for c2, v2 in dop.vc.items():
                    if Ke.get(c2, 0) < v2:
                        Ke[c2] = v2
            op.waits = waits
            vc = dict(Ke)
            if op.dma_sem is not None:
                dma_seen[op.dma_sem] += 1
                vc['dma:' + op.dma_sem] = op.dma_ord
            else:
                vc[e] = op.pos
            op.vc = vc
        for op in ops:
            for ch, cnt in op.waits:
                if not ch.startswith('dma:'):
                    self.eng_ops[ch][cnt - 1].signal = True
        for e in ENGS:
            last = [o for o in self.eng_ops[e] if o.dma_sem is None]
            if last:
                last[-1].signal = True
        self.final_counts = {}
        for e in ENGS:
            c = 0
            for o in self.eng_ops[e]:
                if o.dma_sem is None and o.signal:
                    c += 1
                o.semval = c
            self.final_counts[e] = c

    def emit(self, stack):
        nc = self.nc
        self.finalize()
        sems = {}
        for e in ENGS:
            sems[e] = stack.enter_context(nc.semaphore('s_' + e))
        for k in self.dma_keys:
            sems['dma:' + k] = stack.enter_context(nc.semaphore('d_' + k))
        block = stack.enter_context(nc.Block())
        sched = self

        def run(engname, eng):
            for op in sched.eng_ops[engname]:
                for ch, cnt in op.waits:
                    if ch.startswith('dma:'):
                        eng.wait_ge(sems[ch], 16 * cnt)
                    else:
                        eng.wait_ge(sems[ch], sched.eng_ops[ch][cnt - 1].semval)
                ins = op.fn(eng)
                if op.dma_sem is not None:
                    ins.then_inc(sems['dma:' + op.dma_sem], 16)
                elif op.signal:
                    ins.then_inc(sems[engname], 1)
            if engname == 'sp':
                for k in sched.dma_keys:
                    eng.wait_ge(sems['dma:' + k], 16 * sched.dma_count[k])
                for e2 in ENGS:
                    if e2 != 'sp' and sched.final_counts[e2] > 0:
                        eng.wait_ge(sems[e2], sched.final_counts[e2])

        @block.sync
        def _(eng):
            run('sp', eng)

        @block.scalar
        def _(eng):
            run('act', eng)

        @block.vector
        def _(eng):
            run('dve', eng)

        @block.tensor
        def _(eng):
            run('pe', eng)

        @block.gpsimd
        def _(eng):
            run('pool', eng)


D = 1024
DIN = 10384
NMETA = 16
EPS = 1e-6
GN_EPS = 64 * 1e-5
OFF_A = 3072
NEG = -30000.0
NCMAX = 272
WCOLS = 10368
C_Q, C_K, C_V, C_Z, C_WDAD, C_RW, C_ZB, C_GA, C_GB = 0, 1024, 2048, 3072, 4096, 4224, 7296, 8320, 9344


class Cfg:
    def __init__(self, npb=8, ns=16):
        self.npb = npb
        self.ns = ns
        self.seq = 256 * npb


def build(cfg):
    nc = bass.Bass("TRN2", target_bir_lowering=False)
    st = ExitStack()
    S = Sched(nc)
    NS = cfg.ns
    SEQ = cfg.seq

    def din(name, shape):
        return nc.dram_tensor(name, shape, F32, kind="ExternalInput").ap()

    def dout(name, shape):
        return nc.dram_tensor(name, shape, F32, kind="ExternalOutput").ap()

    xp = din("xp", [SEQ, D])
    xs = din("xs", [NS * 4, D])
    sg_in = din("sg", [NS, 8, 128, 128])
    sc_in = din("sc", [NS * 3, 3072])
    sr_in = din("sr", [NS, 16, 64, 64])
    ss_in = din("ss", [NS, D])
    meta = din("meta", [NMETA, D])
    ln1_w = din("ln1_w", [D])
    w_in = din("w_in", [D, DIN])
    conv_w = din("conv_w", [4, 3072])
    a_log = din("a_log", [8, 1])
    dt_bias = din("dt_bias", [8, 1])
    gnorm_w = din("gnorm_w", [128, 1])
    w_out_a = din("w_out_a", [D, D])
    mu_in = din("mu", [33, 128])
    w0_in = din("w0", [8, 128])
    w2_in = din("w2", [64, D])
    a0_in = din("a0", [8, 128])
    a2_in = din("a2", [64, D])
    kk_in = din("k_k", [8, 128])
    ka_in = din("k_a", [8, 128])
    rk_in = din("r_k", [8, 128])
    gnw_in = din("gn_w", [8, 128])
    gnb_in = din("gn_b", [8, 128])
    w_out_b = din("w_out_b", [D, D])
    w_out = din("w_out", [D, D])
    lnf_w = din("lnf_w", [D])

    yp = dout("yp", [SEQ, D])
    ys = dout("ys", [NS * 4, D])
    ngp = dout("ngp", [8, 128, 128])
    ncp = dout("ncp", [3, 3072])
    nrp = dout("nrp", [16, 64, 64])
    nsp = dout("nsp", [1, D])
    ngs = dout("ngs", [NS, 8, 128, 128])
    ncs = dout("ncs", [NS * 3, 3072])
    nrs = dout("nrs", [NS, 16, 64, 64])
    nss = dout("nss", [NS, D])

    WG = {}
    for _nm, _w in (('q', 1024), ('k', 1024), ('v', 1024), ('z', 1024), ('wdad', 128), ('rw', 3072),
                    ('zb', 1024), ('ga', 1024), ('gb', 1024)):
        WG[_nm] = nc.dram_tensor("Wb_" + _nm, [128, 8, _w], BF16, kind="Internal").ap()
    Wab = nc.dram_tensor("Wab", [128, 8, 16], BF16, kind="Internal").ap()
    Wo = [nc.dram_tensor("Wo%d" % _i, [128, 8, D], BF16, kind="Internal").ap() for _i in range(3)]

    def sb(name, shape, dt=F32):
        return st.enter_context(nc.sbuf_tensor("s_" + name, shape, dt))

    def psum(name, shape, dt=F32):
        return st.enter_context(nc.psum_tensor("p_" + name, shape, dt))

    def rw(aps):
        return [a for a in aps if a is not None and not isinstance(a, (int, float))]

    def mm(out, lhsT, rhs, start=True, stop=True):
        S.add('pe', lambda e: e.matmul(out, lhsT=lhsT, rhs=rhs, start=start, stop=stop),
              reads=[lhsT, rhs], writes=[out])

    def tr(out, in_, idn):
        S.add('pe', lambda e: e.transpose(out=out, in_=in_, identity=idn), reads=[in_, idn], writes=[out])

    def act(out, in_, func, bias=None, scale=None, accum=None):
        kw = {}
        if bias is not None:
            kw['bias'] = bias
        if scale is not None:
            kw['scale'] = scale
        if accum is not None:
            kw['accum_out'] = accum
        S.add('act', lambda e: e.activation(out=out, in_=in_, func=func, **kw),
              reads=rw([in_, bias, scale]), writes=rw([out, accum]))

    def tt(eng, out, in0, in1, op):
        S.add(eng, lambda e: e.tensor_tensor(out=out, in0=in0, in1=in1, op=op), reads=[in0, in1], writes=[out])

    def ts(eng, out, in0, s1, s2=None, op0=ALU.mult, op1=None):
        kw = {}
        if op1 is not None:
            kw['op1'] = op1
        S.add(eng, lambda e: e.tensor_scalar(out=out, in0=in0, scalar1=s1, scalar2=s2, op0=op0, **kw),
              reads=rw([in0, s1, s2]), writes=[out])

    def stt(eng, out, in0, scalar, in1, op0, op1):
        S.add(eng, lambda e: e.scalar_tensor_tensor(out=out, in0=in0, scalar=scalar, in1=in1, op0=op0, op1=op1),
              reads=rw([in0, scalar, in1]), writes=[out])

    def cp(eng, out, in_):
        if eng == 'act':
            S.add('act', lambda e: e.copy(out=out, in_=in_), reads=[in_], writes=[out])
        else:
            S.add(eng, lambda e: e.tensor_copy(out=out, in_=in_), reads=[in_], writes=[out])

    def memset(eng, out, val):
        S.add(eng, lambda e: e.memset(out, val), writes=[out])

    def dma(eng, out, in_, key):
        S.add(eng, lambda e: e.dma_start(out=out, in_=in_), reads=[in_], writes=[out], dma=key)

    def recip(out, in_):
        S.add('dve', lambda e: e.reciprocal(out=out, in_=in_), reads=[in_], writes=[out])

    def rsum(out, in_):
        S.add('dve', lambda e: e.reduce_sum(out=out, in_=in_, axis=AX.X), reads=[in_], writes=[out])

    def scan(out, d0, d1):
        S.add('dve', lambda e: e.tensor_tensor_scan(out=out, data0=d0, data1=d1, initial=0.0,
                                                    op0=ALU.mult, op1=ALU.add), reads=[d0, d1], writes=[out])

    def asel(out, in_, pattern, cmp, fill, base, cm):
        S.add('pool', lambda e: e.affine_select(out=out, in_=in_, pattern=pattern, compare_op=cmp, fill=fill,
                                                base=base, channel_multiplier=cm), reads=[in_], writes=[out])

    def sigmoid_to(out, in_, tmp, bias=None, scale=1.0):
        if bias is not None:
            act(tmp, in_, AF.Exp, bias=bias, scale=-scale)
        else:
            act(tmp, in_, AF.Exp, scale=-scale)
        act(tmp, tmp, AF.Ln, bias=1.0)
        act(out, tmp, AF.Exp, scale=-1.0)

    def run_pipelined(gens, depth):
        active = []
        it = iter(gens)
        done = False
        while True:
            while len(active) < depth and not done:
                try:
                    active.append(next(it))
                except StopIteration:
                    done = True
            if not active:
                break
            for g in list(active):
                try:
                    next(g)
                except StopIteration:
                    active.remove(g)

    def rsqrt_to(out, in_, eps, mult=1.0):
        act(out, in_, AF.Ln, bias=eps, scale=mult)
        act(out, out, AF.Exp, scale=-0.5)

    PP = psum("PP", [128, 2, 512])
    PB = [psum("PB%d" % i, [128, 1024], BF16) for i in range(2)]
    PG = psum("PG", [128, 4, 512])
    ctr = {'pp': 0, 'pb': 0, 'pg': 0}

    def ppbank():
        i = ctr['pp'] % 2
        ctr['pp'] += 1
        return PP[:, i, :]

    def pbbank():
        i = ctr['pb'] % 2
        ctr['pb'] += 1
        return PB[i]

    def pgbank():
        i = ctr['pg'] % 4
        ctr['pg'] += 1
        return PG[:, i, :]

    ident = sb("ident", [128, 128])
    identb = sb("identb", [128, 128], BF16)
    ones = sb("ones", [128, 128])
    bones = sb("bones", [128, 128])
    nmS = sb("nmS", [64, 64])
    nmTI = sb("nmTI", [64, 64])
    m_st = sb("m_st", [64, 64])
    m_stT = sb("m_stT", [64, 64])
    m_inT = sb("m_inT", [64, 64])
    Esel = sb("Esel", [8, 8, 64])
    memset('pool', ident[:], 0.0)
    asel(ident[:], ident[:], [[-1, 128]], ALU.not_equal, 1.0, 0, 1)
    cp('pool', identb[:], ident[:])
    memset('pool', ones[:], 1.0)
    memset('pool', bones[:], 0.0)
    memset('pool', bones[0:64, 0:64], 1.0)
    memset('pool', bones[64:128, 64:128], 1.0)
    memset('pool', nmS[:], 0.0)
    asel(nmS[:], nmS[:], [[-1, 64]], ALU.is_gt, NEG, 0, 1)
    memset('pool', nmTI[:], 0.0)
    asel(nmTI[:], nmTI[:], [[1, 64]], ALU.is_ge, NEG, 0, -1)
    memset('pool', m_st[:], 1.0)
    asel(m_st[:], m_st[:], [[-1, 64]], ALU.is_gt, 0.0, 0, 1)
    memset('pool', m_stT[:], 1.0)
    asel(m_stT[:], m_stT[:], [[1, 64]], ALU.is_gt, 0.0, 0, -1)
    memset('pool', m_inT[:], 1.0)
    asel(m_inT[:], m_inT[:], [[1, 64]], ALU.is_ge, 0.0, 0, -1)
    memset('pool', Esel[:], 0.0)
    asel(Esel[:], Esel[:], [[-1, 8], [0, 64]], ALU.not_equal, 1.0, 0, 1)

    hrow = sb("hrow", [128, D])
    xrow = sb("xrow", [128, D])
    rstat = sb("rstat", [128, 4])
    memset('pool', hrow[:], 0.0)

    pstage = xrow[0:33, 0:128]

    def load_cols(name, src, ntile):
        t = sb(name, [128, ntile])
        dma('sp', pstage[0:ntile, :], src, 'c')
        pt = pgbank()
        tr(pt[:, 0:ntile], pstage[0:ntile, :], ident[0:ntile, 0:ntile])
        cp('act', t[:], pt[:, 0:ntile])
        return t

    mu = load_cols("mu", mu_in, 33)
    w0c = load_cols("w0c", w0_in, 8)
    a0c = load_cols("a0c", a0_in, 8)
    kkc = load_cols("kkc", kk_in, 8)
    kac = load_cols("kac", ka_in, 8)
    rkc = load_cols("rkc", rk_in, 8)
    gnwc = load_cols("gnwc", gnw_in, 8)
    gnbc = load_cols("gnbc", gnb_in, 8)
    nw0c = sb("nw0c", [128, 8])
    na0c = sb("na0c", [128, 8])
    ts('dve', nw0c[:], w0c[:], -1.0)
    ts('dve', na0c[:], a0c[:], -1.0)
    omka = sb("omka", [128, 8])
    ts('dve', omka[:], kac[:], -1.0, 1.0, op0=ALU.mult, op1=ALU.add)
    convw = sb("convw", [128, 24, 4])
    for q in range(3):
        dma('sp', xrow[0:4, :], conv_w[:, q * 1024:(q + 1) * 1024], 'c')
        for j in range(8):
            ft = q * 8 + j
            pt = pgbank()
            tr(pt[:, 0:4], xrow[0:4, j * 128:(j + 1) * 128], ident[0:4, 0:4])
            cp('act', convw[:, ft, :], pt[:, 0:4])
    ln1bc = sb("ln1bc", [128, D])
    lnfbc = sb("lnfbc", [128, D])
    dma('sp', ln1bc[:], ln1_w.partition_broadcast(128), 'c')
    dma('sp', lnfbc[:], lnf_w.partition_broadcast(128), 'c')
    alog = sb("alog", [8, 1])
    dtb = sb("dtb", [8, 1])
    nA = sb("nA", [8, 1])
    dma('sp', alog[:], a_log, 'c')
    dma('sp', dtb[:], dt_bias, 'c')
    act(nA[:], alog[:], AF.Exp)
    ts('dve', nA[:], nA[:], -1.0)
    gnw = sb("gnw", [128, 1])
    dma('sp', gnw[:], gnorm_w, 'c')
    wa2 = sb("wa2", [128, D])
    dma('sp', wa2[0:64, :], w2_in, 'c')
    dma('sp', wa2[64:128, :], a2_in, 'c')

    import os as _os2
    _KS = float(_os2.environ.get('KSTOP', '999'))
    w_v = w_in.rearrange("(dh dl) c -> dl dh c", dl=128)
    if _KS >= 1:
        for dh in range(8):
            dma('pool', Wab[:, dh, :], w_v[:, dh, OFF_A:OFF_A + 16], 'wcast')
        for nm, c0 in (('q', 0), ('k', 1024), ('v', 2048), ('z', 3088), ('wdad', 7184)):
            wd_ = WG[nm].shape[2]
            for dh in range(8):
                dma('pool', WG[nm][:, dh, :], w_v[:, dh, c0:c0 + wd_], 'wcast')
        for dh in range(8):
            for j in range(3):
                dma('pool', WG['rw'][:, dh, :].rearrange("d (p j f) -> d p j f", p=8, j=3)[:, :, j, :],
                    w_v[:, dh, 4112 + j * 1024:4112 + (j + 1) * 1024].rearrange("d (p f) -> d p f", p=8), 'wcast')
        for nm, c0 in (('zb', 7312), ('ga', 8336)):
            for dh in range(8):
                dma('pool', WG[nm][:, dh, :], w_v[:, dh, c0:c0 + 1024], 'wcast')
        for dh in range(8):
            dma('pool', Wo[0][:, dh, :], w_out_a.rearrange("(dh dl) c -> dl dh c", dl=128)[:, dh, :], 'wcast')
        for dh in range(8):
            dma('pool', WG['gb'][:, dh, :], w_v[:, dh, 9360:9360 + 1024], 'wcast')
        for i, wsrc in ((1, w_out_b), (2, w_out)):
            wv = wsrc.rearrange("(dh dl) c -> dl dh c", dl=128)
            for dh in range(8):
                dma('pool', Wo[i][:, dh, :], wv[:, dh, :], 'wcast')
    wab = sb("wab", [128, 8, 16], BF16)
    dma('sp', wab[:], Wab, 'c')

    NWB = 2
    wbuf = [sb("wbuf%d" % i, [128, 8, 512], BF16) for i in range(NWB)]
    wctr = [0]

    def wload(src):
        b = wbuf[wctr[0] % NWB]
        key = 'w%d' % (wctr[0] % NWB)
        wctr[0] += 1
        dma('sp', b[:, :, 0:src.shape[2]], src, key)
        return b

    hT = sb("hT", [128, 8, NCMAX], BF16)
    BT = [sb("BT%d" % i, [128, 8, NCMAX], BF16) for i in range(10)]
    oaT = sb("oaT", [128, 8, NCMAX], BF16)
    obT = BT[0]
    bg2 = sb("bg2", [128, 16, NCMAX], BF16)
    rg2 = sb("rg2", [128, 16, NCMAX], BF16)
    memset('pool', bg2[:], 0.0)
    memset('pool', rg2[:], 0.0)
    mgT = BT[9]
    MG = BT[7]

    Sg = sb("Sg", [128, 8, 128])
    Sgb = sb("Sgb", [128, 8, 128], BF16)
    Mr = sb("Mr", [128, 8, 64])
    Mrb = sb("Mrb", [128, 8, 64], BF16)
    NSQ = max(NS, 1)
    hist = sb("hist", [128, 24, 3 * NSQ])
    pbh = sb("pbh", [128, 33])
    scanm = sb("scanm", [128, NCMAX])

    cin = sb("cin", [128, NCMAX + 4])
    T = [sb("T%d" % i, [128, NCMAX + 4]) for i in range(15)]
    TX = [sb("TX%d" % i, [128, NCMAX + 4]) for i in range(14)]
    g8 = {n: sb("g8" + n, [8, NCMAX]) for n in ('g', 'gc', 'ngc', 'egc', 'beta', 'beg', 'etail')}
    NCHMAX = max(5, NS)
    eglD = sb("eglD", [8, NCHMAX, 8])
    eglbc = sb("eglbc", [128, NCHMAX, 8])
    eglR = sb("eglR", [128, 8, NCHMAX])

    def ctile(name, shape, dt=F32):
        return sb(name, [64] + shape, dt)
    sc24 = ctile("sc24", [24])
    kbg = ctile("kbg", [8, 128], BF16)
    ktl = ctile("ktl", [8, 128], BF16)
    vbt = ctile("vbt", [8, 128], BF16)
    Vtm, atl, ktlR = kbg, ktl, vbt
    tmpA = ctile("tmpA", [8, 64])
    DmS = ctile("DmS", [8, 64])
    DmT = ctile("DmT", [8, 64])
    RHS = tmpA
    Nm = [ctile("Nm%d" % i, [8, 64]) for i in range(2)]
    NTm = [ctile("NTm%d" % i, [8, 64]) for i in range(2)]
    RT = [ctile("RT%d" % i, [8, 64]) for i in range(2)]
    RTb = ctile("RTb", [8, 64], BF16)
    AqkT = ctile("AqkT", [8, 64], BF16)
    AakT = ctile("AakT", [8, 64], BF16)
    AqbT = ctile("AqbT", [8, 64], BF16)
    nwT = sb("nwT", [128, 8, 64], BF16)
    vnew = ctile("vnew", [8, 128], BF16)
    osq = ctile("osq", [4, 128])
    ysq = osq
    ost = ctile("ost", [8, 4])
    yst = ost
    onb = ctile("onb", [8, 128], BF16)
    ynb = onb
    Urw = ctile("Urw", [8, 64], BF16)
    strw = sb("strw", [64, 16, 64])
    cvt = strw[:].rearrange("p a b -> p (a b)")[:, 0:768]

    class Blk:
        pass

    blocks = []
    for b in range(cfg.npb):
        B = Blk()
        B.kind = 'p'
        B.idx = b
        B.ntok = 256 + (NMETA if b == 0 else 0)
        B.ncols = B.ntok
        B.nseg = 1
        B.L = B.ntok
        B.first = (b == 0)
        B.last = (b == cfg.npb - 1)
        chunks = []
        c = 0
        if b == 0:
            chunks.append((0, NMETA))
            c = NMETA
        while c < B.ntok:
            chunks.append((c, 64))
            c += 64
        B.segs = [dict(sid=0, chunks=chunks)]
        B.chunks = chunks
        tiles = []
        r = 0
        while r < B.ntok:
            n = min(128, B.ntok - r)
            tiles.append((r, n))
            r += n
        B.tiles = tiles
        blocks.append(B)
    if NS > 0:
        B = Blk()
        B.kind = 's'
        B.idx = 0
        B.ntok = 4 * NS
        B.ncols = 4 * NS + NS
        B.nseg = NS
        B.L = 4
        B.first = True
        B.last = True
        B.segs = [dict(sid=s, chunks=[(4 * s, 4)]) for s in range(NS)]
        B.chunks = [(4 * s, 4) for s in range(NS)]
        B.tiles = [(0, B.ntok)]
        blocks.append(B)

    def load_x_tile(B, r0, n, dst):
        if B.kind == 'p':
            if B.idx == 0 and r0 == 0:
                dma('sp', dst[0:NMETA, :], meta, 'x')
                dma('sp', dst[NMETA:n, :], xp[0:n - NMETA, :], 'x')
            else:
                s0 = r0 + 256 * B.idx - (NMETA if B.idx == 0 else 0)
                dma('sp', dst[0:n, :], xp[s0:s0 + n, :], 'x')
        else:
            dma('sp', dst[0:n, :], xs[0:n, :], 'x')

    def rms_rows(src, n, dst, wbc):
        act(dst[0:n, :], src[0:n, :], AF.Square, accum=rstat[0:n, 0:1])
        rsqrt_to(rstat[0:n, 3:4], rstat[0:n, 0:1], EPS, 1.0 / D)
        stt('dve', dst[0:n, :], src[0:n, :], rstat[0:n, 3:4], wbc[0:n, :], ALU.mult, ALU.mult)

    def front(B):
        for ti, (r0, n) in enumerate(B.tiles):
            load_x_tile(B, r0, n, xrow)
            rms_rows(xrow, n, hrow, ln1bc)
            if B.kind == 's':
                dma('sp', hrow[64:64 + NS, :], ss_in, 'x')
                for s in range(NS):
                    dma('sp', nss[s:s + 1, :], hrow[4 * s + 3:4 * s + 4, :], 'out')
            elif B.last and ti == len(B.tiles) - 1:
                dma('sp', nsp, hrow[n - 1:n, :], 'out')
            for dc in range(8):
                pt = ppbank()
                if B.kind == 's':
                    tr(pt[:, 0:64 + NS], hrow[0:64 + NS, dc * 128:(dc + 1) * 128], ident[0:64 + NS, 0:64 + NS])
                    cp('act', hT[:, dc, 0:B.ntok], pt[:, 0:B.ntok])
                    cp('act', hT[:, dc, B.ntok:B.ntok + NS], pt[:, 64:64 + NS])
                else:
                    tr(pt[:, 0:n], hrow[0:n, dc * 128:(dc + 1) * 128], ident[0:n, 0:n])
                    cp('act' if dc % 2 == 0 else 'dve', hT[:, dc, r0:r0 + n], pt[:, 0:n])

    def proj(wt, ti, ncols, M=128, wcol0=None):
        ps = ppbank()
        c0 = ti * 128 if wcol0 is None else wcol0
        for dc in range(8):
            mm(ps[0:M, 0:ncols], wt[:, dc, c0:c0 + M], hT[:, dc, 0:ncols], start=(dc == 0), stop=(dc == 7))
        return ps

    def tokview(B, ap2d):
        return ap2d.rearrange("p (s l) -> p s l", l=B.L)

    CHK = [None]

    def gdn_prep(B):
        nt = B.ntok
        qT, kT, qgT, vT, zsT = BT[0], BT[1], BT[2], BT[3], BT[4]
        memset('pool', scanm[:, 0:nt], 1.0)
        for (c0, C) in B.chunks:
            memset('pool', scanm[:, c0:c0 + 1], 0.0)
        psa = proj(wab, 0, nt, M=8, wcol0=0)
        act(T[0][0:8, 0:nt], psa[0:8, 0:nt], AF.Exp, bias=dtb[:])
        act(T[0][0:8, 0:nt], T[0][0:8, 0:nt], AF.Ln, bias=1.0)
        ts('dve', g8['g'][:, 0:nt], T[0][0:8, 0:nt], nA[:])
        psb = proj(wab, 0, nt, M=8, wcol0=8)
        sigmoid_to(g8['beta'][:, 0:nt], psb[0:8, 0:nt], T[0][0:8, 0:nt])
        scan(g8['gc'][:, 0:nt], scanm[0:8, 0:nt], g8['g'][:, 0:nt])
        ts('dve', g8['ngc'][:, 0:nt], g8['gc'][:, 0:nt], -1.0)
        act(g8['egc'][:, 0:nt], g8['gc'][:, 0:nt], AF.Exp)
        tt('dve', g8['beg'][:, 0:nt], g8['beta'][:, 0:nt], g8['egc'][:, 0:nt], ALU.mult)
        for ci, (c0, C) in enumerate(B.chunks):
            last = c0 + C - 1
            ts('dve', g8['etail'][:, c0:c0 + C], g8['ngc'][:, c0:c0 + C], g8['gc'][:, last:last + 1], op0=ALU.add)
            ts('dve', eglD[:, ci, :], ident[0:8, 0:8], g8['egc'][:, last:last + 1])
        act(g8['etail'][:, 0:nt], g8['etail'][:, 0:nt], AF.Exp)
        nch = len(B.chunks)
        pe_ = pgbank()
        mm(pe_[:, 0:nch * 8], ones[0:8, :], eglD[:, 0:nch, :].rearrange("p c h -> p (c h)"))
        cp('act', eglbc[:, 0:nch, :].rearrange("p c h -> p (c h)"), pe_[:, 0:nch * 8])
        wth = {}

        def qkv_tile(gi, gname, ti):
            ft = gi * 8 + ti
            if ti % 4 == 0:
                wth[(gi, ti // 4)] = wload(WG[gname][:, :, ti * 128:ti * 128 + 512])
            wt = wth[(gi, ti // 4)]
            ps = proj(wt, ti % 4, nt)
            if ft % 2 == 0:
                cinx, A0, A1, A2, A3 = cin, T[0], T[1], T[2], T[3]
            else:
                cinx, A0, A1, A2, A3 = T[4], T[5], T[6], T[7], T[8]
            cv = cinx[:, 0:B.nseg * (B.L + 3)].rearrange("p (s l) -> p s l", l=B.L + 3)
            cp('act', cv[:, :, 3:3 + B.L], tokview(B, ps[:, 0:nt]))
            hv = hist[:, ft, 0:3 * B.nseg].rearrange("p (s l) -> p s l", l=3)
            if B.first and B.kind == 'p':
                memset('pool', cv[:, :, 0:3], 0.0)
            else:
                cp('pool', cv[:, :, 0:3], hv)
            yield
            acc = tokview(B, A0[:, 0:nt])
            ts('dve', acc, cv[:, :, 0:B.L], convw[:, ft, 0:1])
            for i in range(1, 4):
                stt('dve', acc, cv[:, :, i:i + B.L], convw[:, ft, i:i + 1], acc, ALU.mult, ALU.add)
            cp('pool', hv, cv[:, :, B.L:B.L + 3])
            yield
            sigmoid_to(A1[:, 0:nt], A0[:, 0:nt], A1[:, 0:nt])
            yield
            if gname == 'v':
                tt('dve', vT[:, ti, 0:nt], A0[:, 0:nt], A1[:, 0:nt], ALU.mult)
                return
            tt('dve', A1[:, 0:nt], A0[:, 0:nt], A1[:, 0:nt], ALU.mult)
            act(A2[:, 0:nt], A1[:, 0:nt], AF.Square)
            yield
            pn = pgbank()
            mm(pn[:, 0:nt], ones[:], A2[:, 0:nt])
            act(A3[:, 0:nt], pn[:, 0:nt], AF.Ln, bias=1e-6)
            act(A3[:, 0:nt], A3[:, 0:nt], AF.Exp, scale=-0.5)
            yield
            if gname == 'k':
                tt('dve', kT[:, ti, 0:nt], A1[:, 0:nt], A3[:, 0:nt], ALU.mult)
            else:
                stt('dve', qT[:, ti, 0:nt], A1[:, 0:nt], 128.0 ** -0.5, A3[:, 0:nt], ALU.mult, ALU.mult)
                ts('dve', A3[0:8, 0:nt], g8['egc'][:, 0:nt], ident[0:8, ti:ti + 1])
                yield
                pq = pgbank()
                mm(pq[:, 0:nt], ones[0:8, :], A3[0:8, 0:nt])
                tt('dve', qgT[:, ti, 0:nt], qT[:, ti, 0:nt], pq[:, 0:nt], ALU.mult)

        run_pipelined((qkv_tile(gi, gname, ti) for gi, gname in enumerate(('q', 'k', 'v')) for ti in range(8)), 2)
        def z_tile(ti):
            if ti % 4 == 0:
                wth[('z', ti // 4)] = wload(WG['z'][:, :, ti * 128:ti * 128 + 512])
            ps = proj(wth[('z', ti // 4)], ti % 4, nt)
            zt = T[ti % 2]
            yield
            sigmoid_to(zt[:, 0:nt], ps[:, 0:nt], zt[:, 0:nt])
            yield
            tt('dve', zsT[:, ti, 0:nt], ps[:, 0:nt], zt[:, 0:nt], ALU.mult)

        run_pipelined((z_tile(ti) for ti in range(8)), 2)

    def tri_inverse(N0, NT0, C, nh=8):
        cur = 0
        Icb = ident[0:C, 0:C].unsqueeze(1).to_broadcast([C, nh, C])
        tt('dve', RT[0][0:C, 0:nh, 0:C], NT0, Icb, ALU.add)
        P, PT = N0, NT0
        nlev = 0
        while (1 << (nlev + 1)) < C:
            nlev += 1
        for m in range(1, nlev + 1):
            pP = pgbank()
            pPv = pP[0:C, 0:nh * C].rearrange("p (h c) -> p h c", c=C)
            for h in range(nh):
                mm(pPv[:, h, :], PT[:, h, :], P[:, h, :])
            newP = Nm[m % 2][0:C, 0:nh, 0:C]
            cp('act', newP, pPv)
            newPT = None
            if m < nlev:
                pT = pgbank()
                pTv = pT[0:C, 0:nh * C].rearrange("p (h c) -> p h c", c=C)
                for h in range(nh):
                    mm(pTv[:, h, :], P[:, h, :], PT[:, h, :])
                newPT = NTm[m % 2][0:C, 0:nh, 0:C]
                cp('act', newPT, pTv)
            pR = pgbank()
            pRv = pR[0:C, 0:nh * C].rearrange("p (h c) -> p h c", c=C)
            old = RT[cur][0:C, 0:nh, 0:C]
            for h in range(nh):
                mm(pRv[:, h, :], newP[:, h, :], old[:, h, :])
            cur ^= 1
            tt('dve', RT[cur][0:C, 0:nh, 0:C], pRv, old, ALU.add)
            P = newP
            PT = newPT
        return RT[cur][0:C, 0:nh, 0:C]

    def gdn_chunk(B, ci, c0, C):
        qT, kT, qgT, vT, zsT = BT[0], BT[1], BT[2], BT[3], BT[4]
        cs = slice(c0, c0 + C)
        p1 = pgbank()
        for i, nm in enumerate(('beta', 'beg', 'etail')):
            tr(p1[0:C, 8 * i:8 * i + 8], g8[nm][:, cs], ident[0:8, 0:8])
        cp('act', sc24[0:C, :], p1[0:C, 0:24])

        def scb(i, w):
            return sc24[0:C, 8 * i:8 * i + 8].unsqueeze(2).to_broadcast([C, 8, w])

        def hv3(ap2, w):
            return ap2.rearrange("p (h c) -> p h c", c=w)
        pk = pbbank()
        pkv = hv3(pk[0:C, :], 128)
        for h in range(8):
            tr(pkv[:, h, :], kT[:, h, cs], identb[:])
        tt('dve', kbg[0:C], pkv, scb(1, 128), ALU.mult)
        tt('dve', ktl[0:C], pkv, scb(2, 128), ALU.mult)
        pv = pbbank()
        pvv = hv3(pv[0:C, :], 128)
        for h in range(8):
            tr(pvv[:, h, :], vT[:, h, cs], identb[:])
        tt('dve', vbt[0:C], pvv, scb(0, 128), ALU.mult)
        pd = pgbank()
        pdv = hv3(pd[0:C, 0:8 * C], C)
        for h in range(8):
            mm(pdv[:, h, :], g8['gc'][:, cs], Esel[:, h, 0:C], start=True, stop=False)
            mm(pdv[:, h, :], Esel[:, h, 0:C], g8['ngc'][:, cs], start=False, stop=True)
        tt('dve', DmS[0:C, :, 0:C], pdv, nmS[0:C, 0:C].unsqueeze(1).to_broadcast([C, 8, C]), ALU.add)
        act(DmS[0:C, :, 0:C], DmS[0:C, :, 0:C], AF.Exp)
        pdT = pgbank()
        pdTv = hv3(pdT[0:C, 0:8 * C], C)
        for h in range(8):
            mm(pdTv[:, h, :], Esel[:, h, 0:C], g8['gc'][:, cs], start=True, stop=False)
            mm(pdTv[:, h, :], g8['ngc'][:, cs], Esel[:, h, 0:C], start=False, stop=True)
        tt('dve', DmT[0:C, :, 0:C], pdTv, nmTI[0:C, 0:C].unsqueeze(1).to_broadcast([C, 8, C]), ALU.add)
        act(DmT[0:C, :, 0:C], DmT[0:C, :, 0:C], AF.Exp)
        pg_ = pgbank()
        pgv = hv3(pg_[0:C, 0:8 * C], C)
        for h in range(8):
            mm(pgv[:, h, :], kT[:, h, cs], kT[:, h, cs])
        stt('dve', tmpA[0:C, :, 0:C], pgv, -1.0, DmS[0:C, :, 0:C], ALU.mult, ALU.mult)
        N0 = Nm[0][0:C, :, 0:C]
        tt('dve', N0, tmpA[0:C, :, 0:C], scb(0, C), ALU.mult)
        pq = pgbank()
        pqv = hv3(pq[0:C, 0:8 * C], C)
        for h in range(8):
            mm(pqv[:, h, :], kT[:, h, cs], qT[:, h, cs])
        tt('dve', AqkT[0:C, :, 0:C], pqv, DmT[0:C, :, 0:C], ALU.mult)
        pn = pgbank()
        pnv = hv3(pn[0:C, 0:8 * C], C)
        for h in range(8):
            tr(pnv[:, h, :], N0[:, h, :], ident[0:C, 0:C])
        NT0 = NTm[0][0:C, :, 0:C]
        cp('act', NT0, pnv)
        RTf = tri_inverse(N0, NT0, C)
        cp('act', RTb[0:C, :, 0:C], RTf)
        pw = pgbank()
        pwv = hv3(pw[:, 0:8 * C], C)
        for h in range(8):
            mm(pwv[:, h, :], kbg[0:C, h, :], RTb[0:C, h, 0:C])
        ts('dve', nwT[:, :, 0:C], pwv, -1.0)
        for half in range(2):
            p2 = pgbank()
            p2v = hv3(p2[0:C, :], 128)
            for hh in range(4):
                h = half * 4 + hh
                mm(p2v[:, hh, :], RTb[0:C, h, 0:C], vbt[0:C, h, :], start=True, stop=False)
                mm(p2v[:, hh, :], nwT[:, h, 0:C], Sgb[:, h, :], start=False, stop=True)
            cp('act', vnew[0:C, half * 4:half * 4 + 4, :], p2v)
        for half in range(2):
            p3 = pgbank()
            p3v = hv3(p3[0:C, :], 128)
            for hh in range(4):
                h = half * 4 + hh
                mm(p3v[:, hh, :], qgT[:, h, cs], Sgb[:, h, :], start=True, stop=False)
                mm(p3v[:, hh, :], AqkT[0:C, h, 0:C], vnew[0:C, h, :], start=False, stop=True)
            hs = slice(half * 4, half * 4 + 4)
            act(osq[0:C], p3v, AF.Square)
            rsum(ost[0:C, hs, 0:1], osq[0:C])
            rsqrt_to(ost[0:C, hs, 3:4], ost[0:C, hs, 0:1], EPS, 1.0 / 128)
            tt('dve', onb[0:C, hs, :], p3v, ost[0:C, hs, 3:4].to_broadcast([C, 4, 128]), ALU.mult)
        po = pbbank()
        pov = hv3(po[:, 0:8 * C], C)
        for h in range(8):
            tr(pov[:, h, :], onb[0:C, h, :], identb[0:C, 0:C])
        stt('dve', oaT[:, :, cs], pov, gnw[:, 0:1], zsT[:, :, cs], ALU.mult, ALU.mult)
        for half in range(2):
            p4 = pgbank()
            p4v = hv3(p4[:, :], 128)
            for hh in range(4):
                h = half * 4 + hh
                mm(p4v[:, hh, :], ktl[0:C, h, :], vnew[0:C, h, :])
            for hh in range(4):
                h = half * 4 + hh
                stt('dve', Sg[:, h, :], Sg[:, h, :], eglbc[:, ci, h:h + 1], p4v[:, hh, :], ALU.mult, ALU.add)
        cp('pool', Sgb[:], Sg[:])

    def rwkv_prep(B):
        nt = B.ntok
        ncols = B.ncols
        bgT, agT, kgT, rgT, atT, ktT, rvT, bonT, zbT = BT[0], BT[1], BT[2], BT[3], BT[4], BT[5], BT[6], BT[7], BT[8]

        def mixed(ps, fidx, dst, rawbuf=cin):
            raw = rawbuf[:, 0:B.nseg * (B.L + 1)].rearrange("p (s l) -> p s l", l=B.L + 1)
            cp('act', raw[:, :, 1:1 + B.L], tokview(B, ps[:, 0:nt]))
            if B.kind == 'p':
                if B.first:
                    memset('pool', raw[:, :, 0:1], 0.0)
                else:
                    cp('pool', raw[:, 0, 0:1], pbh[:, fidx:fidx + 1])
                cp('pool', pbh[:, fidx:fidx + 1], raw[:, 0, B.L:B.L + 1])
            else:
                cp('act', raw[:, :, 0:1], ps[:, nt:nt + B.nseg].unsqueeze(2))
            dv = tokview(B, dst)
            tt('dve', dv, raw[:, :, 0:B.L], raw[:, :, 1:1 + B.L], ALU.subtract)
            stt('dve', dv, dv, mu[:, fidx:fidx + 1], raw[:, :, 1:1 + B.L], ALU.mult, ALU.add)

        wt = wload(WG['wdad'])
        ps = proj(wt, 0, ncols)
        mixed(ps, 24, T[0][:, 0:nt])
        sigmoid_to(T[1][0:64, 0:nt], T[0][0:64, 0:nt], T[1][0:64, 0:nt], scale=2.0)
        ts('dve', T[1][0:64, 0:nt], T[1][0:64, 0:nt], 2.0, -1.0, op0=ALU.mult, op1=ALU.add)
        def pair_gen(p):
            if p % 2 == 0:
                rawb = cin
                TT = {i: T[i] for i in range(2, 15)}
            else:
                rawb = TX[0]
                TT = {i: TX[i - 1] for i in range(2, 15)}
            Tr, Tk, Tv = TT[2], TT[3], TT[4]
            wt = wload(WG['rw'][:, :, p * 384:(p + 1) * 384])
            for j, dst in enumerate((Tr, Tk, Tv)):
                ps = proj(wt, j, ncols)
                mixed(ps, 8 * j + p, dst[:, 0:nt], rawb)
                yield
            r_p, k_p, v_p = Tr[:, 0:nt], Tk[:, 0:nt], Tv[:, 0:nt]
            wlog, gc, eg, eng, egp = (TT[i][:, 0:nt] for i in (5, 6, 7, 8, 9))
            a_, kk, rn, kb2, kg32 = (TT[i][:, 0:nt] for i in (10, 11, 12, 13, 14))
            psw = pgbank()
            mm(psw[:, 0:nt], wa2[0:64, p * 128:(p + 1) * 128], T[1][0:64, 0:nt])
            sigmoid_to(wlog, psw[:, 0:nt], wlog, bias=nw0c[:, p:p + 1])
            psa = pgbank()
            mm(psa[:, 0:nt], wa2[64:128, p * 128:(p + 1) * 128], T[0][64:128, 0:nt])
            sigmoid_to(a_, psa[:, 0:nt], a_, bias=na0c[:, p:p + 1])
            ts('pool', kk, k_p, kkc[:, p:p + 1])
            act(rn, kk, AF.Square)
            yield
            ts('pool', wlog, wlog, -float(np.exp(-0.5)))
            scan(gc, scanm[:, 0:nt], wlog)
            pss = pgbank()
            mm(pss[:, 0:nt], bones[:], rn)
            act(rn, pss[:, 0:nt], AF.Ln, bias=1e-6)
            act(rn, rn, AF.Exp, scale=-0.5)
            yield
            act(eg, gc, AF.Exp)
            act(eng, gc, AF.Exp, scale=-1.0)
            tt('pool', egp, gc, wlog, ALU.subtract)
            act(egp, egp, AF.Exp)
            tt('dve', kk, kk, rn, ALU.mult)
            ts('dve', kb2, a_, kac[:, p:p + 1], omka[:, p:p + 1], op0=ALU.mult, op1=ALU.add)
            tt('dve', kb2, kb2, k_p, ALU.mult)
            yield
            for ci, (c0, C) in enumerate(B.chunks):
                cp('pool', eglR[:, p, ci:ci + 1], eg[:, c0 + C - 1:c0 + C])
            for e_ in range(2):
                hsl = slice(64 * e_, 64 * e_ + 64)
                tt('dve', bg2[hsl, 2 * p + e_, 0:nt], kk[hsl], egp[hsl], ALU.mult)
            ka = wlog
            tt('pool', ka, kk, a_, ALU.mult)
            tt('dve', kg32, kb2, eng, ALU.mult)
            yield
            ag32 = gc
            stt('dve', ag32, ka, -1.0, eng, ALU.mult, ALU.mult)
            cp('pool', kgT[:, p, 0:nt], kg32)
            for e_ in range(2):
                hsl = slice(64 * e_, 64 * e_ + 64)
                tt('dve', rg2[hsl, 2 * p + e_, 0:nt], r_p[hsl], eg[hsl], ALU.mult)
            cp('act', rvT[:, p, 0:nt], v_p)
            yield
            cp('pool', agT[:, p, 0:nt], ag32)
            for ci, (c0, C) in enumerate(B.chunks):
                ts('pool', atT[:, p, c0:c0 + C], ag32[:, c0:c0 + C], eglR[:, p, ci:ci + 1])
                ts('pool', ktT[:, p, c0:c0 + C], kg32[:, c0:c0 + C], eglR[:, p, ci:ci + 1])
            prod = egp
            stt('dve', prod, r_p, rkc[:, p:p + 1], kb2, ALU.mult, ALU.mult)
            yield
            psb_ = pgbank()
            mm(psb_[:, 0:nt], bones[:], prod)
            tt('dve', bonT[:, p, 0:nt], psb_[:, 0:nt], v_p, ALU.mult)

        run_pipelined((pair_gen(p) for p in range(8)), 2)
        wth = {}
        def zb_tile(ti):
            if ti % 4 == 0:
                wth[ti // 4] = wload(WG['zb'][:, :, ti * 128:ti * 128 + 512])
            ps = proj(wth[ti // 4], ti % 4, ncols)
            if ti % 2 == 0:
                zr, z1, z2 = cin, T[14], T[13]
            else:
                zr, z1, z2 = TX[0], TX[13], TX[12]
            mixed(ps, 25 + ti, z1[:, 0:nt], zr)
            yield
            sigmoid_to(z2[:, 0:nt], z1[:, 0:nt], z2[:, 0:nt])
            yield
            tt('dve', zbT[:, ti, 0:nt], z1[:, 0:nt], z2[:, 0:nt], ALU.mult)

        run_pipelined((zb_tile(ti) for ti in range(8)), 2)

    def rwkv_chunk(B, ci, c0, C):
        chk = CHK[0]
        bgT, agT, kgT, rgT, atT, ktT, rvT, bonT, zbT, ynT = BT
        cs = slice(c0, c0 + C)

        def hv3(ap2, w):
            return ap2.rearrange("p (h c) -> p h c", c=w)
        import os as _os3
        _kv = int(_os3.environ.get('KVAR', '3'))
        for k_, (src, dst) in enumerate(((rvT, Vtm), (atT, atl), (ktT, ktlR))[:_kv]):
            pk = pbbank()
            pkv = hv3(pk[0:C, :], 128)
            for p in range(8):
                tr(pkv[:, p, :], src[:, p, cs], identb[:])
            cp('act' if k_ == 0 else 'dve', dst[0:C], pkv)
        Vv = Vtm[0:C].rearrange("p a (e v) -> p (a e) v", e=2)
        chk(6.1)
        for half in range(2):
            heads = list(range(half * 8, half * 8 + 8))

            def hop(T_, h):
                if T_ is bg2 or T_ is rg2:
                    return T_[:, h, cs]
                return T_[:, h // 2, cs]

            import os as _os4
            _ksc = [int(_os4.environ.get('KSC', '99'))]

            def score(lt, rt, mask, dst):
                _ksc[0] -= 1
                if _ksc[0] < 0:
                    return
                ps_ = pgbank()
                psv = hv3(ps_[0:C, 0:8 * C], C)
                for hi, h in enumerate(heads):
                    mm(psv[:, hi, :], hop(lt, h), hop(rt, h))
                tt('dve', dst, psv, mask[0:C, 0:C].unsqueeze(1).to_broadcast([C, 8, C]), ALU.mult)
            N0 = Nm[0][0:C, :, 0:C]
            NT0 = NTm[0][0:C, :, 0:C]
            score(bg2, agT, m_st, N0)
            score(agT, bg2, m_stT, NT0)
            score(kgT, bg2, m_stT, AakT[0:C, :, 0:C])
            score(agT, rg2, m_inT, AqbT[0:C, :, 0:C])
            score(kgT, rg2, m_inT, AqkT[0:C, :, 0:C])
            chk(6.2)
            RTf = tri_inverse(N0, NT0, C)
            chk(6.3)
            p1 = pgbank()
            p1v = hv3(p1[0:C, 0:8 * 64], 64)
            for hi, h in enumerate(heads):
                p, e = h // 2, h % 2
                mm(p1v[:, hi, :], hop(bg2, h), Mrb[:, p, :], start=True, stop=False)
                mm(p1v[:, hi, :], AakT[0:C, hi, 0:C], Vv[:, h, :], start=False, stop=True)
            cp('act', RHS[0:C], p1v)
            p2 = pgbank()
            p2v = hv3(p2[0:C, 0:8 * 64], 64)
            for hi, h in enumerate(heads):
                mm(p2v[:, hi, :], RTf[:, hi, :], RHS[0:C, hi, :])
            cp('act', Urw[0:C], p2v)
            p3 = pgbank()
            p3v = hv3(p3[0:C, 0:8 * 64], 64)
            for hi, h in enumerate(heads):
                p, e = h // 2, h % 2
                mm(p3v[:, hi, :], hop(rg2, h), Mrb[:, p, :], start=True, stop=False)
                mm(p3v[:, hi, :], AqbT[0:C, hi, 0:C], Urw[0:C, hi, :], start=False, stop=False)
                mm(p3v[:, hi, :], AqkT[0:C, hi, 0:C], Vv[:, h, :], start=False, stop=True)
            chk(6.4)
            ysv = ysq[0:C].rearrange("p a (b v) -> p (a b) v", b=2)
            Ysb = tmpA[0:C]
            cp('act', Ysb, p3v)
            rsum(yst[0:C, :, 0:1], Ysb)
            chk(6.41)
            act(ysv, Ysb, AF.Square)
            rsum(yst[0:C, :, 1:2], ysv)
            chk(6.42)
            ts('dve', yst[0:C, :, 0:1], yst[0:C, :, 0:1], 1.0 / 64)
            tt('dve', yst[0:C, :, 2:3], yst[0:C, :, 0:1], yst[0:C, :, 0:1], ALU.mult)
            stt('dve', yst[0:C, :, 1:2], yst[0:C, :, 1:2], 1.0 / 64, yst[0:C, :, 2:3], ALU.mult, ALU.subtract)
            ts('dve', yst[0:C, :, 1:2], yst[0:C, :, 1:2], GN_EPS, op0=ALU.add)
            act(yst[0:C, :, 3:4], yst[0:C, :, 1:2], AF.Ln)
            act(yst[0:C, :, 3:4], yst[0:C, :, 3:4], AF.Exp, scale=-0.5)
            chk(6.43)
            tt('dve', ysv, Ysb, yst[0:C, :, 0:1].to_broadcast([C, 8, 64]), ALU.subtract)
            ynv8 = ynb[0:C, 0:4, :].rearrange("p a (b v) -> p (a b) v", b=2)
            tt('dve', ynv8, ysv, yst[0:C, :, 3:4].to_broadcast([C, 8, 64]), ALU.mult)
            chk(6.44)
            po = pbbank()
            pov = hv3(po[:, 0:4 * C], C)
            for a in range(4):
                tr(pov[:, a, :], ynb[0:C, a, :], identb[0:C, 0:C])
            cp('act', ynT[:, half * 4:half * 4 + 4, cs], pov)
            chk(6.5)
            pA = pgbank()
            pAv = hv3(pA[:, 0:4 * 64], 64)
            pBk = pgbank()
            pBv = hv3(pBk[:, 0:4 * 64], 64)
            for a in range(4):
                p = half * 4 + a
                for e, pv in ((0, pAv), (1, pBv)):
                    hi = 2 * a + e
                    h = 2 * p + e
                    mm(pv[:, a, :], atl[0:C, p, :], Urw[0:C, hi, :], start=True, stop=False)
                    mm(pv[:, a, :], ktlR[0:C, p, :], Vv[:, h, :], start=False, stop=True)
            ps_ = slice(half * 4, half * 4 + 4)
            for e, pv in ((0, pAv), (1, pBv)):
                rows = slice(64 * e, 64 * e + 64)
                tt('dve', Mr[rows, ps_, :], Mr[rows, ps_, :],
                   eglR[rows, ps_, ci:ci + 1].to_broadcast([64, 4, 64]), ALU.mult)
                tt('dve', Mr[rows, ps_, :], Mr[rows, ps_, :], pv[rows], ALU.add)
        cp('pool', Mrb[:], Mr[:])

    def out_stage(B):
        nt = B.ntok
        bonT, zbT, ynT = BT[7], BT[8], BT[9]
        for p in range(8):
            ob_t = T[p % 2]
            ts('dve', ob_t[:, 0:nt], ynT[:, p, 0:nt], gnwc[:, p:p + 1], gnbc[:, p:p + 1], op0=ALU.mult, op1=ALU.add)
            tt('pool', ob_t[:, 0:nt], ob_t[:, 0:nt], bonT[:, p, 0:nt], ALU.add)
            tt('dve', obT[:, p, 0:nt], ob_t[:, 0:nt], zbT[:, p, 0:nt], ALU.mult)
        wgh = {}

        def gate_tile(bi, src, gcol, ti):
            if ti % 4 == 0:
                wgh[(bi, ti // 4)] = (wload(WG[gcol][:, :, ti * 128:ti * 128 + 512]),
                                      wload(Wo[bi][:, :, ti * 128:ti * 128 + 512]))
            wg, wo = wgh[(bi, ti // 4)]
            psg = proj(wg, ti % 4, nt)
            G1 = T[1] if ti % 2 == 0 else T[3]
            G2 = T[2] if ti % 2 == 0 else T[4]
            pbr = pgbank()
            for ec in range(8):
                mm(pbr[:, 0:nt], wo[:, ec, (ti % 4) * 128:(ti % 4 + 1) * 128], src[:, ec, 0:nt], start=(ec == 0), stop=(ec == 7))
            yield
            sigmoid_to(G1[:, 0:nt], psg[:, 0:nt], G1[:, 0:nt])
            yield
            if bi == 0:
                tt('dve', MG[:, ti, 0:nt], pbr[:, 0:nt], G1[:, 0:nt], ALU.mult)
            else:
                tt('dve', G2[:, 0:nt], pbr[:, 0:nt], G1[:, 0:nt], ALU.mult)
                tt('dve', mgT[:, ti, 0:nt], G2[:, 0:nt], MG[:, ti, 0:nt], ALU.add)

        for bi, (src, gcol) in enumerate(((oaT, 'ga'), (obT, 'gb'))):
            run_pipelined((gate_tile(bi, src, gcol, ti) for ti in range(8)), 2)
        wos = [wload(Wo[2][:, :, 0:512]), wload(Wo[2][:, :, 512:1024])]
        for ti, (r0, n) in enumerate(B.tiles):
            load_x_tile(B, r0, n, hrow)
            for half in range(2):
                px = ppbank()
                for dc in range(8):
                    mm(px[0:n, :], mgT[:, dc, r0:r0 + n], wos[half][:, dc, :], start=(dc == 0), stop=(dc == 7))
                tt('dve', xrow[0:n, half * 512:(half + 1) * 512], px[0:n, :], hrow[0:n, half * 512:(half + 1) * 512], ALU.add)
            rms_rows(xrow, n, hrow, lnfbc)
            if B.kind == 'p':
                if B.idx == 0 and r0 == 0:
                    dma('sp', yp[0:n - NMETA, :], hrow[NMETA:n, :], 'out')
                else:
                    s0 = r0 + 256 * B.idx - (NMETA if B.idx == 0 else 0)
                    dma('sp', yp[s0:s0 + n, :], hrow[0:n, :], 'out')
            else:
                dma('sp', ys[0:n, :], hrow[0:n, :], 'out')

    def load_gdn_state(s):
        dma('sp', Sg[:], sg_in[s].rearrange("h k v -> k h v"), 'st')
        cp('pool', Sgb[:], Sg[:])

    def store_gdn_state(dst):
        dma('sp', dst.rearrange("h k v -> k h v"), Sg[:], 'out')

    def load_rwkv_state(s):
        dma('sp', strw[:], sr_in[s].rearrange("h v k -> v h k"), 'st')
        for p in range(8):
            pt = pgbank()
            tr(pt[:, 0:64], strw[:, 2 * p:2 * p + 2, :].rearrange("p a b -> p (a b)"), ident[0:64, 0:64])
            cp('act', Mr[:, p, :], pt[:, 0:64])
        cp('pool', Mrb[:], Mr[:])

    def store_rwkv_state(dst):
        for p in range(8):
            pt = pgbank()
            tr(pt[0:64, 0:128], Mr[:, p, :], ident[:])
            cp('act', strw[:, 2 * p:2 * p + 2, :].rearrange("p a b -> p (a b)"), pt[0:64, 0:128])
        dma('sp', dst.rearrange("h v k -> v h k"), strw[:], 'out')

    def store_conv(B, dst):
        n3 = 3 * B.nseg
        for q in range(4):
            for j in range(6):
                ft = q * 6 + j
                pt = pgbank()
                tr(pt[0:n3, 0:128], hist[:, ft, 0:n3], ident[:])
                cp('act' if j % 2 == 0 else 'dve', cvt[0:n3, j * 128:(j + 1) * 128], pt[0:n3, 0:128])
            dma('sp', dst[:, q * 768:(q + 1) * 768], cvt[0:n3, :], 'out')

    def load_conv_hist(B):
        n3 = 3 * B.nseg
        for q in range(4):
            dma('sp', cvt[0:n3, :], sc_in[:, q * 768:(q + 1) * 768], 'st')
            for j in range(6):
                ft = q * 6 + j
                pt = pgbank()
                tr(pt[:, 0:n3], cvt[0:n3, j * 128:(j + 1) * 128], ident[0:n3, 0:n3])
                cp('act' if j % 2 == 0 else 'dve', hist[:, ft, 0:n3], pt[:, 0:n3])

    import os as _os
    KSTOP = float(_os.environ.get('KSTOP', '999'))

    class _Stop(Exception):
        pass

    def chk(n):
        if n > KSTOP:
            raise _Stop()

    CHK[0] = chk

    def main_prog():
      memset('pool', Sg[:], 0.0)
      memset('pool', Sgb[:], 0.0)
      memset('pool', Mr[:], 0.0)
      memset('pool', Mrb[:], 0.0)
      for B in blocks:
        chk(2)
        front(B)
        if B.kind == 's':
            load_conv_hist(B)
        chk(3)
        gdn_prep(B)
        ci = 0
        for seg in B.segs:
            if B.kind == 's':
                load_gdn_state(seg['sid'])
            for (c0, C) in seg['chunks']:
                chk(4)
                gdn_chunk(B, ci, c0, C)
                ci += 1
            if B.kind == 's':
                store_gdn_state(ngs[seg['sid']])
        if B.kind == 'p' and B.last:
            store_gdn_state(ngp)
        if B.last:
            store_conv(B, ncp if B.kind == 'p' else ncs)
        chk(5)
        rwkv_prep(B)
        ci = 0
        for seg in B.segs:
            if B.kind == 's':
                load_rwkv_state(seg['sid'])
            for (c0, C) in seg['chunks']:
                chk(6)
                rwkv_chunk(B, ci, c0, C)
                ci += 1
            if B.kind == 's':
                store_rwkv_state(nrs[seg['sid']])
        if B.kind == 'p' and B.last:
            store_rwkv_state(nrp)
        chk(7)
        out_stage(B)

    try:
        main_prog()
    except _Stop:
        pass
    if _os.environ.get('KDBG'):
        dbg = dout("dbg", [10, 128, 8, NCMAX])
        for i in range(9):
            for p in range(8):
                cp('dve', T[0][:, 0:NCMAX], BT[i][:, p, :])
                dma('sp', dbg[i, :, p, :], T[0][:, 0:NCMAX], 'out')

    S.emit(st)
    st.close()
    return nc


_NC_CACHE = {}


def make_in_maps(cfg, inputs, ncores):
    f = lambda a: np.ascontiguousarray(np.asarray(a, dtype=np.float32))
    ns = cfg.ns
    shared = {
        "meta": f(inputs["meta_tokens"]),
        "ln1_w": f(inputs["ln1_w"]).reshape(D),
        "w_in": f(inputs["w_in"]).reshape(D, DIN),
        "conv_w": f(inputs["gdn_conv_w"]).reshape(4, 3072),
        "a_log": f(inputs["gdn_a_log"]).reshape(8, 1),
        "dt_bias": f(inputs["gdn_dt_bias"]).reshape(8, 1),
        "gnorm_w": f(inputs["gdn_norm_w"]).reshape(128, 1),
        "w_out_a": f(inputs["w_out_a"]).reshape(D, D),
        "mu": f(inputs["rwkv_mu"]).reshape(33, 128),
        "w0": f(inputs["rwkv_w0"]).reshape(8, 128),
        "w2": f(inputs["rwkv_w2"]).reshape(64, D),
        "a0": f(inputs["rwkv_a0"]).reshape(8, 128),
        "a2": f(inputs["rwkv_a2"]).reshape(64, D),
        "k_k": f(inputs["rwkv_k_k"]).reshape(8, 128),
        "k_a": f(inputs["rwkv_k_a"]).reshape(8, 128),
        "r_k": f(inputs["rwkv_r_k"]).reshape(8, 128),
        "gn_w": f(inputs["rwkv_gn_w"]).reshape(8, 128),
        "gn_b": f(inputs["rwkv_gn_b"]).reshape(8, 128),
        "w_out_b": f(inputs["w_out_b"]).reshape(D, D),
        "w_out": f(inputs["w_out"]).reshape(D, D),
        "lnf_w": f(inputs["lnf_w"]).reshape(D),
    }
    maps = []
    for c in range(ncores):
        m = dict(shared)
        m["xp"] = f(inputs["x_prompt"][c])
        if ns > 0:
            sl = slice(c * ns, (c + 1) * ns)
            m["xs"] = f(inputs["x_sample"][sl]).reshape(ns * 4, D)
            m["sg"] = f(inputs["state_gdn"][0, sl])
            m["sc"] = f(inputs["state_gdn_conv"][0, sl]).reshape(ns * 3, 3072)
            m["sr"] = f(inputs["state_rwkv"][0, sl])
            m["ss"] = f(inputs["state_shift"][0, sl])
        maps.append(m)
    return maps


def gather(cfg, results, ncores):
    ns = cfg.ns
    cat = lambda k, shp: np.concatenate([np.asarray(r[k], dtype=np.float32).reshape(shp) for r in results], axis=0)
    y_prompt = cat("yp", (1, cfg.seq, D))
    ngp = cat("ngp", (1, 8, 128, 128))[None]
    ncp = cat("ncp", (1, 3, 3072))[None]
    nrp = cat("nrp", (1, 16, 64, 64))[None]
    nsp = cat("nsp", (1, D))[None]
    y_sample = cat("ys", (ns, 4, D))
    ngs = cat("ngs", (ns, 8, 128, 128))[None]
    ncs = cat("ncs", (ns, 3, 3072))[None]
    nrs = cat("nrs", (ns, 16, 64, 64))[None]
    nss = cat("nss", (ns, D))[None]
    return (y_prompt, y_sample, ngp, ncp, nrp, nsp, ngs, ncs, nrs, nss)


def kernel(**inputs):
    cfg = Cfg(8, 16)
    if 'nc' not in _NC_CACHE:
        _NC_CACHE['nc'] = build(cfg)
    nc = _NC_CACHE['nc']
    maps = make_in_maps(cfg, inputs, 8)
    res = run_bass_kernel_spmd(nc, maps, core_ids=list(range(8)))
    return gather(cfg, res.results, 8)
```

```python
import numpy as np
from contextlib import ExitStack
import concourse.bass as bass
import concourse.mybir as mybir
from concourse.bass_utils import run_bass_kernel_spmd

F32 = mybir.dt.float32
BF16 = mybir.dt.bfloat16
AF = mybir.ActivationFunctionType
ALU = mybir.AluOpType
AX = mybir.AxisListType

ENGS = ['sp', 'act', 'dve', 'pe', 'pool']


class Op:
    __slots__ = ('eng', 'fn', 'idx', 'pos', 'deps', 'dma_sem', 'dma_ord', 'waits', 'signal', 'semval', 'vc')


class Sched:
    DMA_POOL = {'sp': 32, 'pool': 8, 'act': 8}

    def __init__(self, nc):
        self.nc = nc
        self.ops = []
        self.eng_ops = {e: [] for e in ENGS}
        self.acc = {}
        self.dma_count = {}
        self.dma_keys = []
        self.dma_rr = {}
        self.dma_last = {}

    @staticmethod
    def _box(a):
        t = a.tensor
        name = t.name
        shape = list(t.shape)
        isdram = type(t).__name__.startswith('DRam')
        off = int(a.offset)
        if isdram:
            lo = off
            hi = off
            for st, cnt in a.ap:
                if cnt > 1:
                    if st >= 0:
                        hi += st * (cnt - 1)
                    else:
                        lo += st * (cnt - 1)
            return name, (0, 0, lo, hi)
        row = 1
        for s in shape[1:]:
            row *= s
        p0 = off // row
        f0 = off % row
        pe = 0
        fe = 0
        for st, cnt in a.ap:
            if cnt <= 1 or st == 0:
                continue
            if st % row == 0:
                pe += (st // row) * (cnt - 1)
            else:
                fe += st * (cnt - 1)
        p1 = p0 + pe
        f1 = f0 + fe
        if type(t).__name__.startswith('PSum'):
            eb = 1024 if 'bfloat16' in str(t.dtype) else 512
            f0 = (f0 // eb) * eb
            f1 = (f1 // eb) * eb + eb - 1
            p0 = (p0 // 32) * 32
            p1 = (p1 // 32) * 32 + 31
        return name, (p0, p1, f0, f1)

    @staticmethod
    def _ovl(a, b):
        return not (a[1] < b[0] or b[1] < a[0] or a[3] < b[2] or b[3] < a[2])

    @staticmethod
    def _covers(a, b):
        return a[0] <= b[0] and a[1] >= b[1] and a[2] <= b[2] and a[3] >= b[3]

    def add(self, eng, fn, reads=(), writes=(), dma=None):
        op = Op()
        op.eng = eng
        op.fn = fn
        op.idx = len(self.ops)
        op.deps = set()
        op.dma_sem = dma
        op.dma_ord = None
        op.signal = False
        for a in reads:
            name, box = self._box(a)
            lst = self.acc.setdefault(name, [])
            for rec in lst:
                if rec[2] and self._ovl(rec[0], box):
                    op.deps.add(rec[1])
            lst.append([box, op.idx, False])
        for a in writes:
            name, box = self._box(a)
            lst = self.acc.setdefault(name, [])
            keep = []
            for rec in lst:
                if rec[1] == op.idx:
                    keep.append(rec)
                    continue
                if self._ovl(rec[0], box):
                    op.deps.add(rec[1])
                    if self._covers(box, rec[0]):
                        continue
                keep.append(rec)
            keep.append([box, op.idx, True])
            self.acc[name] = keep
        op.deps.discard(op.idx)
        if eng == 'pe' and dma is None:
            op.deps = set(d for d in op.deps if not (self.ops[d].eng == 'pe' and self.ops[d].dma_sem is None))
        if dma is not None:
            npool = self.DMA_POOL.get(eng, 4)
            self.dma_rr[eng] = self.dma_rr.get(eng, 0) + 1
            dma = '%s%d' % (eng, self.dma_rr[eng] % npool)
            op.dma_sem = dma
            if dma not in self.dma_count:
                self.dma_count[dma] = 0
                self.dma_keys.append(dma)
            else:
                op.deps.add(self.dma_last[dma])
            self.dma_count[dma] += 1
            op.dma_ord = self.dma_count[dma]
            self.dma_last[dma] = op.idx
        op.pos = len(self.eng_ops[eng]) + 1
        self.eng_ops[eng].append(op)
        self.ops.append(op)
        return op

    def finalize(self):
        ops = self.ops
        K = {e: {} for e in ENGS}
        dma_seen = {k: 0 for k in self.dma_keys}
        for op in ops:
            e = op.eng
            need = {}
            for d in op.deps:
                dop = ops[d]
                if dop.dma_sem is not None:
                    ch = 'dma:' + dop.dma_sem
                    cnt = dop.dma_ord
                else:
                    ch = dop.eng
                    cnt = dop.pos
                if ch not in need or need[ch][0] < cnt:
                    need[ch] = (cnt, dop)
            waits = []
            Ke = K[e]
            for ch, (cnt, dop) in sorted(need.items(), key=lambda kv: -kv[1][1].idx):
                if Ke.get(ch, 0) >= cnt:
                    continue
                waits.append((ch, cnt))
                Ke[ch] = cnt
                for c2, v2 in dop.vc.items():
                    if Ke.get(c2, 0) < v2:
                        Ke[c2] = v2
            op.waits = waits
            vc = dict(Ke)
            if op.dma_sem is not None:
                dma_seen[op.dma_sem] += 1
                vc['dma:' + op.dma_sem] = op.dma_ord
            else:
                vc[e] = op.pos
            op.vc = vc
        for op in ops:
            for ch, cnt in op.waits:
                if not ch.startswith('dma:'):
                    self.eng_ops[ch][cnt - 1].signal = True
        for e in ENGS:
            last = [o for o in self.eng_ops[e] if o.dma_sem is None]
            if last:
                last[-1].signal = True
        self.final_counts = {}
        for e in ENGS:
            c = 0
            for o in self.eng_ops[e]:
                if o.dma_sem is None and o.signal:
                    c += 1
                o.semval = c
            self.final_counts[e] = c

    def emit(self, stack):
        nc = self.nc
        self.finalize()
        sems = {}
        for e in ENGS:
            sems[e] = stack.enter_context(nc.semaphore('s_' + e))
        for k in self.dma_keys:
            sems['dma:' + k] = stack.enter_context(nc.semaphore('d_' + k))
        block = stack.enter_context(nc.Block())
        sched = self

        def run(engname, eng):
            for op in sched.eng_ops[engname]:
                for ch, cnt in op.waits:
                    if ch.startswith('dma:'):
                        eng.wait_ge(sems[ch], 16 * cnt)
                    else:
                        eng.wait_ge(sems[ch], sched.eng_ops[ch][cnt - 1].semval)
                ins = op.fn(eng)
                if op.dma_sem is not None:
                    ins.then_inc(sems['dma:' + op.dma_sem], 16)
                elif op.signal:
                    ins.then_inc(sems[engname], 1)
            if engname == 'sp':
                for k in sched.dma_keys:
                    eng.wait_ge(sems['dma:' + k], 16 * sched.dma_count[k])
                for e2 in ENGS:
                    if e2 != 'sp' and sched.final_counts[e2] > 0:
                        eng.wait_ge(sems[e2], sched.final_counts[e2])

        @block.sync
        def _(eng):
            run('sp', eng)

        @block.scalar
        def _(eng):
            run('act', eng)

        @block.vector
        def _(eng):
            run('dve', eng)

        @block.tensor
        def _(eng):
            run('pe', eng)

        @block.gpsimd
        def _(eng):
            run('pool', eng)


D = 1024
DIN = 10384
NMETA = 16
EPS = 1e-6
GN_EPS = 64 * 1e-5
OFF_A = 3072
NEG = -30000.0
NCMAX = 272
WCOLS = 10368
C_Q, C_K, C_V, C_Z, C_WDAD, C_RW, C_ZB, C_GA, C_GB = 0, 1024, 2048, 3072, 4096, 4224, 7296, 8320, 9344


class Cfg:
    def __init__(self, npb=8, ns=16):
        self.npb = npb
        self.ns = ns
        self.seq = 256 * npb


def build(cfg):
    nc = bass.Bass("TRN2", target_bir_lowering=False)
    st = ExitStack()
    S = Sched(nc)
    NS = cfg.ns
    SEQ = cfg.seq

    def din(name, shape):
        return nc.dram_tensor(name, shape, F32, kind="ExternalInput").ap()

    def dout(name, shape):
        return nc.dram_tensor(name, shape, F32, kind="ExternalOutput").ap()

    xp = din("xp", [SEQ, D])
    xs = din("xs", [NS * 4, D])
    sg_in = din("sg", [NS, 8, 128, 128])
    sc_in = din("sc", [NS * 3, 3072])
    sr_in = din("sr", [NS, 16, 64, 64])
    ss_in = din("ss", [NS, D])
    meta = din("meta", [NMETA, D])
    ln1_w = din("ln1_w", [D])
    w_in = din("w_in", [D, DIN])
    conv_w = din("conv_w", [4, 3072])
    a_log = din("a_log", [8, 1])
    dt_bias = din("dt_bias", [8, 1])
    gnorm_w = din("gnorm_w", [128, 1])
    w_out_a = din("w_out_a", [D, D])
    mu_in = din("mu", [33, 128])
    w0_in = din("w0", [8, 128])
    w2_in = din("w2", [64, D])
    a0_in = din("a0", [8, 128])
    a2_in = din("a2", [64, D])
    kk_in = din("k_k", [8, 128])
    ka_in = din("k_a", [8, 128])
    rk_in = din("r_k", [8, 128])
    gnw_in = din("gn_w", [8, 128])
    gnb_in = din("gn_b", [8, 128])
    w_out_b = din("w_out_b", [D, D])
    w_out = din("w_out", [D, D])
    lnf_w = din("lnf_w", [D])

    yp = dout("yp", [SEQ, D])
    ys = dout("ys", [NS * 4, D])
    ngp = dout("ngp", [8, 128, 128])
    ncp = dout("ncp", [3, 3072])
    nrp = dout("nrp", [16, 64, 64])
    nsp = dout("nsp", [1, D])
    ngs = dout("ngs", [NS, 8, 128, 128])
    ncs = dout("ncs", [NS * 3, 3072])
    nrs = dout("nrs", [NS, 16, 64, 64])
    nss = dout("nss", [NS, D])

    WG = {}
    for _nm, _w in (('q', 1024), ('k', 1024), ('v', 1024), ('z', 1024), ('wdad', 128), ('rw', 3072),
                    ('zb', 1024), ('ga', 1024), ('gb', 1024)):
        WG[_nm] = nc.dram_tensor("Wb_" + _nm, [128, 8, _w], BF16, kind="Internal").ap()
    Wab = nc.dram_tensor("Wab", [128, 8, 16], BF16, kind="Internal").ap()
    Wo = [nc.dram_tensor("Wo%d" % _i, [128, 8, D], BF16, kind="Internal").ap() for _i in range(3)]

    def sb(name, shape, dt=F32):
        return st.enter_context(nc.sbuf_tensor("s_" + name, shape, dt))

    def psum(name, shape, dt=F32):
        return st.enter_context(nc.psum_tensor("p_" + name, shape, dt))

    def rw(aps):
        return [a for a in aps if a is not None and not isinstance(a, (int, float))]

    def mm(out, lhsT, rhs, start=True, stop=True):
        S.add('pe', lambda e: e.matmul(out, lhsT=lhsT, rhs=rhs, start=start, stop=stop),
              reads=[lhsT, rhs], writes=[out])

    def tr(out, in_, idn):
        S.add('pe', lambda e: e.transpose(out=out, in_=in_, identity=idn), reads=[in_, idn], writes=[out])

    def act(out, in_, func, bias=None, scale=None, accum=None):
        kw = {}
        if bias is not None:
            kw['bias'] = bias
        if scale is not None:
            kw['scale'] = scale
        if accum is not None:
            kw['accum_out'] = accum
        S.add('act', lambda e: e.activation(out=out, in_=in_, func=func, **kw),
              reads=rw([in_, bias, scale]), writes=rw([out, accum]))

    def tt(eng, out, in0, in1, op):
        S.add(eng, lambda e: e.tensor_tensor(out=out, in0=in0, in1=in1, op=op), reads=[in0, in1], writes=[out])

    def ts(eng, out, in0, s1, s2=None, op0=ALU.mult, op1=None):
        kw = {}
        if op1 is not None:
            kw['op1'] = op1
        S.add(eng, lambda e: e.tensor_scalar(out=out, in0=in0, scalar1=s1, scalar2=s2, op0=op0, **kw),
              reads=rw([in0, s1, s2]), writes=[out])

    def stt(eng, out, in0, scalar, in1, op0, op1):
        S.add(eng, lambda e: e.scalar_tensor_tensor(out=out, in0=in0, scalar=scalar, in1=in1, op0=op0, op1=op1),
              reads=rw([in0, scalar, in1]), writes=[out])

    def cp(eng, out, in_):
        if eng == 'act':
            S.add('act', lambda e: e.copy(out=out, in_=in_), reads=[in_], writes=[out])
        else:
            S.add(eng, lambda e: e.tensor_copy(out=out, in_=in_), reads=[in_], writes=[out])

    def memset(eng, out, val):
        S.add(eng, lambda e: e.memset(out, val), writes=[out])

    def dma(eng, out, in_, key):
        S.add(eng, lambda e: e.dma_start(out=out, in_=in_), reads=[in_], writes=[out], dma=key)

    def recip(out, in_):
        S.add('dve', lambda e: e.reciprocal(out=out, in_=in_), reads=[in_], writes=[out])

    def rsum(out, in_):
        S.add('dve', lambda e: e.reduce_sum(out=out, in_=in_, axis=AX.X), reads=[in_], writes=[out])

    def scan(out, d0, d1):
        S.add('dve', lambda e: e.tensor_tensor_scan(out=out, data0=d0, data1=d1, initial=0.0,
                                                    op0=ALU.mult, op1=ALU.add), reads=[d0, d1], writes=[out])

    def asel(out, in_, pattern, cmp, fill, base, cm):
        S.add('pool', lambda e: e.affine_select(out=out, in_=in_, pattern=pattern, compare_op=cmp, fill=fill,
                                                base=base, channel_multiplier=cm), reads=[in_], writes=[out])

    def sigmoid_to(out, in_, tmp, bias=None, scale=1.0):
        if bias is not None:
            act(tmp, in_, AF.Exp, bias=bias, scale=-scale)
        else:
            act(tmp, in_, AF.Exp, scale=-scale)
        act(tmp, tmp, AF.Ln, bias=1.0)
        act(out, tmp, AF.Exp, scale=-1.0)

    def run_pipelined(gens, depth):
        active = []
        it = iter(gens)
        done = False
        while True:
            while len(active) < depth and not done:
                try:
                    active.append(next(it))
                except StopIteration:
                    done = True
            if not active:
                break
            for g in list(active):
                try:
                    next(g)
                except StopIteration:
                    active.remove(g)

    def rsqrt_to(out, in_, eps, mult=1.0):
        act(out, in_, AF.Ln, bias=eps, scale=mult)
        act(out, out, AF.Exp, scale=-0.5)

    PP = psum("PP", [128, 2, 512])
    PB = [psum("PB%d" % i, [128, 1024], BF16) for i in range(2)]
    PG = psum("PG", [128, 4, 512])
    ctr = {'pp': 0, 'pb': 0, 'pg': 0}

    def ppbank():
        i = ctr['pp'] % 2
        ctr['pp'] += 1
        return PP[:, i, :]

    def pbbank():
        i = ctr['pb'] % 2
        ctr['pb'] += 1
        return PB[i]

    def pgbank():
        i = ctr['pg'] % 4
        ctr['pg'] += 1
        return PG[:, i, :]

    ident = sb("ident", [128, 128])
    identb = sb("identb", [128, 128], BF16)
    ones = sb("ones", [128, 128])
    bones = sb("bones", [128, 128])
    nmS = sb("nmS", [64, 64])
    nmTI = sb("nmTI", [64, 64])
    m_st = sb("m_st", [64, 64])
    m_stT = sb("m_stT", [64, 64])
    m_inT = sb("m_inT", [64, 64])
    Esel = sb("Esel", [8, 8, 64])
    memset('pool', ident[:], 0.0)
    asel(ident[:], ident[:], [[-1, 128]], ALU.not_equal, 1.0, 0, 1)
    cp('pool', identb[:], ident[:])
    memset('pool', ones[:], 1.0)
    memset('pool', bones[:], 0.0)
    memset('pool', bones[0:64, 0:64], 1.0)
    memset('pool', bones[64:128, 64:128], 1.0)
    memset('pool', nmS[:], 0.0)
    asel(nmS[:], nmS[:], [[-1, 64]], ALU.is_gt, NEG, 0, 1)
    memset('pool', nmTI[:], 0.0)
    asel(nmTI[:], nmTI[:], [[1, 64]], ALU.is_ge, NEG, 0, -1)
    memset('pool', m_st[:], 1.0)
    asel(m_st[:], m_st[:], [[-1, 64]], ALU.is_gt, 0.0, 0, 1)
    memset('pool', m_stT[:], 1.0)
    asel(m_stT[:], m_stT[:], [[1, 64]], ALU.is_gt, 0.0, 0, -1)
    memset('pool', m_inT[:], 1.0)
    asel(m_inT[:], m_inT[:], [[1, 64]], ALU.is_ge, 0.0, 0, -1)
    memset('pool', Esel[:], 0.0)
    asel(Esel[:], Esel[:], [[-1, 8], [0, 64]], ALU.not_equal, 1.0, 0, 1)

    hrow = sb("hrow", [128, D])
    xrow = sb("xrow", [128, D])
    rstat = sb("rstat", [128, 4])
    memset('pool', hrow[:], 0.0)

    pstage = xrow[0:33, 0:128]

    def load_cols(name, src, ntile):
        t = sb(name, [128, ntile])
        dma('sp', pstage[0:ntile, :], src, 'c')
        pt = pgbank()
        tr(pt[:, 0:ntile], pstage[0:ntile, :], ident[0:ntile, 0:ntile])
        cp('act', t[:], pt[:, 0:ntile])
        return t

    mu = load_cols("mu", mu_in, 33)
    w0c = load_cols("w0c", w0_in, 8)
    a0c = load_cols("a0c", a0_in, 8)
    kkc = load_cols("kkc", kk_in, 8)
    kac = load_cols("kac", ka_in, 8)
    rkc = load_cols("rkc", rk_in, 8)
    gnwc = load_cols("gnwc", gnw_in, 8)
    gnbc = load_cols("gnbc", gnb_in, 8)
    nw0c = sb("nw0c", [128, 8])
    na0c = sb("na0c", [128, 8])
    ts('dve', nw0c[:], w0c[:], -1.0)
    ts('dve', na0c[:], a0c[:], -1.0)
    omka = sb("omka", [128, 8])
    ts('dve', omka[:], kac[:], -1.0, 1.0, op0=ALU.mult, op1=ALU.add)
    convw = sb("convw", [128, 24, 4])
    for q in range(3):
        dma('sp', xrow[0:4, :], conv_w[:, q * 1024:(q + 1) * 1024], 'c')
        for j in range(8):
            ft = q * 8 + j
            pt = pgbank()
            tr(pt[:, 0:4], xrow[0:4, j * 128:(j + 1) * 128], ident[0:4, 0:4])
            cp('act', convw[:, ft, :], pt[:, 0:4])
    ln1bc = sb("ln1bc", [128, D])
    lnfbc = sb("lnfbc", [128, D])
    dma('sp', ln1bc[:], ln1_w.partition_broadcast(128), 'c')
    dma('sp', lnfbc[:], lnf_w.partition_broadcast(128), 'c')
    alog = sb("alog", [8, 1])
    dtb = sb("dtb", [8, 1])
    nA = sb("nA", [8, 1])
    dma('sp', alog[:], a_log, 'c')
    dma('sp', dtb[:], dt_bias, 'c')
    act(nA[:], alog[:], AF.Exp)
    ts('dve', nA[:], nA[:], -1.0)
    gnw = sb("gnw", [128, 1])
    dma('sp', gnw[:], gnorm_w, 'c')
    wa2 = sb("wa2", [128, D])
    dma('sp', wa2[0:64, :], w2_in, 'c')
    dma('sp', wa2[64:128, :], a2_in, 'c')

    import os as _os2
    _KS = float(_os2.environ.get('KSTOP', '999'))
    w_v = w_in.rearrange("(dh dl) c -> dl dh c", dl=128)
    if _KS >= 1:
        for dh in range(8):
            dma('pool', Wab[:, dh, :], w_v[:, dh, OFF_A:OFF_A + 16], 'wcast')
        for nm, c0 in (('q', 0), ('k', 1024), ('v', 2048), ('z', 3088), ('wdad', 7184)):
            wd_ = WG[nm].shape[2]
            for dh in range(8):
                dma('pool', WG[nm][:, dh, :], w_v[:, dh, c0:c0 + wd_], 'wcast')
        for dh in range(8):
            for j in range(3):
                dma('pool', WG['rw'][:, dh, :].rearrange("d (p j f) -> d p j f", p=8, j=3)[:, :, j, :],
                    w_v[:, dh, 4112 + j * 1024:4112 + (j + 1) * 1024].rearrange("d (p f) -> d p f", p=8), 'wcast')
        for nm, c0 in (('zb', 7312), ('ga', 8336)):
            for dh in range(8):
                dma('pool', WG[nm][:, dh, :], w_v[:, dh, c0:c0 + 1024], 'wcast')
        for dh in range(8):
            dma('pool', Wo[0][:, dh, :], w_out_a.rearrange("(dh dl) c -> dl dh c", dl=128)[:, dh, :], 'wcast')
        for dh in range(8):
            dma('pool', WG['gb'][:, dh, :], w_v[:, dh, 9360:9360 + 1024], 'wcast')
        for i, wsrc in ((1, w_out_b), (2, w_out)):
            wv = wsrc.rearrange("(dh dl) c -> dl dh c", dl=128)
            for dh in range(8):
                dma('pool', Wo[i][:, dh, :], wv[:, dh, :], 'wcast')
    wab = sb("wab", [128, 8, 16], BF16)
    dma('sp', wab[:], Wab, 'c')

    NWB = 2
    wbuf = [sb("wbuf%d" % i, [128, 8, 512], BF16) for i in range(NWB)]
    wctr = [0]

    def wload(src):
        b = wbuf[wctr[0] % NWB]
        key = 'w%d' % (wctr[0] % NWB)
        wctr[0] += 1
        dma('sp', b[:, :, 0:src.shape[2]], src, key)
        return b

    hT = sb("hT", [128, 8, NCMAX], BF16)
    BT = [sb("BT%d" % i, [128, 8, NCMAX], BF16) for i in range(10)]
    oaT = sb("oaT", [128, 8, NCMAX], BF16)
    obT = BT[0]
    bg2 = sb("bg2", [128, 16, NCMAX], BF16)
    rg2 = sb("rg2", [128, 16, NCMAX], BF16)
    memset('pool', bg2[:], 0.0)
    memset('pool', rg2[:], 0.0)
    mgT = BT[9]
    MG = BT[7]

    Sg = sb("Sg", [128, 8, 128])
    Sgb = sb("Sgb", [128, 8, 128], BF16)
    Mr = sb("Mr", [128, 8, 64])
    Mrb = sb("Mrb", [128, 8, 64], BF16)
    NSQ = max(NS, 1)
    hist = sb("hist", [128, 24, 3 * NSQ])
    pbh = sb("pbh", [128, 33])
    scanm = sb("scanm", [128, NCMAX])

    cin = sb("cin", [128, NCMAX + 4])
    T = [sb("T%d" % i, [128, NCMAX + 4]) for i in range(15)]
    TX = [sb("TX%d" % i, [128, NCMAX + 4]) for i in range(14)]
    g8 = {n: sb("g8" + n, [8, NCMAX]) for n in ('g', 'gc', 'ngc', 'egc', 'beta', 'beg', 'etail')}
    NCHMAX = max(5, NS)
    eglD = sb("eglD", [8, NCHMAX, 8])
    eglbc = sb("eglbc", [128, NCHMAX, 8])
    eglR = sb("eglR", [128, 8, NCHMAX])

    def ctile(name, shape, dt=F32):
        return sb(name, [64] + shape, dt)
    sc24 = ctile("sc24", [24])
    kbg = ctile("kbg", [8, 128], BF16)
    ktl = ctile("ktl", [8, 128], BF16)
    vbt = ctile("vbt", [8, 128], BF16)
    Vtm, atl, ktlR = kbg, ktl, vbt
    tmpA = ctile("tmpA", [8, 64])
    DmS = ctile("DmS", [8, 64])
    DmT = ctile("DmT", [8, 64])
    RHS = ctile("RHS", [8, 64], BF16)
    Nm = [ctile("Nm%d" % i, [8, 64], BF16) for i in range(2)]
    NTm = [ctile("NTm%d" % i, [8, 64], BF16) for i in range(2)]
    RT = [ctile("RT%d" % i, [8, 64], BF16) for i in range(2)]
    AqkT = ctile("AqkT", [8, 64], BF16)
    AakT = ctile("AakT", [8, 64], BF16)
    AqbT = ctile("AqbT", [8, 64], BF16)
    nwT = sb("nwT", [128, 8, 64], BF16)
    vnew = ctile("vnew", [8, 128], BF16)
    osq = ctile("osq", [4, 128])
    ysq = osq
    ost = ctile("ost", [8, 4])
    yst = ost
    onb = ctile("onb", [8, 128], BF16)
    ynb = onb
    Urw = ctile("Urw", [8, 64], BF16)
    strw = sb("strw", [64, 16, 64])
    cvt = strw[:].rearrange("p a b -> p (a b)")[:, 0:768]

    class Blk:
        pass

    blocks = []
    for b in range(cfg.npb):
        B = Blk()
        B.kind = 'p'
        B.idx = b
        B.ntok = 256 + (NMETA if b == 0 else 0)
        B.ncols = B.ntok
        B.nseg = 1
        B.L = B.ntok
        B.first = (b == 0)
        B.last = (b == cfg.npb - 1)
        chunks = []
        c = 0
        if b == 0:
            chunks.append((0, NMETA))
            c = NMETA
        while c < B.ntok:
            chunks.append((c, 64))
            c += 64
        B.segs = [dict(sid=0, chunks=chunks)]
        B.chunks = chunks
        tiles = []
        r = 0
        while r < B.ntok:
            n = min(128, B.ntok - r)
            tiles.append((r, n))
            r += n
        B.tiles = tiles
        blocks.append(B)
    if NS > 0:
        B = Blk()
        B.kind = 's'
        B.idx = 0
        B.ntok = 4 * NS
        B.ncols = 4 * NS + NS
        B.nseg = NS
        B.L = 4
        B.first = True
        B.last = True
        B.segs = [dict(sid=s, chunks=[(4 * s, 4)]) for s in range(NS)]
        B.chunks = [(4 * s, 4) for s in range(NS)]
        B.tiles = [(0, B.ntok)]
        blocks.append(B)

    def load_x_tile(B, r0, n, dst):
        if B.kind == 'p':
            if B.idx == 0 and r0 == 0:
                dma('sp', dst[0:NMETA, :], meta, 'x')
                dma('sp', dst[NMETA:n, :], xp[0:n - NMETA, :], 'x')
            else:
                s0 = r0 + 256 * B.idx - (NMETA if B.idx == 0 else 0)
                dma('sp', dst[0:n, :], xp[s0:s0 + n, :], 'x')
        else:
            dma('sp', dst[0:n, :], xs[0:n, :], 'x')

    def rms_rows(src, n, dst, wbc):
        act(dst[0:n, :], src[0:n, :], AF.Square, accum=rstat[0:n, 0:1])
        rsqrt_to(rstat[0:n, 3:4], rstat[0:n, 0:1], EPS, 1.0 / D)
        stt('dve', dst[0:n, :], src[0:n, :], rstat[0:n, 3:4], wbc[0:n, :], ALU.mult, ALU.mult)

    def front(B):
        for ti, (r0, n) in enumerate(B.tiles):
            load_x_tile(B, r0, n, xrow)
            rms_rows(xrow, n, hrow, ln1bc)
            if B.kind == 's':
                dma('sp', hrow[64:64 + NS, :], ss_in, 'x')
                for s in range(NS):
                    dma('sp', nss[s:s + 1, :], hrow[4 * s + 3:4 * s + 4, :], 'out')
            elif B.last and ti == len(B.tiles) - 1:
                dma('sp', nsp, hrow[n - 1:n, :], 'out')
            for dc in range(8):
                pt = ppbank()
                if B.kind == 's':
                    tr(pt[:, 0:64 + NS], hrow[0:64 + NS, dc * 128:(dc + 1) * 128], ident[0:64 + NS, 0:64 + NS])
                    cp('act', hT[:, dc, 0:B.ntok], pt[:, 0:B.ntok])
                    cp('act', hT[:, dc, B.ntok:B.ntok + NS], pt[:, 64:64 + NS])
                else:
                    tr(pt[:, 0:n], hrow[0:n, dc * 128:(dc + 1) * 128], ident[0:n, 0:n])
                    cp('act' if dc % 2 == 0 else 'dve', hT[:, dc, r0:r0 + n], pt[:, 0:n])

    def proj(wt, ti, ncols, M=128, wcol0=None):
        ps = ppbank()
        c0 = ti * 128 if wcol0 is None else wcol0
        for dc in range(8):
            mm(ps[0:M, 0:ncols], wt[:, dc, c0:c0 + M], hT[:, dc, 0:ncols], start=(dc == 0), stop=(dc == 7))
        return ps

    def tokview(B, ap2d):
        return ap2d.rearrange("p (s l) -> p s l", l=B.L)

    CHK = [None]

    def gdn_prep(B):
        nt = B.ntok
        qT, kT, qgT, vT, zsT = BT[0], BT[1], BT[2], BT[3], BT[4]
        memset('pool', scanm[:, 0:nt], 1.0)
        for (c0, C) in B.chunks:
            memset('pool', scanm[:, c0:c0 + 1], 0.0)
        psa = proj(wab, 0, nt, M=8, wcol0=0)
        act(T[0][0:8, 0:nt], psa[0:8, 0:nt], AF.Exp, bias=dtb[:])
        act(T[0][0:8, 0:nt], T[0][0:8, 0:nt], AF.Ln, bias=1.0)
        ts('dve', g8['g'][:, 0:nt], T[0][0:8, 0:nt], nA[:])
        psb = proj(wab, 0, nt, M=8, wcol0=8)
        sigmoid_to(g8['beta'][:, 0:nt], psb[0:8, 0:nt], T[0][0:8, 0:nt])
        scan(g8['gc'][:, 0:nt], scanm[0:8, 0:nt], g8['g'][:, 0:nt])
        ts('dve', g8['ngc'][:, 0:nt], g8['gc'][:, 0:nt], -1.0)
        act(g8['egc'][:, 0:nt], g8['gc'][:, 0:nt], AF.Exp)
        tt('dve', g8['beg'][:, 0:nt], g8['beta'][:, 0:nt], g8['egc'][:, 0:nt], ALU.mult)
        for ci, (c0, C) in enumerate(B.chunks):
            last = c0 + C - 1
            ts('dve', g8['etail'][:, c0:c0 + C], g8['ngc'][:, c0:c0 + C], g8['gc'][:, last:last + 1], op0=ALU.add)
            ts('dve', eglD[:, ci, :], ident[0:8, 0:8], g8['egc'][:, last:last + 1])
        act(g8['etail'][:, 0:nt], g8['etail'][:, 0:nt], AF.Exp)
        nch = len(B.chunks)
        pe_ = pgbank()
        mm(pe_[:, 0:nch * 8], ones[0:8, :], eglD[:, 0:nch, :].rearrange("p c h -> p (c h)"))
        cp('act', eglbc[:, 0:nch, :].rearrange("p c h -> p (c h)"), pe_[:, 0:nch * 8])
        wth = {}

        def qkv_tile(gi, gname, ti):
            ft = gi * 8 + ti
            if ti % 4 == 0:
                wth[(gi, ti // 4)] = wload(WG[gname][:, :, ti * 128:ti * 128 + 512])
            wt = wth[(gi, ti // 4)]
            ps = proj(wt, ti % 4, nt)
            if ft % 2 == 0:
                cinx, A0, A1, A2, A3 = cin, T[0], T[1], T[2], T[3]
            else:
                cinx, A0, A1, A2, A3 = T[4], T[5], T[6], T[7], T[8]
            cv = cinx[:, 0:B.nseg * (B.L + 3)].rearrange("p (s l) -> p s l", l=B.L + 3)
            cp('act', cv[:, :, 3:3 + B.L], tokview(B, ps[:, 0:nt]))
            hv = hist[:, ft, 0:3 * B.nseg].rearrange("p (s l) -> p s l", l=3)
            if B.first and B.kind == 'p':
                memset('pool', cv[:, :, 0:3], 0.0)
            else:
                cp('pool', cv[:, :, 0:3], hv)
            yield
            acc = tokview(B, A0[:, 0:nt])
            ts('dve', acc, cv[:, :, 0:B.L], convw[:, ft, 0:1])
            for i in range(1, 4):
                stt('dve', acc, cv[:, :, i:i + B.L], convw[:, ft, i:i + 1], acc, ALU.mult, ALU.add)
            cp('pool', hv, cv[:, :, B.L:B.L + 3])
            yield
            sigmoid_to(A1[:, 0:nt], A0[:, 0:nt], A1[:, 0:nt])
            yield
            if gname == 'v':
                tt('dve', vT[:, ti, 0:nt], A0[:, 0:nt], A1[:, 0:nt], ALU.mult)
                return
            tt('dve', A1[:, 0:nt], A0[:, 0:nt], A1[:, 0:nt], ALU.mult)
            act(A2[:, 0:nt], A1[:, 0:nt], AF.Square)
            yield
            pn = pgbank()
            mm(pn[:, 0:nt], ones[:], A2[:, 0:nt])
            act(A3[:, 0:nt], pn[:, 0:nt], AF.Ln, bias=1e-6)
            act(A3[:, 0:nt], A3[:, 0:nt], AF.Exp, scale=-0.5)
            yield
            if gname == 'k':
                tt('dve', kT[:, ti, 0:nt], A1[:, 0:nt], A3[:, 0:nt], ALU.mult)
            else:
                stt('dve', qT[:, ti, 0:nt], A1[:, 0:nt], 128.0 ** -0.5, A3[:, 0:nt], ALU.mult, ALU.mult)
                ts('dve', A3[0:8, 0:nt], g8['egc'][:, 0:nt], ident[0:8, ti:ti + 1])
                yield
                pq = pgbank()
                mm(pq[:, 0:nt], ones[0:8, :], A3[0:8, 0:nt])
                tt('dve', qgT[:, ti, 0:nt], qT[:, ti, 0:nt], pq[:, 0:nt], ALU.mult)

        run_pipelined((qkv_tile(gi, gname, ti) for gi, gname in enumerate(('q', 'k', 'v')) for ti in range(8)), 2)
        def z_tile(ti):
            if ti % 4 == 0:
                wth[('z', ti // 4)] = wload(WG['z'][:, :, ti * 128:ti * 128 + 512])
            ps = proj(wth[('z', ti // 4)], ti % 4, nt)
            zt = T[ti % 2]
            yield
            sigmoid_to(zt[:, 0:nt], ps[:, 0:nt], zt[:, 0:nt])
            yield
            tt('dve', zsT[:, ti, 0:nt], ps[:, 0:nt], zt[:, 0:nt], ALU.mult)

        run_pipelined((z_tile(ti) for ti in range(8)), 2)

    def tri_inverse(N0, NT0, C, nh=8):
        cur = 0
        Icb = ident[0:C, 0:C].unsqueeze(1).to_broadcast([C, nh, C])
        tt('dve', RT[0][0:C, 0:nh, 0:C], NT0, Icb, ALU.add)
        P, PT = N0, NT0
        nlev = 0
        while (1 << (nlev + 1)) < C:
            nlev += 1
        for m in range(1, nlev + 1):
            pP = pgbank()
            pPv = pP[0:C, 0:nh * C].rearrange("p (h c) -> p h c", c=C)
            for h in range(nh):
                mm(pPv[:, h, :], PT[:, h, :], P[:, h, :])
            newP = Nm[m % 2][0:C, 0:nh, 0:C]
            cp('act', newP, pPv)
            newPT = None
            if m < nlev:
                pT = pgbank()
                pTv = pT[0:C, 0:nh * C].rearrange("p (h c) -> p h c", c=C)
                for h in range(nh):
                    mm(pTv[:, h, :], P[:, h, :], PT[:, h, :])
                newPT = NTm[m % 2][0:C, 0:nh, 0:C]
                cp('act', newPT, pTv)
            pR = pgbank()
            pRv = pR[0:C, 0:nh * C].rearrange("p (h c) -> p h c", c=C)
            old = RT[cur][0:C, 0:nh, 0:C]
            for h in range(nh):
                mm(pRv[:, h, :], newP[:, h, :], old[:, h, :])
            cur ^= 1
            tt('dve', RT[cur][0:C, 0:nh, 0:C], pRv, old, ALU.add)
            P = newP
            PT = newPT
        return RT[cur][0:C, 0:nh, 0:C]

    def gdn_chunk(B, ci, c0, C):
        qT, kT, qgT, vT, zsT = BT[0], BT[1], BT[2], BT[3], BT[4]
        cs = slice(c0, c0 + C)
        p1 = pgbank()
        for i, nm in enumerate(('beta', 'beg', 'etail')):
            tr(p1[0:C, 8 * i:8 * i + 8], g8[nm][:, cs], ident[0:8, 0:8])
        cp('act', sc24[0:C, :], p1[0:C, 0:24])

        def scb(i, w):
            return sc24[0:C, 8 * i:8 * i + 8].unsqueeze(2).to_broadcast([C, 8, w])

        def hv3(ap2, w):
            return ap2.rearrange("p (h c) -> p h c", c=w)
        pk = pbbank()
        pkv = hv3(pk[0:C, :], 128)
        for h in range(8):
            tr(pkv[:, h, :], kT[:, h, cs], identb[:])
        tt('dve', kbg[0:C], pkv, scb(1, 128), ALU.mult)
        tt('dve', ktl[0:C], pkv, scb(2, 128), ALU.mult)
        pv = pbbank()
        pvv = hv3(pv[0:C, :], 128)
        for h in range(8):
            tr(pvv[:, h, :], vT[:, h, cs], identb[:])
        tt('dve', vbt[0:C], pvv, scb(0, 128), ALU.mult)
        pd = pgbank()
        pdv = hv3(pd[0:C, 0:8 * C], C)
        for h in range(8):
            mm(pdv[:, h, :], g8['gc'][:, cs], Esel[:, h, 0:C], start=True, stop=False)
            mm(pdv[:, h, :], Esel[:, h, 0:C], g8['ngc'][:, cs], start=False, stop=True)
        tt('dve', DmS[0:C, :, 0:C], pdv, nmS[0:C, 0:C].unsqueeze(1).to_broadcast([C, 8, C]), ALU.add)
        act(DmS[0:C, :, 0:C], DmS[0:C, :, 0:C], AF.Exp)
        pdT = pgbank()
        pdTv = hv3(pdT[0:C, 0:8 * C], C)
        for h in range(8):
            mm(pdTv[:, h, :], Esel[:, h, 0:C], g8['gc'][:, cs], start=True, stop=False)
            mm(pdTv[:, h, :], g8['ngc'][:, cs], Esel[:, h, 0:C], start=False, stop=True)
        tt('dve', DmT[0:C, :, 0:C], pdTv, nmTI[0:C, 0:C].unsqueeze(1).to_broadcast([C, 8, C]), ALU.add)
        act(DmT[0:C, :, 0:C], DmT[0:C, :, 0:C], AF.Exp)
        pg_ = pgbank()
        pgv = hv3(pg_[0:C, 0:8 * C], C)
        for h in range(8):
            mm(pgv[:, h, :], kT[:, h, cs], kT[:, h, cs])
        stt('dve', tmpA[0:C, :, 0:C], pgv, -1.0, DmS[0:C, :, 0:C], ALU.mult, ALU.mult)
        N0 = Nm[0][0:C, :, 0:C]
        tt('dve', N0, tmpA[0:C, :, 0:C], scb(0, C), ALU.mult)
        pq = pgbank()
        pqv = hv3(pq[0:C, 0:8 * C], C)
        for h in range(8):
            mm(pqv[:, h, :], kT[:, h, cs], qT[:, h, cs])
        tt('dve', AqkT[0:C, :, 0:C], pqv, DmT[0:C, :, 0:C], ALU.mult)
        pn = pbbank()
        pnv = hv3(pn[0:C, 0:8 * C], C)
        for h in range(8):
            tr(pnv[:, h, :], N0[:, h, :], identb[0:C, 0:C])
        NT0 = NTm[0][0:C, :, 0:C]
        cp('act', NT0, pnv)
        RTf = tri_inverse(N0, NT0, C)
        pw = pgbank()
        pwv = hv3(pw[:, 0:8 * C], C)
        for h in range(8):
            mm(pwv[:, h, :], kbg[0:C, h, :], RTf[:, h, :])
        ts('dve', nwT[:, :, 0:C], pwv, -1.0)
        for half in range(2):
            p2 = pgbank()
            p2v = hv3(p2[0:C, :], 128)
            for hh in range(4):
                h = half * 4 + hh
                mm(p2v[:, hh, :], RTf[:, h, :], vbt[0:C, h, :], start=True, stop=False)
                mm(p2v[:, hh, :], nwT[:, h, 0:C], Sgb[:, h, :], start=False, stop=True)
            cp('act', vnew[0:C, half * 4:half * 4 + 4, :], p2v)
        for half in range(2):
            p3 = pgbank()
            p3v = hv3(p3[0:C, :], 128)
            for hh in range(4):
                h = half * 4 + hh
                mm(p3v[:, hh, :], qgT[:, h, cs], Sgb[:, h, :], start=True, stop=False)
                mm(p3v[:, hh, :], AqkT[0:C, h, 0:C], vnew[0:C, h, :], start=False, stop=True)
            hs = slice(half * 4, half * 4 + 4)
            act(osq[0:C], p3v, AF.Square)
            rsum(ost[0:C, hs, 0:1], osq[0:C])
            rsqrt_to(ost[0:C, hs, 3:4], ost[0:C, hs, 0:1], EPS, 1.0 / 128)
            tt('dve', onb[0:C, hs, :], p3v, ost[0:C, hs, 3:4].to_broadcast([C, 4, 128]), ALU.mult)
        po = pbbank()
        pov = hv3(po[:, 0:8 * C], C)
        for h in range(8):
            tr(pov[:, h, :], onb[0:C, h, :], identb[0:C, 0:C])
        stt('dve', oaT[:, :, cs], pov, gnw[:, 0:1], zsT[:, :, cs], ALU.mult, ALU.mult)
        for half in range(2):
            p4 = pgbank()
            p4v = hv3(p4[:, :], 128)
            for hh in range(4):
                h = half * 4 + hh
                mm(p4v[:, hh, :], ktl[0:C, h, :], vnew[0:C, h, :])
            for hh in range(4):
                h = half * 4 + hh
                stt('dve', Sg[:, h, :], Sg[:, h, :], eglbc[:, ci, h:h + 1], p4v[:, hh, :], ALU.mult, ALU.add)
            cp('act', Sgb[:, half * 4:half * 4 + 4, :], Sg[:, half * 4:half * 4 + 4, :])

    def rwkv_prep(B):
        nt = B.ntok
        ncols = B.ncols
        bgT, agT, kgT, rgT, atT, ktT, rvT, bonT, zbT = BT[0], BT[1], BT[2], BT[3], BT[4], BT[5], BT[6], BT[7], BT[8]

        def mixed(ps, fidx, dst, rawbuf=cin):
            raw = rawbuf[:, 0:B.nseg * (B.L + 1)].rearrange("p (s l) -> p s l", l=B.L + 1)
            cp('act', raw[:, :, 1:1 + B.L], tokview(B, ps[:, 0:nt]))
            if B.kind == 'p':
                if B.first:
                    memset('pool', raw[:, :, 0:1], 0.0)
                else:
                    cp('pool', raw[:, 0, 0:1], pbh[:, fidx:fidx + 1])
                cp('pool', pbh[:, fidx:fidx + 1], raw[:, 0, B.L:B.L + 1])
            else:
                cp('act', raw[:, :, 0:1], ps[:, nt:nt + B.nseg].unsqueeze(2))
            dv = tokview(B, dst)
            tt('dve', dv, raw[:, :, 0:B.L], raw[:, :, 1:1 + B.L], ALU.subtract)
            stt('dve', dv, dv, mu[:, fidx:fidx + 1], raw[:, :, 1:1 + B.L], ALU.mult, ALU.add)

        wt = wload(WG['wdad'])
        ps = proj(wt, 0, ncols)
        mixed(ps, 24, T[0][:, 0:nt])
        sigmoid_to(T[1][0:64, 0:nt], T[0][0:64, 0:nt], T[1][0:64, 0:nt], scale=2.0)
        ts('dve', T[1][0:64, 0:nt], T[1][0:64, 0:nt], 2.0, -1.0, op0=ALU.mult, op1=ALU.add)
        def pair_gen(p):
            if p % 2 == 0:
                rawb = cin
                TT = {i: T[i] for i in range(2, 15)}
            else:
                rawb = TX[0]
                TT = {i: TX[i - 1] for i in range(2, 15)}
            Tr, Tk, Tv = TT[2], TT[3], TT[4]
            wt = wload(WG['rw'][:, :, p * 384:(p + 1) * 384])
            for j, dst in enumerate((Tr, Tk, Tv)):
                ps = proj(wt, j, ncols)
                mixed(ps, 8 * j + p, dst[:, 0:nt], rawb)
                yield
            r_p, k_p, v_p = Tr[:, 0:nt], Tk[:, 0:nt], Tv[:, 0:nt]
            wlog, gc, eg, eng, egp = (TT[i][:, 0:nt] for i in (5, 6, 7, 8, 9))
            a_, kk, rn, kb2, kg32 = (TT[i][:, 0:nt] for i in (10, 11, 12, 13, 14))
            psw = pgbank()
            mm(psw[:, 0:nt], wa2[0:64, p * 128:(p + 1) * 128], T[1][0:64, 0:nt])
            sigmoid_to(wlog, psw[:, 0:nt], wlog, bias=nw0c[:, p:p + 1])
            psa = pgbank()
            mm(psa[:, 0:nt], wa2[64:128, p * 128:(p + 1) * 128], T[0][64:128, 0:nt])
            sigmoid_to(a_, psa[:, 0:nt], a_, bias=na0c[:, p:p + 1])
            ts('pool', kk, k_p, kkc[:, p:p + 1])
            act(rn, kk, AF.Square)
            yield
            ts('pool', wlog, wlog, -float(np.exp(-0.5)))
            scan(gc, scanm[:, 0:nt], wlog)
            pss = pgbank()
            mm(pss[:, 0:nt], bones[:], rn)
            act(rn, pss[:, 0:nt], AF.Ln, bias=1e-6)
            act(rn, rn, AF.Exp, scale=-0.5)
            yield
            act(eg, gc, AF.Exp)
            act(eng, gc, AF.Exp, scale=-1.0)
            tt('pool', egp, gc, wlog, ALU.subtract)
            act(egp, egp, AF.Exp)
            tt('dve', kk, kk, rn, ALU.mult)
            ts('dve', kb2, a_, kac[:, p:p + 1], omka[:, p:p + 1], op0=ALU.mult, op1=ALU.add)
            tt('dve', kb2, kb2, k_p, ALU.mult)
            yield
            for ci, (c0, C) in enumerate(B.chunks):
                cp('pool', eglR[:, p, ci:ci + 1], eg[:, c0 + C - 1:c0 + C])
            for e_ in range(2):
                hsl = slice(64 * e_, 64 * e_ + 64)
                tt('dve', bg2[hsl, 2 * p + e_, 0:nt], kk[hsl], egp[hsl], ALU.mult)
            ka = wlog
            tt('pool', ka, kk, a_, ALU.mult)
            tt('dve', kg32, kb2, eng, ALU.mult)
            yield
            ag32 = gc
            stt('dve', ag32, ka, -1.0, eng, ALU.mult, ALU.mult)
            cp('pool', kgT[:, p, 0:nt], kg32)
            for e_ in range(2):
                hsl = slice(64 * e_, 64 * e_ + 64)
                tt('dve', rg2[hsl, 2 * p + e_, 0:nt], r_p[hsl], eg[hsl], ALU.mult)
            cp('act', rvT[:, p, 0:nt], v_p)
            yield
            cp('pool', agT[:, p, 0:nt], ag32)
            for ci, (c0, C) in enumerate(B.chunks):
                ts('pool', atT[:, p, c0:c0 + C], ag32[:, c0:c0 + C], eglR[:, p, ci:ci + 1])
                ts('pool', ktT[:, p, c0:c0 + C], kg32[:, c0:c0 + C], eglR[:, p, ci:ci + 1])
            prod = egp
            stt('dve', prod, r_p, rkc[:, p:p + 1], kb2, ALU.mult, ALU.mult)
            yield
            psb_ = pgbank()
            mm(psb_[:, 0:nt], bones[:], prod)
            tt('dve', bonT[:, p, 0:nt], psb_[:, 0:nt], v_p, ALU.mult)

        run_pipelined((pair_gen(p) for p in range(8)), 2)
        wth = {}
        def zb_tile(ti):
            if ti % 4 == 0:
                wth[ti // 4] = wload(WG['zb'][:, :, ti * 128:ti * 128 + 512])
            ps = proj(wth[ti // 4], ti % 4, ncols)
            if ti % 2 == 0:
                zr, z1, z2 = cin, T[14], T[13]
            else:
                zr, z1, z2 = TX[0], TX[13], TX[12]
            mixed(ps, 25 + ti, z1[:, 0:nt], zr)
            yield
            sigmoid_to(z2[:, 0:nt], z1[:, 0:nt], z2[:, 0:nt])
            yield
            tt('dve', zbT[:, ti, 0:nt], z1[:, 0:nt], z2[:, 0:nt], ALU.mult)

        run_pipelined((zb_tile(ti) for ti in range(8)), 2)

    def rwkv_chunk(B, ci, c0, C):
        chk = CHK[0]
        bgT, agT, kgT, rgT, atT, ktT, rvT, bonT, zbT, ynT = BT
        cs = slice(c0, c0 + C)

        def hv3(ap2, w):
            return ap2.rearrange("p (h c) -> p h c", c=w)
        import os as _os3
        _kv = int(_os3.environ.get('KVAR', '3'))
        for k_, (src, dst) in enumerate(((rvT, Vtm), (atT, atl), (ktT, ktlR))[:_kv]):
            pk = pbbank()
            pkv = hv3(pk[0:C, :], 128)
            for p in range(8):
                tr(pkv[:, p, :], src[:, p, cs], identb[:])
            cp('act' if k_ == 0 else 'dve', dst[0:C], pkv)
        Vv = Vtm[0:C].rearrange("p a (e v) -> p (a e) v", e=2)
        chk(6.1)
        for half in range(2):
            heads = list(range(half * 8, half * 8 + 8))

            def hop(T_, h):
                if T_ is bg2 or T_ is rg2:
                    return T_[:, h, cs]
                return T_[:, h // 2, cs]

            import os as _os4
            _ksc = [int(_os4.environ.get('KSC', '99'))]

            def score(lt, rt, mask, dst):
                _ksc[0] -= 1
                if _ksc[0] < 0:
                    return
                ps_ = pgbank()
                psv = hv3(ps_[0:C, 0:8 * C], C)
                for hi, h in enumerate(heads):
                    mm(psv[:, hi, :], hop(lt, h), hop(rt, h))
                tt('dve', dst, psv, mask[0:C, 0:C].unsqueeze(1).to_broadcast([C, 8, C]), ALU.mult)
            N0 = Nm[0][0:C, :, 0:C]
            NT0 = NTm[0][0:C, :, 0:C]
            score(bg2, agT, m_st, N0)
            score(agT, bg2, m_stT, NT0)
            score(kgT, bg2, m_stT, AakT[0:C, :, 0:C])
            score(agT, rg2, m_inT, AqbT[0:C, :, 0:C])
            score(kgT, rg2, m_inT, AqkT[0:C, :, 0:C])
            chk(6.2)
            RTf = tri_inverse(N0, NT0, C)
            chk(6.3)
            p1 = pgbank()
            p1v = hv3(p1[0:C, 0:8 * 64], 64)
            for hi, h in enumerate(heads):
                p, e = h // 2, h % 2
                mm(p1v[:, hi, :], hop(bg2, h), Mrb[:, p, :], start=True, stop=False)
                mm(p1v[:, hi, :], AakT[0:C, hi, 0:C], Vv[:, h, :], start=False, stop=True)
            cp('act', RHS[0:C], p1v)
            p2 = pgbank()
            p2v = hv3(p2[0:C, 0:8 * 64], 64)
            for hi, h in enumerate(heads):
                mm(p2v[:, hi, :], RTf[:, hi, :], RHS[0:C, hi, :])
            cp('act', Urw[0:C], p2v)
            p3 = pgbank()
            p3v = hv3(p3[0:C, 0:8 * 64], 64)
            for hi, h in enumerate(heads):
                p, e = h // 2, h % 2
                mm(p3v[:, hi, :], hop(rg2, h), Mrb[:, p, :], start=True, stop=False)
                mm(p3v[:, hi, :], AqbT[0:C, hi, 0:C], Urw[0:C, hi, :], start=False, stop=False)
                mm(p3v[:, hi, :], AqkT[0:C, hi, 0:C], Vv[:, h, :], start=False, stop=True)
            chk(6.4)
            ysv = ysq[0:C].rearrange("p a (b v) -> p (a b) v", b=2)
            Ysb = tmpA[0:C]
            cp('act', Ysb, p3v)
            rsum(yst[0:C, :, 0:1], Ysb)
            chk(6.41)
            act(ysv, Ysb, AF.Square)
            rsum(yst[0:C, :, 1:2], ysv)
            chk(6.42)
            ts('dve', yst[0:C, :, 0:1], yst[0:C, :, 0:1], 1.0 / 64)
            tt('dve', yst[0:C, :, 2:3], yst[0:C, :, 0:1], yst[0:C, :, 0:1], ALU.mult)
            stt('dve', yst[0:C, :, 1:2], yst[0:C, :, 1:2], 1.0 / 64, yst[0:C, :, 2:3], ALU.mult, ALU.subtract)
            ts('dve', yst[0:C, :, 1:2], yst[0:C, :, 1:2], GN_EPS, op0=ALU.add)
            act(yst[0:C, :, 3:4], yst[0:C, :, 1:2], AF.Ln)
            act(yst[0:C, :, 3:4], yst[0:C, :, 3:4], AF.Exp, scale=-0.5)
            chk(6.43)
            tt('dve', ysv, Ysb, yst[0:C, :, 0:1].to_broadcast([C, 8, 64]), ALU.subtract)
            ynv8 = ynb[0:C, 0:4, :].rearrange("p a (b v) -> p (a b) v", b=2)
            tt('dve', ynv8, ysv, yst[0:C, :, 3:4].to_broadcast([C, 8, 64]), ALU.mult)
            chk(6.44)
            po = pbbank()
            pov = hv3(po[:, 0:4 * C], C)
            for a in range(4):
                tr(pov[:, a, :], ynb[0:C, a, :], identb[0:C, 0:C])
            cp('act', ynT[:, half * 4:half * 4 + 4, cs], pov)
            chk(6.5)
            pA = pgbank()
            pAv = hv3(pA[:, 0:4 * 64], 64)
            pBk = pgbank()
            pBv = hv3(pBk[:, 0:4 * 64], 64)
            for a in range(4):
                p = half * 4 + a
                for e, pv in ((0, pAv), (1, pBv)):
                    hi = 2 * a + e
                    h = 2 * p + e
                    mm(pv[:, a, :], atl[0:C, p, :], Urw[0:C, hi, :], start=True, stop=False)
                    mm(pv[:, a, :], ktlR[0:C, p, :], Vv[:, h, :], start=False, stop=True)
            ps_ = slice(half * 4, half * 4 + 4)
            for e, pv in ((0, pAv), (1, pBv)):
                rows = slice(64 * e, 64 * e + 64)
                tt('dve', Mr[rows, ps_, :], Mr[rows, ps_, :],
                   eglR[rows, ps_, ci:ci + 1].to_broadcast([64, 4, 64]), ALU.mult)
                tt('dve', Mr[rows, ps_, :], Mr[rows, ps_, :], pv[rows], ALU.add)
            cp('act', Mrb[:, ps_, :], Mr[:, ps_, :])

    def out_stage(B):
        nt = B.ntok
        bonT, zbT, ynT = BT[7], BT[8], BT[9]
        for p in range(8):
            ob_t = T[p % 2]
            ts('dve', ob_t[:, 0:nt], ynT[:, p, 0:nt], gnwc[:, p:p + 1], gnbc[:, p:p + 1], op0=ALU.mult, op1=ALU.add)
            tt('pool', ob_t[:, 0:nt], ob_t[:, 0:nt], bonT[:, p, 0:nt], ALU.add)
            tt('dve', obT[:, p, 0:nt], ob_t[:, 0:nt], zbT[:, p, 0:nt], ALU.mult)
        wgh = {}

        def gate_tile(bi, src, gcol, ti):
            if ti % 4 == 0:
                wgh[(bi, ti // 4)] = (wload(WG[gcol][:, :, ti * 128:ti * 128 + 512]),
                                      wload(Wo[bi][:, :, ti * 128:ti * 128 + 512]))
            wg, wo = wgh[(bi, ti // 4)]
            psg = proj(wg, ti % 4, nt)
            G1 = T[1] if ti % 2 == 0 else T[3]
            G2 = T[2] if ti % 2 == 0 else T[4]
            pbr = pgbank()
            for ec in range(8):
                mm(pbr[:, 0:nt], wo[:, ec, (ti % 4) * 128:(ti % 4 + 1) * 128], src[:, ec, 0:nt], start=(ec == 0), stop=(ec == 7))
            yield
            sigmoid_to(G1[:, 0:nt], psg[:, 0:nt], G1[:, 0:nt])
            yield
            if bi == 0:
                tt('dve', MG[:, ti, 0:nt], pbr[:, 0:nt], G1[:, 0:nt], ALU.mult)
            else:
                tt('dve', G2[:, 0:nt], pbr[:, 0:nt], G1[:, 0:nt], ALU.mult)
                tt('dve', mgT[:, ti, 0:nt], G2[:, 0:nt], MG[:, ti, 0:nt], ALU.add)

        for bi, (src, gcol) in enumerate(((oaT, 'ga'), (obT, 'gb'))):
            run_pipelined((gate_tile(bi, src, gcol, ti) for ti in range(8)), 2)
        wos = [wload(Wo[2][:, :, 0:512]), wload(Wo[2][:, :, 512:1024])]
        for ti, (r0, n) in enumerate(B.tiles):
            load_x_tile(B, r0, n, hrow)
            for half in range(2):
                px = ppbank()
                for dc in range(8):
                    mm(px[0:n, :], mgT[:, dc, r0:r0 + n], wos[half][:, dc, :], start=(dc == 0), stop=(dc == 7))
                tt('dve', xrow[0:n, half * 512:(half + 1) * 512], px[0:n, :], hrow[0:n, half * 512:(half + 1) * 512], ALU.add)
            rms_rows(xrow, n, hrow, lnfbc)
            if B.kind == 'p':
                if B.idx == 0 and r0 == 0:
                    dma('sp', yp[0:n - NMETA, :], hrow[NMETA:n, :], 'out')
                else:
                    s0 = r0 + 256 * B.idx - (NMETA if B.idx == 0 else 0)
                    dma('sp', yp[s0:s0 + n, :], hrow[0:n, :], 'out')
            else:
                dma('sp', ys[0:n, :], hrow[0:n, :], 'out')

    def load_gdn_state(s):
        dma('sp', Sg[:], sg_in[s].rearrange("h k v -> k h v"), 'st')
        cp('pool', Sgb[:], Sg[:])

    def store_gdn_state(dst):
        dma('sp', dst.rearrange("h k v -> k h v"), Sg[:], 'out')

    def load_rwkv_state(s):
        dma('sp', strw[:], sr_in[s].rearrange("h v k -> v h k"), 'st')
        for p in range(8):
            pt = pgbank()
            tr(pt[:, 0:64], strw[:, 2 * p:2 * p + 2, :].rearrange("p a b -> p (a b)"), ident[0:64, 0:64])
            cp('act', Mr[:, p, :], pt[:, 0:64])
        cp('pool', Mrb[:], Mr[:])

    def store_rwkv_state(dst):
        for p in range(8):
            pt = pgbank()
            tr(pt[0:64, 0:128], Mr[:, p, :], ident[:])
            cp('act', strw[:, 2 * p:2 * p + 2, :].rearrange("p a b -> p (a b)"), pt[0:64, 0:128])
        dma('sp', dst.rearrange("h v k -> v h k"), strw[:], 'out')

    def store_conv(B, dst):
        n3 = 3 * B.nseg
        for q in range(4):
            for j in range(6):
                ft = q * 6 + j
                pt = pgbank()
                tr(pt[0:n3, 0:128], hist[:, ft, 0:n3], ident[:])
                cp('act' if j % 2 == 0 else 'dve', cvt[0:n3, j * 128:(j + 1) * 128], pt[0:n3, 0:128])
            dma('sp', dst[:, q * 768:(q + 1) * 768], cvt[0:n3, :], 'out')

    def load_conv_hist(B):
        n3 = 3 * B.nseg
        for q in range(4):
            dma('sp', cvt[0:n3, :], sc_in[:, q * 768:(q + 1) * 768], 'st')
            for j in range(6):
                ft = q * 6 + j
                pt = pgbank()
                tr(pt[:, 0:n3], cvt[0:n3, j * 128:(j + 1) * 128], ident[0:n3, 0:n3])
                cp('act' if j % 2 == 0 else 'dve', hist[:, ft, 0:n3], pt[:, 0:n3])

    import os as _os
    KSTOP = float(_os.environ.get('KSTOP', '999'))

    class _Stop(Exception):
        pass

    def chk(n):
        if n > KSTOP:
            raise _Stop()

    CHK[0] = chk

    def main_prog():
      memset('pool', Sg[:], 0.0)
      memset('pool', Sgb[:], 0.0)
      memset('pool', Mr[:], 0.0)
      memset('pool', Mrb[:], 0.0)
      for B in blocks:
        chk(2)
        front(B)
        if B.kind == 's':
            load_conv_hist(B)
        chk(3)
        gdn_prep(B)
        ci = 0
        for seg in B.segs:
            if B.kind == 's':
                load_gdn_state(seg['sid'])
            for (c0, C) in seg['chunks']:
                chk(4)
                gdn_chunk(B, ci, c0, C)
                ci += 1
            if B.kind == 's':
                store_gdn_state(ngs[seg['sid']])
        if B.kind == 'p' and B.last:
            store_gdn_state(ngp)
        if B.last:
            store_conv(B, ncp if B.kind == 'p' else ncs)
        chk(5)
        rwkv_prep(B)
        ci = 0
        for seg in B.segs:
            if B.kind == 's':
                load_rwkv_state(seg['sid'])
            for (c0, C) in seg['chunks']:
                chk(6)
                rwkv_chunk(B, ci, c0, C)
                ci += 1
            if B.kind == 's':
                store_rwkv_state(nrs[seg['sid']])
        if B.kind == 'p' and B.last:
            store_rwkv_state(nrp)
        chk(7)
        out_stage(B)

    try:
        main_prog()
    except _Stop:
        pass
    if _os.environ.get('KDBG'):
        dbg = dout("dbg", [10, 128, 8, NCMAX])
        for i in range(9):
            for p in range(8):
                cp('dve', T[0][:, 0:NCMAX], BT[i][:, p, :])
                dma('sp', dbg[i, :, p, :], T[0][:, 0:NCMAX], 'out')

    S.emit(st)
    st.close()
    return nc


_NC_CACHE = {}


def make_in_maps(cfg, inputs, ncores):
    f = lambda a: np.ascontiguousarray(np.asarray(a, dtype=np.float32))
    ns = cfg.ns
    shared = {
        "meta": f(inputs["meta_tokens"]),
        "ln1_w": f(inputs["ln1_w"]).reshape(D),
        "w_in": f(inputs["w_in"]).reshape(D, DIN),
        "conv_w": f(inputs["gdn_conv_w"]).reshape(4, 3072),
        "a_log": f(inputs["gdn_a_log"]).reshape(8, 1),
        "dt_bias": f(inputs["gdn_dt_bias"]).reshape(8, 1),
        "gnorm_w": f(inputs["gdn_norm_w"]).reshape(128, 1),
        "w_out_a": f(inputs["w_out_a"]).reshape(D, D),
        "mu": f(inputs["rwkv_mu"]).reshape(33, 128),
        "w0": f(inputs["rwkv_w0"]).reshape(8, 128),
        "w2": f(inputs["rwkv_w2"]).reshape(64, D),
        "a0": f(inputs["rwkv_a0"]).reshape(8, 128),
        "a2": f(inputs["rwkv_a2"]).reshape(64, D),
        "k_k": f(inputs["rwkv_k_k"]).reshape(8, 128),
        "k_a": f(inputs["rwkv_k_a"]).reshape(8, 128),
        "r_k": f(inputs["rwkv_r_k"]).reshape(8, 128),
        "gn_w": f(inputs["rwkv_gn_w"]).reshape(8, 128),
        "gn_b": f(inputs["rwkv_gn_b"]).reshape(8, 128),
        "w_out_b": f(inputs["w_out_b"]).reshape(D, D),
        "w_out": f(inputs["w_out"]).reshape(D, D),
        "lnf_w": f(inputs["lnf_w"]).reshape(D),
    }
    maps = []
    for c in range(ncores):
        m = dict(shared)
        m["xp"] = f(inputs["x_prompt"][c])
        if ns > 0:
            sl = slice(c * ns, (c + 1) * ns)
            m["xs"] = f(inputs["x_sample"][sl]).reshape(ns * 4, D)
            m["sg"] = f(inputs["state_gdn"][0, sl])
            m["sc"] = f(inputs["state_gdn_conv"][0, sl]).reshape(ns * 3, 3072)
            m["sr"] = f(inputs["state_rwkv"][0, sl])
            m["ss"] = f(inputs["state_shift"][0, sl])
        maps.append(m)
    return maps


def gather(cfg, results, ncores):
    ns = cfg.ns
    cat = lambda k, shp: np.concatenate([np.asarray(r[k], dtype=np.float32).reshape(shp) for r in results], axis=0)
    y_prompt = cat("yp", (1, cfg.seq, D))
    ngp = cat("ngp", (1, 8, 128, 128))[None]
    ncp = cat("ncp", (1, 3, 3072))[None]
    nrp = cat("nrp", (1, 16, 64, 64))[None]
    nsp = cat("nsp", (1, D))[None]
    y_sample = cat("ys", (ns, 4, D))
    ngs = cat("ngs", (ns, 8, 128, 128))[None]
    ncs = cat("ncs", (ns, 3, 3072))[None]
    nrs = cat("nrs", (ns, 16, 64, 64))[None]
    nss = cat("nss", (ns, D))[None]
    return (y_prompt, y_sample, ngp, ncp, nrp, nsp, ngs, ncs, nrs, nss)


def kernel(**inputs):
    cfg = Cfg(8, 16)
    if 'nc' not in _NC_CACHE:
        _NC_CACHE['nc'] = build(cfg)
    nc = _NC_CACHE['nc']
    maps = make_in_maps(cfg, inputs, 8)
    res = run_bass_kernel_spmd(nc, maps, core_ids=list(range(8)))
    return gather(cfg, res.results, 8)
```

```python
import numpy as np
from contextlib import ExitStack
import concourse.bass as bass
import concourse.mybir as mybir
from concourse.bass_utils import run_bass_kernel_spmd

F32 = mybir.dt.float32
BF16 = mybir.dt.bfloat16
AF = mybir.ActivationFunctionType
ALU = mybir.AluOpType
AX = mybir.AxisListType

ENGS = ['sp', 'act', 'dve', 'pe', 'pool']


class Op:
    __slots__ = ('eng', 'fn', 'idx', 'pos', 'deps', 'dma_sem', 'dma_ord', 'waits', 'signal', 'semval', 'vc')


class Sched:
    DMA_POOL = {'sp': 32, 'pool': 8, 'act': 8}

    def __init__(self, nc):
        self.nc = nc
        self.ops = []
        self.eng_ops = {e: [] for e in ENGS}
        self.acc = {}
        self.dma_count = {}
        self.dma_keys = []
        self.dma_rr = {}
        self.dma_last = {}

    @staticmethod
    def _box(a):
        t = a.tensor
        name = t.name
        shape = list(t.shape)
        isdram = type(t).__name__.startswith('DRam')
        off = int(a.offset)
        if isdram:
            lo = off
            hi = off
            for st, cnt in a.ap:
                if cnt > 1:
                    if st >= 0:
                        hi += st * (cnt - 1)
                    else:
                        lo += st * (cnt - 1)
            return name, (0, 0, lo, hi)
        row = 1
        for s in shape[1:]:
            row *= s
        p0 = off // row
        f0 = off % row
        pe = 0
        fe = 0
        for st, cnt in a.ap:
            if cnt <= 1 or st == 0:
                continue
            if st % row == 0:
                pe += (st // row) * (cnt - 1)
            else:
                fe += st * (cnt - 1)
        p1 = p0 + pe
        f1 = f0 + fe
        if type(t).__name__.startswith('PSum'):
            eb = 1024 if 'bfloat16' in str(t.dtype) else 512
            f0 = (f0 // eb) * eb
            f1 = (f1 // eb) * eb + eb - 1
            p0 = (p0 // 32) * 32
            p1 = (p1 // 32) * 32 + 31
        return name, (p0, p1, f0, f1)

    @staticmethod
    def _ovl(a, b):
        return not (a[1] < b[0] or b[1] < a[0] or a[3] < b[2] or b[3] < a[2])

    @staticmethod
    def _covers(a, b):
        return a[0] <= b[0] and a[1] >= b[1] and a[2] <= b[2] and a[3] >= b[3]

    def add(self, eng, fn, reads=(), writes=(), dma=None):
        op = Op()
        op.eng = eng
        op.fn = fn
        op.idx = len(self.ops)
        op.deps = set()
        op.dma_sem = dma
        op.dma_ord = None
        op.signal = False
        for a in reads:
            name, box = self._box(a)
            lst = self.acc.setdefault(name, [])
            for rec in lst:
                if rec[2] and self._ovl(rec[0], box):
                    op.deps.add(rec[1])
            lst.append([box, op.idx, False])
        for a in writes:
            name, box = self._box(a)
            lst = self.acc.setdefault(name, [])
            keep = []
            for rec in lst:
                if rec[1] == op.idx:
                    keep.append(rec)
                    continue
                if self._ovl(rec[0], box):
                    op.deps.add(rec[1])
                    if self._covers(box, rec[0]):
                        continue
                keep.append(rec)
            keep.append([box, op.idx, True])
            self.acc[name] = keep
        op.deps.discard(op.idx)
        if eng == 'pe' and dma is None:
            op.deps = set(d for d in op.deps if not (self.ops[d].eng == 'pe' and self.ops[d].dma_sem is None))
        if dma is not None:
            npool = self.DMA_POOL.get(eng, 4)
            self.dma_rr[eng] = self.dma_rr.get(eng, 0) + 1
            dma = '%s%d' % (eng, self.dma_rr[eng] % npool)
            op.dma_sem = dma
            if dma not in self.dma_count:
                self.dma_count[dma] = 0
                self.dma_keys.append(dma)
            else:
                op.deps.add(self.dma_last[dma])
            self.dma_count[dma] += 1
            op.dma_ord = self.dma_count[dma]
            self.dma_last[dma] = op.idx
        op.pos = len(self.eng_ops[eng]) + 1
        self.eng_ops[eng].append(op)
        self.ops.append(op)
        return op

    def finalize(self):
        ops = self.ops
        K = {e: {} for e in ENGS}
        dma_seen = {k: 0 for k in self.dma_keys}
        for op in ops:
            e = op.eng
            need = {}
            for d in op.deps:
                dop = ops[d]
                if dop.dma_sem is not None:
                    ch = 'dma:' + dop.dma_sem
                    cnt = dop.dma_ord
                else:
                    ch = dop.eng
                    cnt = dop.pos
                if ch not in need or need[ch][0] < cnt:
                    need[ch] = (cnt, dop)
            waits = []
            Ke = K[e]
            for ch, (cnt, dop) in sorted(need.items(), key=lambda kv: -kv[1][1].idx):
                if Ke.get(ch, 0) >= cnt:
                    continue
                waits.append((ch, cnt))
                Ke[ch] = cnt
                for c2, v2 in dop.vc.items():
                    if Ke.get(c2, 0) < v2:
                        Ke[c2] = v2
            op.waits = waits
            vc = dict(Ke)
            if op.dma_sem is not None:
                dma_seen[op.dma_sem] += 1
                vc['dma:' + op.dma_sem] = op.dma_ord
            else:
                vc[e] = op.pos
            op.vc = vc
        for op in ops:
            for ch, cnt in op.waits:
                if not ch.startswith('dma:'):
                    self.eng_ops[ch][cnt - 1].signal = True
        for e in ENGS:
            last = [o for o in self.eng_ops[e] if o.dma_sem is None]
            if last:
                last[-1].signal = True
        self.final_counts = {}
        for e in ENGS:
            c = 0
            for o in self.eng_ops[e]:
                if o.dma_sem is None and o.signal:
                    c += 1
                o.semval = c
            self.final_counts[e] = c

    def emit(self, stack):
        nc = self.nc
        self.finalize()
        sems = {}
        for e in ENGS:
            sems[e] = stack.enter_context(nc.semaphore('s_' + e))
        for k in self.dma_keys:
            sems['dma:' + k] = stack.enter_context(nc.semaphore('d_' + k))
        block = stack.enter_context(nc.Block())
        sched = self

        def run(engname, eng):
            for op in sched.eng_ops[engname]:
                for ch, cnt in op.waits:
                    if ch.startswith('dma:'):
                        eng.wait_ge(sems[ch], 16 * cnt)
                    else:
                        eng.wait_ge(sems[ch], sched.eng_ops[ch][cnt - 1].semval)
                ins = op.fn(eng)
                if op.dma_sem is not None:
                    ins.then_inc(sems['dma:' + op.dma_sem], 16)
                elif op.signal:
                    ins.then_inc(sems[engname], 1)
            if engname == 'sp':
                for k in sched.dma_keys:
                    eng.wait_ge(sems['dma:' + k], 16 * sched.dma_count[k])
                for e2 in ENGS:
                    if e2 != 'sp' and sched.final_counts[e2] > 0:
                        eng.wait_ge(sems[e2], sched.final_counts[e2])

        @block.sync
        def _(eng):
            run('sp', eng)

        @block.scalar
        def _(eng):
            run('act', eng)

        @block.vector
        def _(eng):
            run('dve', eng)

        @block.tensor
        def _(eng):
            run('pe', eng)

        @block.gpsimd
        def _(eng):
            run('pool', eng)


D = 1024
DIN = 10384
NMETA = 16
EPS = 1e-6
GN_EPS = 64 * 1e-5
OFF_A = 3072
NEG = -30000.0
NCMAX = 272
WCOLS = 10368
C_Q, C_K, C_V, C_Z, C_WDAD, C_RW, C_ZB, C_GA, C_GB = 0, 1024, 2048, 3072, 4096, 4224, 7296, 8320, 9344


class Cfg:
    def __init__(self, npb=8, ns=16):
        self.npb = npb
        self.ns = ns
        self.seq = 256 * npb


def build(cfg):
    nc = bass.Bass("TRN2", target_bir_lowering=False)
    st = ExitStack()
    S = Sched(nc)
    NS = cfg.ns
    SEQ = cfg.seq

    def din(name, shape):
        return nc.dram_tensor(name, shape, F32, kind="ExternalInput").ap()

    def dout(name, shape):
        return nc.dram_tensor(name, shape, F32, kind="ExternalOutput").ap()

    xp = din("xp", [SEQ, D])
    xs = din("xs", [NS * 4, D])
    sg_in = din("sg", [NS, 8, 128, 128])
    sc_in = din("sc", [NS * 3, 3072])
    sr_in = din("sr", [NS, 16, 64, 64])
    ss_in = din("ss", [NS, D])
    meta = din("meta", [NMETA, D])
    ln1_w = din("ln1_w", [D])
    w_in = din("w_in", [D, DIN])
    conv_w = din("conv_w", [4, 3072])
    a_log = din("a_log", [8, 1])
    dt_bias = din("dt_bias", [8, 1])
    gnorm_w = din("gnorm_w", [128, 1])
    w_out_a = din("w_out_a", [D, D])
    mu_in = din("mu", [33, 128])
    w0_in = din("w0", [8, 128])
    w2_in = din("w2", [64, D])
    a0_in = din("a0", [8, 128])
    a2_in = din("a2", [64, D])
    kk_in = din("k_k", [8, 128])
    ka_in = din("k_a", [8, 128])
    rk_in = din("r_k", [8, 128])
    gnw_in = din("gn_w", [8, 128])
    gnb_in = din("gn_b", [8, 128])
    w_out_b = din("w_out_b", [D, D])
    w_out = din("w_out", [D, D])
    lnf_w = din("lnf_w", [D])

    yp = dout("yp", [SEQ, D])
    ys = dout("ys", [NS * 4, D])
    ngp = dout("ngp", [8, 128, 128])
    ncp = dout("ncp", [3, 3072])
    nrp = dout("nrp", [16, 64, 64])
    nsp = dout("nsp", [1, D])
    ngs = dout("ngs", [NS, 8, 128, 128])
    ncs = dout("ncs", [NS * 3, 3072])
    nrs = dout("nrs", [NS, 16, 64, 64])
    nss = dout("nss", [NS, D])

    WG = {}
    for _nm, _w in (('q', 1024), ('k', 1024), ('v', 1024), ('z', 1024), ('wdad', 128), ('rw', 3072),
                    ('zb', 1024), ('ga', 1024), ('gb', 1024)):
        WG[_nm] = nc.dram_tensor("Wb_" + _nm, [128, 8, _w], BF16, kind="Internal").ap()
    Wab = nc.dram_tensor("Wab", [128, 8, 16], BF16, kind="Internal").ap()
    Wo = [nc.dram_tensor("Wo%d" % _i, [128, 8, D], BF16, kind="Internal").ap() for _i in range(3)]

    def sb(name, shape, dt=F32):
        return st.enter_context(nc.sbuf_tensor("s_" + name, shape, dt))

    def psum(name, shape, dt=F32):
        return st.enter_context(nc.psum_tensor("p_" + name, shape, dt))

    def rw(aps):
        return [a for a in aps if a is not None and not isinstance(a, (int, float))]

    def mm(out, lhsT, rhs, start=True, stop=True):
        S.add('pe', lambda e: e.matmul(out, lhsT=lhsT, rhs=rhs, start=start, stop=stop),
              reads=[lhsT, rhs], writes=[out])

    def tr(out, in_, idn):
        S.add('pe', lambda e: e.transpose(out=out, in_=in_, identity=idn), reads=[in_, idn], writes=[out])

    def act(out, in_, func, bias=None, scale=None, accum=None):
        kw = {}
        if bias is not None:
            kw['bias'] = bias
        if scale is not None:
            kw['scale'] = scale
        if accum is not None:
            kw['accum_out'] = accum
        S.add('act', lambda e: e.activation(out=out, in_=in_, func=func, **kw),
              reads=rw([in_, bias, scale]), writes=rw([out, accum]))

    def tt(eng, out, in0, in1, op):
        S.add(eng, lambda e: e.tensor_tensor(out=out, in0=in0, in1=in1, op=op), reads=[in0, in1], writes=[out])

    def ts(eng, out, in0, s1, s2=None, op0=ALU.mult, op1=None):
        kw = {}
        if op1 is not None:
            kw['op1'] = op1
        S.add(eng, lambda e: e.tensor_scalar(out=out, in0=in0, scalar1=s1, scalar2=s2, op0=op0, **kw),
              reads=rw([in0, s1, s2]), writes=[out])

    def stt(eng, out, in0, scalar, in1, op0, op1):
        S.add(eng, lambda e: e.scalar_tensor_tensor(out=out, in0=in0, scalar=scalar, in1=in1, op0=op0, op1=op1),
              reads=rw([in0, scalar, in1]), writes=[out])

    def cp(eng, out, in_):
        if eng == 'act':
            S.add('act', lambda e: e.copy(out=out, in_=in_), reads=[in_], writes=[out])
        else:
            S.add(eng, lambda e: e.tensor_copy(out=out, in_=in_), reads=[in_], writes=[out])

    def memset(eng, out, val):
        S.add(eng, lambda e: e.memset(out, val), writes=[out])

    def dma(eng, out, in_, key):
        S.add(eng, lambda e: e.dma_start(out=out, in_=in_), reads=[in_], writes=[out], dma=key)

    def recip(out, in_):
        S.add('dve', lambda e: e.reciprocal(out=out, in_=in_), reads=[in_], writes=[out])

    def rsum(out, in_):
        S.add('dve', lambda e: e.reduce_sum(out=out, in_=in_, axis=AX.X), reads=[in_], writes=[out])

    def scan(out, d0, d1):
        S.add('dve', lambda e: e.tensor_tensor_scan(out=out, data0=d0, data1=d1, initial=0.0,
                                                    op0=ALU.mult, op1=ALU.add), reads=[d0, d1], writes=[out])

    def asel(out, in_, pattern, cmp, fill, base, cm):
        S.add('pool', lambda e: e.affine_select(out=out, in_=in_, pattern=pattern, compare_op=cmp, fill=fill,
                                                base=base, channel_multiplier=cm), reads=[in_], writes=[out])

    def sigmoid_to(out, in_, tmp, bias=None, scale=1.0):
        if bias is not None:
            act(tmp, in_, AF.Exp, bias=bias, scale=-scale)
        else:
            act(tmp, in_, AF.Exp, scale=-scale)
        act(tmp, tmp, AF.Ln, bias=1.0)
        act(out, tmp, AF.Exp, scale=-1.0)

    def run_pipelined(gens, depth):
        active = []
        it = iter(gens)
        done = False
        while True:
            while len(active) < depth and not done:
                try:
                    active.append(next(it))
                except StopIteration:
                    done = True
            if not active:
                break
            for g in list(active):
                try:
                    next(g)
                except StopIteration:
                    active.remove(g)

    def rsqrt_to(out, in_, eps, mult=1.0):
        act(out, in_, AF.Ln, bias=eps, scale=mult)
        act(out, out, AF.Exp, scale=-0.5)

    PP = psum("PP", [128, 2, 512])
    PB = [psum("PB%d" % i, [128, 1024], BF16) for i in range(2)]
    PG = psum("PG", [128, 4, 512])
    ctr = {'pp': 0, 'pb': 0, 'pg': 0}

    def ppbank():
        i = ctr['pp'] % 2
        ctr['pp'] += 1
        return PP[:, i, :]

    def pbbank():
        i = ctr['pb'] % 2
        ctr['pb'] += 1
        return PB[i]

    def pgbank():
        i = ctr['pg'] % 4
        ctr['pg'] += 1
        return PG[:, i, :]

    ident = sb("ident", [128, 128])
    identb = sb("identb", [128, 128], BF16)
    ones = sb("ones", [128, 128])
    bones = sb("bones", [128, 128])
    nmS = sb("nmS", [64, 64])
    nmTI = sb("nmTI", [64, 64])
    m_st = sb("m_st", [64, 64])
    m_stT = sb("m_stT", [64, 64])
    m_inT = sb("m_inT", [64, 64])
    Esel = sb("Esel", [8, 8, 64])
    memset('pool', ident[:], 0.0)
    asel(ident[:], ident[:], [[-1, 128]], ALU.not_equal, 1.0, 0, 1)
    cp('pool', identb[:], ident[:])
    memset('pool', ones[:], 1.0)
    memset('pool', bones[:], 0.0)
    memset('pool', bones[0:64, 0:64], 1.0)
    memset('pool', bones[64:128, 64:128], 1.0)
    memset('pool', nmS[:], 0.0)
    asel(nmS[:], nmS[:], [[-1, 64]], ALU.is_gt, NEG, 0, 1)
    memset('pool', nmTI[:], 0.0)
    asel(nmTI[:], nmTI[:], [[1, 64]], ALU.is_ge, NEG, 0, -1)
    memset('pool', m_st[:], 1.0)
    asel(m_st[:], m_st[:], [[-1, 64]], ALU.is_gt, 0.0, 0, 1)
    memset('pool', m_stT[:], 1.0)
    asel(m_stT[:], m_stT[:], [[1, 64]], ALU.is_gt, 0.0, 0, -1)
    memset('pool', m_inT[:], 1.0)
    asel(m_inT[:], m_inT[:], [[1, 64]], ALU.is_ge, 0.0, 0, -1)
    memset('pool', Esel[:], 0.0)
    asel(Esel[:], Esel[:], [[-1, 8], [0, 64]], ALU.not_equal, 1.0, 0, 1)

    hrow = sb("hrow", [128, D])
    xrow = sb("xrow", [128, D])
    rstat = sb("rstat", [128, 4])
    memset('pool', hrow[:], 0.0)

    pstage = xrow[0:33, 0:128]

    def load_cols(name, src, ntile):
        t = sb(name, [128, ntile])
        dma('sp', pstage[0:ntile, :], src, 'c')
        pt = pgbank()
        tr(pt[:, 0:ntile], pstage[0:ntile, :], ident[0:ntile, 0:ntile])
        cp('act', t[:], pt[:, 0:ntile])
        return t

    mu = load_cols("mu", mu_in, 33)
    w0c = load_cols("w0c", w0_in, 8)
    a0c = load_cols("a0c", a0_in, 8)
    kkc = load_cols("kkc", kk_in, 8)
    kac = load_cols("kac", ka_in, 8)
    rkc = load_cols("rkc", rk_in, 8)
    gnwc = load_cols("gnwc", gnw_in, 8)
    gnbc = load_cols("gnbc", gnb_in, 8)
    nw0c = sb("nw0c", [128, 8])
    na0c = sb("na0c", [128, 8])
    ts('dve', nw0c[:], w0c[:], -1.0)
    ts('dve', na0c[:], a0c[:], -1.0)
    omka = sb("omka", [128, 8])
    ts('dve', omka[:], kac[:], -1.0, 1.0, op0=ALU.mult, op1=ALU.add)
    convw = sb("convw", [128, 24, 4])
    for q in range(3):
        dma('sp', xrow[0:4, :], conv_w[:, q * 1024:(q + 1) * 1024], 'c')
        for j in range(8):
            ft = q * 8 + j
            pt = pgbank()
            tr(pt[:, 0:4], xrow[0:4, j * 128:(j + 1) * 128], ident[0:4, 0:4])
            cp('act', convw[:, ft, :], pt[:, 0:4])
    ln1bc = sb("ln1bc", [128, D])
    lnfbc = sb("lnfbc", [128, D])
    dma('sp', ln1bc[:], ln1_w.partition_broadcast(128), 'c')
    dma('sp', lnfbc[:], lnf_w.partition_broadcast(128), 'c')
    alog = sb("alog", [8, 1])
    dtb = sb("dtb", [8, 1])
    nA = sb("nA", [8, 1])
    dma('sp', alog[:], a_log, 'c')
    dma('sp', dtb[:], dt_bias, 'c')
    act(nA[:], alog[:], AF.Exp)
    ts('dve', nA[:], nA[:], -1.0)
    gnw = sb("gnw", [128, 1])
    dma('sp', gnw[:], gnorm_w, 'c')
    wa2 = sb("wa2", [128, D])
    dma('sp', wa2[0:64, :], w2_in, 'c')
    dma('sp', wa2[64:128, :], a2_in, 'c')

    import os as _os2
    _KS = float(_os2.environ.get('KSTOP', '999'))
    w_v = w_in.rearrange("(dh dl) c -> dl dh c", dl=128)
    if _KS >= 1:
        for dh in range(8):
            dma('pool', Wab[:, dh, :], w_v[:, dh, OFF_A:OFF_A + 16], 'wcast')
        for nm, c0 in (('q', 0), ('k', 1024), ('v', 2048), ('z', 3088), ('wdad', 7184)):
            wd_ = WG[nm].shape[2]
            for dh in range(8):
                dma('pool', WG[nm][:, dh, :], w_v[:, dh, c0:c0 + wd_], 'wcast')
        for dh in range(8):
            for j in range(3):
                dma('pool', WG['rw'][:, dh, :].rearrange("d (p j f) -> d p j f", p=8, j=3)[:, :, j, :],
                    w_v[:, dh, 4112 + j * 1024:4112 + (j + 1) * 1024].rearrange("d (p f) -> d p f", p=8), 'wcast')
        for nm, c0 in (('zb', 7312), ('ga', 8336)):
            for dh in range(8):
                dma('pool', WG[nm][:, dh, :], w_v[:, dh, c0:c0 + 1024], 'wcast')
        for dh in range(8):
            dma('pool', Wo[0][:, dh, :], w_out_a.rearrange("(dh dl) c -> dl dh c", dl=128)[:, dh, :], 'wcast')
        for dh in range(8):
            dma('pool', WG['gb'][:, dh, :], w_v[:, dh, 9360:9360 + 1024], 'wcast')
        for i, wsrc in ((1, w_out_b), (2, w_out)):
            wv = wsrc.rearrange("(dh dl) c -> dl dh c", dl=128)
            for dh in range(8):
                dma('pool', Wo[i][:, dh, :], wv[:, dh, :], 'wcast')
    wab = sb("wab", [128, 8, 16], BF16)
    dma('sp', wab[:], Wab, 'c')

    NWB = 2
    wbuf = [sb("wbuf%d" % i, [128, 8, 512], BF16) for i in range(NWB)]
    wctr = [0]

    def wload(src):
        b = wbuf[wctr[0] % NWB]
        key = 'w%d' % (wctr[0] % NWB)
        wctr[0] += 1
        dma('sp', b[:, :, 0:src.shape[2]], src, key)
        return b

    hT = sb("hT", [128, 8, NCMAX], BF16)
    BT = [sb("BT%d" % i, [128, 8, NCMAX], BF16) for i in range(10)]
    oaT = sb("oaT", [128, 8, NCMAX], BF16)
    obT = BT[0]
    bg2 = sb("bg2", [128, 16, NCMAX], BF16)
    rg2 = sb("rg2", [128, 16, NCMAX], BF16)
    memset('pool', bg2[:], 0.0)
    memset('pool', rg2[:], 0.0)
    mgT = BT[9]
    MG = BT[7]

    Sg = sb("Sg", [128, 8, 128])
    Sgb = sb("Sgb", [128, 8, 128], BF16)
    Mr = sb("Mr", [128, 8, 64])
    Mrb = sb("Mrb", [128, 8, 64], BF16)
    NSQ = max(NS, 1)
    hist = sb("hist", [128, 24, 3 * NSQ])
    pbh = sb("pbh", [128, 33])
    scanm = sb("scanm", [128, NCMAX])

    cin = sb("cin", [128, NCMAX + 4])
    T = [sb("T%d" % i, [128, NCMAX + 4]) for i in range(15)]
    TX = [sb("TX%d" % i, [128, NCMAX + 4]) for i in range(14)]
    g8 = {n: sb("g8" + n, [8, NCMAX]) for n in ('g', 'gc', 'ngc', 'egc', 'beta', 'beg', 'etail')}
    NCHMAX = max(5, NS)
    eglD = sb("eglD", [8, NCHMAX, 8])
    eglbc = sb("eglbc", [128, NCHMAX, 8])
    eglR = sb("eglR", [128, 8, NCHMAX])

    def ctile(name, shape, dt=F32):
        return sb(name, [64] + shape, dt)
    sc24 = ctile("sc24", [24])
    kbg = ctile("kbg", [8, 128], BF16)
    ktl = ctile("ktl", [8, 128], BF16)
    vbt = ctile("vbt", [8, 128], BF16)
    Vtm, atl, ktlR = kbg, ktl, vbt
    tmpA = ctile("tmpA", [8, 64])
    DmS = ctile("DmS", [8, 64])
    DmT = ctile("DmT", [8, 64])
    RHS = ctile("RHS", [8, 64], BF16)
    Nm = [ctile("Nm%d" % i, [8, 64], BF16) for i in range(2)]
    NTm = [ctile("NTm%d" % i, [8, 64], BF16) for i in range(2)]
    RT = [ctile("RT%d" % i, [8, 64], BF16) for i in range(2)]
    AqkT = ctile("AqkT", [8, 64], BF16)
    AakT = ctile("AakT", [8, 64], BF16)
    AqbT = ctile("AqbT", [8, 64], BF16)
    nwT = sb("nwT", [128, 8, 64], BF16)
    vnew = ctile("vnew", [8, 128], BF16)
    osq = ctile("osq", [4, 128])
    ysq = osq
    ost = ctile("ost", [8, 4])
    yst = ost
    onb = ctile("onb", [8, 128], BF16)
    ynb = onb
    Urw = ctile("Urw", [8, 64], BF16)
    Nm2 = [ctile("Nm2_%d" % i, [8, 64], BF16) for i in range(2)]
    NTm2 = [ctile("NTm2_%d" % i, [8, 64], BF16) for i in range(2)]
    RT2 = [ctile("RT2_%d" % i, [8, 64], BF16) for i in range(2)]
    RHS2 = ctile("RHS2", [8, 64], BF16)
    Urw2 = ctile("Urw2", [8, 64], BF16)
    RSET = [dict(Nm=Nm, NTm=NTm, RT=RT, RHS=RHS, Urw=Urw, AakT=AakT, AqbT=AqbT, AqkT=AqkT),
            dict(Nm=Nm2, NTm=NTm2, RT=RT2, RHS=RHS2, Urw=Urw2, AakT=vnew[:, :, 0:64], AqbT=vnew[:, :, 64:128],
                 AqkT=nwT[0:64, :, :])]
    strw = xrow[0:64, :].rearrange("p (h k) -> p h k", k=64)
    cvt = xrow[0:64, 0:768]

    class Blk:
        pass

    blocks = []
    for b in range(cfg.npb):
        B = Blk()
        B.kind = 'p'
        B.idx = b
        B.ntok = 256 + (NMETA if b == 0 else 0)
        B.ncols = B.ntok
        B.nseg = 1
        B.L = B.ntok
        B.first = (b == 0)
        B.last = (b == cfg.npb - 1)
        chunks = []
        c = 0
        if b == 0:
            chunks.append((0, NMETA))
            c = NMETA
        while c < B.ntok:
            chunks.append((c, 64))
            c += 64
        B.segs = [dict(sid=0, chunks=chunks)]
        B.chunks = chunks
        tiles = []
        r = 0
        while r < B.ntok:
            n = min(128, B.ntok - r)
            tiles.append((r, n))
            r += n
        B.tiles = tiles
        blocks.append(B)
    if NS > 0:
        B = Blk()
        B.kind = 's'
        B.idx = 0
        B.ntok = 4 * NS
        B.ncols = 4 * NS + NS
        B.nseg = NS
        B.L = 4
        B.first = True
        B.last = True
        B.segs = [dict(sid=s, chunks=[(4 * s, 4)]) for s in range(NS)]
        B.chunks = [(4 * s, 4) for s in range(NS)]
        B.tiles = [(0, B.ntok)]
        blocks.append(B)

    def load_x_tile(B, r0, n, dst):
        if B.kind == 'p':
            if B.idx == 0 and r0 == 0:
                dma('sp', dst[0:NMETA, :], meta, 'x')
                dma('sp', dst[NMETA:n, :], xp[0:n - NMETA, :], 'x')
            else:
                s0 = r0 + 256 * B.idx - (NMETA if B.idx == 0 else 0)
                dma('sp', dst[0:n, :], xp[s0:s0 + n, :], 'x')
        else:
            dma('sp', dst[0:n, :], xs[0:n, :], 'x')

    def rms_rows(src, n, dst, wbc):
        act(dst[0:n, :], src[0:n, :], AF.Square, accum=rstat[0:n, 0:1])
        rsqrt_to(rstat[0:n, 3:4], rstat[0:n, 0:1], EPS, 1.0 / D)
        stt('dve', dst[0:n, :], src[0:n, :], rstat[0:n, 3:4], wbc[0:n, :], ALU.mult, ALU.mult)

    def front(B):
        for ti, (r0, n) in enumerate(B.tiles):
            load_x_tile(B, r0, n, xrow)
            rms_rows(xrow, n, hrow, ln1bc)
            if B.kind == 's':
                dma('sp', hrow[64:64 + NS, :], ss_in, 'x')
                for s in range(NS):
                    dma('sp', nss[s:s + 1, :], hrow[4 * s + 3:4 * s + 4, :], 'out')
            elif B.last and ti == len(B.tiles) - 1:
                dma('sp', nsp, hrow[n - 1:n, :], 'out')
            for dc in range(8):
                pt = ppbank()
                if B.kind == 's':
                    tr(pt[:, 0:64 + NS], hrow[0:64 + NS, dc * 128:(dc + 1) * 128], ident[0:64 + NS, 0:64 + NS])
                    cp('act', hT[:, dc, 0:B.ntok], pt[:, 0:B.ntok])
                    cp('act', hT[:, dc, B.ntok:B.ntok + NS], pt[:, 64:64 + NS])
                else:
                    tr(pt[:, 0:n], hrow[0:n, dc * 128:(dc + 1) * 128], ident[0:n, 0:n])
                    cp('act' if dc % 2 == 0 else 'dve', hT[:, dc, r0:r0 + n], pt[:, 0:n])

    def proj(wt, ti, ncols, M=128, wcol0=None):
        ps = ppbank()
        c0 = ti * 128 if wcol0 is None else wcol0
        for dc in range(8):
            mm(ps[0:M, 0:ncols], wt[:, dc, c0:c0 + M], hT[:, dc, 0:ncols], start=(dc == 0), stop=(dc == 7))
        return ps

    def tokview(B, ap2d):
        return ap2d.rearrange("p (s l) -> p s l", l=B.L)

    CHK = [None]

    def gdn_prep(B):
        nt = B.ntok
        qT, kT, qgT, vT, zsT = BT[0], BT[1], BT[2], BT[3], BT[4]
        memset('pool', scanm[:, 0:nt], 1.0)
        for (c0, C) in B.chunks:
            memset('pool', scanm[:, c0:c0 + 1], 0.0)
        psa = proj(wab, 0, nt, M=8, wcol0=0)
        act(T[0][0:8, 0:nt], psa[0:8, 0:nt], AF.Exp, bias=dtb[:])
        act(T[0][0:8, 0:nt], T[0][0:8, 0:nt], AF.Ln, bias=1.0)
        ts('dve', g8['g'][:, 0:nt], T[0][0:8, 0:nt], nA[:])
        psb = proj(wab, 0, nt, M=8, wcol0=8)
        sigmoid_to(g8['beta'][:, 0:nt], psb[0:8, 0:nt], T[0][0:8, 0:nt])
        scan(g8['gc'][:, 0:nt], scanm[0:8, 0:nt], g8['g'][:, 0:nt])
        ts('dve', g8['ngc'][:, 0:nt], g8['gc'][:, 0:nt], -1.0)
        act(g8['egc'][:, 0:nt], g8['gc'][:, 0:nt], AF.Exp)
        tt('dve', g8['beg'][:, 0:nt], g8['beta'][:, 0:nt], g8['egc'][:, 0:nt], ALU.mult)
        for ci, (c0, C) in enumerate(B.chunks):
            last = c0 + C - 1
            ts('dve', g8['etail'][:, c0:c0 + C], g8['ngc'][:, c0:c0 + C], g8['gc'][:, last:last + 1], op0=ALU.add)
            ts('dve', eglD[:, ci, :], ident[0:8, 0:8], g8['egc'][:, last:last + 1])
        act(g8['etail'][:, 0:nt], g8['etail'][:, 0:nt], AF.Exp)
        nch = len(B.chunks)
        pe_ = pgbank()
        mm(pe_[:, 0:nch * 8], ones[0:8, :], eglD[:, 0:nch, :].rearrange("p c h -> p (c h)"))
        cp('act', eglbc[:, 0:nch, :].rearrange("p c h -> p (c h)"), pe_[:, 0:nch * 8])
        wth = {}

        def qkv_tile(gi, gname, ti):
            ft = gi * 8 + ti
            if ti % 4 == 0:
                wth[(gi, ti // 4)] = wload(WG[gname][:, :, ti * 128:ti * 128 + 512])
            wt = wth[(gi, ti // 4)]
            ps = proj(wt, ti % 4, nt)
            if ft % 2 == 0:
                cinx, A0, A1, A2, A3 = cin, T[0], T[1], T[2], T[3]
            else:
                cinx, A0, A1, A2, A3 = T[4], T[5], T[6], T[7], T[8]
            cv = cinx[:, 0:B.nseg * (B.L + 3)].rearrange("p (s l) -> p s l", l=B.L + 3)
            cp('act', cv[:, :, 3:3 + B.L], tokview(B, ps[:, 0:nt]))
            hv = hist[:, ft, 0:3 * B.nseg].rearrange("p (s l) -> p s l", l=3)
            if B.first and B.kind == 'p':
                memset('pool', cv[:, :, 0:3], 0.0)
            else:
                cp('pool', cv[:, :, 0:3], hv)
            yield
            acc = tokview(B, A0[:, 0:nt])
            ts('dve', acc, cv[:, :, 0:B.L], convw[:, ft, 0:1])
            for i in range(1, 4):
                stt('dve', acc, cv[:, :, i:i + B.L], convw[:, ft, i:i + 1], acc, ALU.mult, ALU.add)
            cp('pool', hv, cv[:, :, B.L:B.L + 3])
            yield
            sigmoid_to(A1[:, 0:nt], A0[:, 0:nt], A1[:, 0:nt])
            yield
            if gname == 'v':
                tt('dve', vT[:, ti, 0:nt], A0[:, 0:nt], A1[:, 0:nt], ALU.mult)
                return
            tt('dve', A1[:, 0:nt], A0[:, 0:nt], A1[:, 0:nt], ALU.mult)
            act(A2[:, 0:nt], A1[:, 0:nt], AF.Square)
            yield
            pn = pgbank()
            mm(pn[:, 0:nt], ones[:], A2[:, 0:nt])
            act(A3[:, 0:nt], pn[:, 0:nt], AF.Ln, bias=1e-6)
            act(A3[:, 0:nt], A3[:, 0:nt], AF.Exp, scale=-0.5)
            yield
            if gname == 'k':
                tt('dve', kT[:, ti, 0:nt], A1[:, 0:nt], A3[:, 0:nt], ALU.mult)
            else:
                stt('dve', qT[:, ti, 0:nt], A1[:, 0:nt], 128.0 ** -0.5, A3[:, 0:nt], ALU.mult, ALU.mult)
                ts('dve', A3[0:8, 0:nt], g8['egc'][:, 0:nt], ident[0:8, ti:ti + 1])
                yield
                pq = pgbank()
                mm(pq[:, 0:nt], ones[0:8, :], A3[0:8, 0:nt])
                tt('dve', qgT[:, ti, 0:nt], qT[:, ti, 0:nt], pq[:, 0:nt], ALU.mult)

        run_pipelined((qkv_tile(gi, gname, ti) for gi, gname in enumerate(('q', 'k', 'v')) for ti in range(8)), 2)
        def z_tile(ti):
            if ti % 4 == 0:
                wth[('z', ti // 4)] = wload(WG['z'][:, :, ti * 128:ti * 128 + 512])
            ps = proj(wth[('z', ti // 4)], ti % 4, nt)
            zt = T[ti % 2]
            yield
            sigmoid_to(zt[:, 0:nt], ps[:, 0:nt], zt[:, 0:nt])
            yield
            tt('dve', zsT[:, ti, 0:nt], ps[:, 0:nt], zt[:, 0:nt], ALU.mult)

        run_pipelined((z_tile(ti) for ti in range(8)), 2)

    def tri_inverse_gen(N0, NT0, C, res, Nm_=None, NTm_=None, RT_=None, nh=8):
        Nm_ = Nm if Nm_ is None else Nm_
        NTm_ = NTm if NTm_ is None else NTm_
        RT_ = RT if RT_ is None else RT_
        cur = 0
        Icb = ident[0:C, 0:C].unsqueeze(1).to_broadcast([C, nh, C])
        tt('dve', RT_[0][0:C, 0:nh, 0:C], NT0, Icb, ALU.add)
        P, PT = N0, NT0
        nlev = 0
        while (1 << (nlev + 1)) < C:
            nlev += 1
        for m in range(1, nlev + 1):
            pP = pgbank()
            pPv = pP[0:C, 0:nh * C].rearrange("p (h c) -> p h c", c=C)
            for h in range(nh):
                mm(pPv[:, h, :], PT[:, h, :], P[:, h, :])
            newP = Nm_[m % 2][0:C, 0:nh, 0:C]
            cp('act', newP, pPv)
            newPT = None
            if m < nlev:
                pT = pgbank()
                pTv = pT[0:C, 0:nh * C].rearrange("p (h c) -> p h c", c=C)
                for h in range(nh):
                    mm(pTv[:, h, :], P[:, h, :], PT[:, h, :])
                newPT = NTm_[m % 2][0:C, 0:nh, 0:C]
                cp('act', newPT, pTv)
            yield
            pR = pgbank()
            pRv = pR[0:C, 0:nh * C].rearrange("p (h c) -> p h c", c=C)
            old = RT_[cur][0:C, 0:nh, 0:C]
            for h in range(nh):
                mm(pRv[:, h, :], newP[:, h, :], old[:, h, :])
            cur ^= 1
            tt('dve', RT_[cur][0:C, 0:nh, 0:C], pRv, old, ALU.add)
            P = newP
            PT = newPT
            yield
        res['RT'] = RT_[cur][0:C, 0:nh, 0:C]

    def tri_inverse(N0, NT0, C):
        res = {}
        for _ in tri_inverse_gen(N0, NT0, C, res):
            pass
        return res['RT']

    def gdn_chunk(B, ci, c0, C):
        qT, kT, qgT, vT, zsT = BT[0], BT[1], BT[2], BT[3], BT[4]
        cs = slice(c0, c0 + C)
        p1 = pgbank()
        for i, nm in enumerate(('beta', 'beg', 'etail')):
            tr(p1[0:C, 8 * i:8 * i + 8], g8[nm][:, cs], ident[0:8, 0:8])
        cp('act', sc24[0:C, :], p1[0:C, 0:24])

        def scb(i, w):
            return sc24[0:C, 8 * i:8 * i + 8].unsqueeze(2).to_broadcast([C, 8, w])

        def hv3(ap2, w):
            return ap2.rearrange("p (h c) -> p h c", c=w)
        pk = pbbank()
        pkv = hv3(pk[0:C, :], 128)
        for h in range(8):
            tr(pkv[:, h, :], kT[:, h, cs], identb[:])
        tt('dve', kbg[0:C], pkv, scb(1, 128), ALU.mult)
        tt('dve', ktl[0:C], pkv, scb(2, 128), ALU.mult)
        pv = pbbank()
        pvv = hv3(pv[0:C, :], 128)
        for h in range(8):
            tr(pvv[:, h, :], vT[:, h, cs], identb[:])
        tt('dve', vbt[0:C], pvv, scb(0, 128), ALU.mult)
        pd = pgbank()
        pdv = hv3(pd[0:C, 0:8 * C], C)
        for h in range(8):
            mm(pdv[:, h, :], g8['gc'][:, cs], Esel[:, h, 0:C], start=True, stop=False)
            mm(pdv[:, h, :], Esel[:, h, 0:C], g8['ngc'][:, cs], start=False, stop=True)
        tt('dve', DmS[0:C, :, 0:C], pdv, nmS[0:C, 0:C].unsqueeze(1).to_broadcast([C, 8, C]), ALU.add)
        act(DmS[0:C, :, 0:C], DmS[0:C, :, 0:C], AF.Exp)
        pdT = pgbank()
        pdTv = hv3(pdT[0:C, 0:8 * C], C)
        for h in range(8):
            mm(pdTv[:, h, :], Esel[:, h, 0:C], g8['gc'][:, cs], start=True, stop=False)
            mm(pdTv[:, h, :], g8['ngc'][:, cs], Esel[:, h, 0:C], start=False, stop=True)
        tt('dve', DmT[0:C, :, 0:C], pdTv, nmTI[0:C, 0:C].unsqueeze(1).to_broadcast([C, 8, C]), ALU.add)
        act(DmT[0:C, :, 0:C], DmT[0:C, :, 0:C], AF.Exp)
        pg_ = pgbank()
        pgv = hv3(pg_[0:C, 0:8 * C], C)
        for h in range(8):
            mm(pgv[:, h, :], kT[:, h, cs], kT[:, h, cs])
        stt('dve', tmpA[0:C, :, 0:C], pgv, -1.0, DmS[0:C, :, 0:C], ALU.mult, ALU.mult)
        N0 = Nm[0][0:C, :, 0:C]
        tt('dve', N0, tmpA[0:C, :, 0:C], scb(0, C), ALU.mult)
        pq = pgbank()
        pqv = hv3(pq[0:C, 0:8 * C], C)
        for h in range(8):
            mm(pqv[:, h, :], kT[:, h, cs], qT[:, h, cs])
        tt('dve', AqkT[0:C, :, 0:C], pqv, DmT[0:C, :, 0:C], ALU.mult)
        pn = pbbank()
        pnv = hv3(pn[0:C, 0:8 * C], C)
        for h in range(8):
            tr(pnv[:, h, :], N0[:, h, :], identb[0:C, 0:C])
        NT0 = NTm[0][0:C, :, 0:C]
        cp('act', NT0, pnv)
        RTf = tri_inverse(N0, NT0, C)
        pw = pgbank()
        pwv = hv3(pw[:, 0:8 * C], C)
        for h in range(8):
            mm(pwv[:, h, :], kbg[0:C, h, :], RTf[:, h, :])
        ts('dve', nwT[:, :, 0:C], pwv, -1.0)
        for half in range(2):
            p2 = pgbank()
            p2v = hv3(p2[0:C, :], 128)
            for hh in range(4):
                h = half * 4 + hh
                mm(p2v[:, hh, :], RTf[:, h, :], vbt[0:C, h, :], start=True, stop=False)
                mm(p2v[:, hh, :], nwT[:, h, 0:C], Sgb[:, h, :], start=False, stop=True)
            cp('act', vnew[0:C, half * 4:half * 4 + 4, :], p2v)
        for half in range(2):
            p3 = pgbank()
            p3v = hv3(p3[0:C, :], 128)
            for hh in range(4):
                h = half * 4 + hh
                mm(p3v[:, hh, :], qgT[:, h, cs], Sgb[:, h, :], start=True, stop=False)
                mm(p3v[:, hh, :], AqkT[0:C, h, 0:C], vnew[0:C, h, :], start=False, stop=True)
            hs = slice(half * 4, half * 4 + 4)
            act(osq[0:C], p3v, AF.Square)
            rsum(ost[0:C, hs, 0:1], osq[0:C])
            rsqrt_to(ost[0:C, hs, 3:4], ost[0:C, hs, 0:1], EPS, 1.0 / 128)
            tt('dve', onb[0:C, hs, :], p3v, ost[0:C, hs, 3:4].to_broadcast([C, 4, 128]), ALU.mult)
        po = pbbank()
        pov = hv3(po[:, 0:8 * C], C)
        for h in range(8):
            tr(pov[:, h, :], onb[0:C, h, :], identb[0:C, 0:C])
        stt('dve', oaT[:, :, cs], pov, gnw[:, 0:1], zsT[:, :, cs], ALU.mult, ALU.mult)
        for half in range(2):
            p4 = pgbank()
            p4v = hv3(p4[:, :], 128)
            for hh in range(4):
                h = half * 4 + hh
                mm(p4v[:, hh, :], ktl[0:C, h, :], vnew[0:C, h, :])
            for hh in range(4):
                h = half * 4 + hh
                stt('dve', Sg[:, h, :], Sg[:, h, :], eglbc[:, ci, h:h + 1], p4v[:, hh, :], ALU.mult, ALU.add)
            cp('act', Sgb[:, half * 4:half * 4 + 4, :], Sg[:, half * 4:half * 4 + 4, :])

    def rwkv_prep(B):
        nt = B.ntok
        ncols = B.ncols
        bgT, agT, kgT, rgT, atT, ktT, rvT, bonT, zbT = BT[0], BT[1], BT[2], BT[3], BT[4], BT[5], BT[6], BT[7], BT[8]

        def mixed(ps, fidx, dst, rawbuf=cin):
            raw = rawbuf[:, 0:B.nseg * (B.L + 1)].rearrange("p (s l) -> p s l", l=B.L + 1)
            cp('act', raw[:, :, 1:1 + B.L], tokview(B, ps[:, 0:nt]))
            if B.kind == 'p':
                if B.first:
                    memset('pool', raw[:, :, 0:1], 0.0)
                else:
                    cp('pool', raw[:, 0, 0:1], pbh[:, fidx:fidx + 1])
                cp('pool', pbh[:, fidx:fidx + 1], raw[:, 0, B.L:B.L + 1])
            else:
                cp('act', raw[:, :, 0:1], ps[:, nt:nt + B.nseg].unsqueeze(2))
            dv = tokview(B, dst)
            tt('dve', dv, raw[:, :, 0:B.L], raw[:, :, 1:1 + B.L], ALU.subtract)
            stt('dve', dv, dv, mu[:, fidx:fidx + 1], raw[:, :, 1:1 + B.L], ALU.mult, ALU.add)

        wt = wload(WG['wdad'])
        ps = proj(wt, 0, ncols)
        mixed(ps, 24, T[0][:, 0:nt])
        sigmoid_to(T[1][0:64, 0:nt], T[0][0:64, 0:nt], T[1][0:64, 0:nt], scale=2.0)
        ts('dve', T[1][0:64, 0:nt], T[1][0:64, 0:nt], 2.0, -1.0, op0=ALU.mult, op1=ALU.add)
        def pair_gen(p):
            if p % 2 == 0:
                rawb = cin
                TT = {i: T[i] for i in range(2, 15)}
            else:
                rawb = TX[0]
                TT = {i: TX[i - 1] for i in range(2, 15)}
            Tr, Tk, Tv = TT[2], TT[3], TT[4]
            wt = wload(WG['rw'][:, :, p * 384:(p + 1) * 384])
            for j, dst in enumerate((Tr, Tk, Tv)):
                ps = proj(wt, j, ncols)
                mixed(ps, 8 * j + p, dst[:, 0:nt], rawb)
                yield
            r_p, k_p, v_p = Tr[:, 0:nt], Tk[:, 0:nt], Tv[:, 0:nt]
            wlog, gc, eg, eng, egp = (TT[i][:, 0:nt] for i in (5, 6, 7, 8, 9))
            a_, kk, rn, kb2, kg32 = (TT[i][:, 0:nt] for i in (10, 11, 12, 13, 14))
            psw = pgbank()
            mm(psw[:, 0:nt], wa2[0:64, p * 128:(p + 1) * 128], T[1][0:64, 0:nt])
            sigmoid_to(wlog, psw[:, 0:nt], wlog, bias=nw0c[:, p:p + 1])
            psa = pgbank()
            mm(psa[:, 0:nt], wa2[64:128, p * 128:(p + 1) * 128], T[0][64:128, 0:nt])
            sigmoid_to(a_, psa[:, 0:nt], a_, bias=na0c[:, p:p + 1])
            ts('pool', kk, k_p, kkc[:, p:p + 1])
            act(rn, kk, AF.Square)
            yield
            ts('pool', wlog, wlog, -float(np.exp(-0.5)))
            scan(gc, scanm[:, 0:nt], wlog)
            pss = pgbank()
            mm(pss[:, 0:nt], bones[:], rn)
            act(rn, pss[:, 0:nt], AF.Ln, bias=1e-6)
            act(rn, rn, AF.Exp, scale=-0.5)
            yield
            act(eg, gc, AF.Exp)
            act(eng, gc, AF.Exp, scale=-1.0)
            tt('pool', egp, gc, wlog, ALU.subtract)
            act(egp, egp, AF.Exp)
            tt('dve', kk, kk, rn, ALU.mult)
            ts('dve', kb2, a_, kac[:, p:p + 1], omka[:, p:p + 1], op0=ALU.mult, op1=ALU.add)
            tt('dve', kb2, kb2, k_p, ALU.mult)
            yield
            for ci, (c0, C) in enumerate(B.chunks):
                cp('pool', eglR[:, p, ci:ci + 1], eg[:, c0 + C - 1:c0 + C])
            for e_ in range(2):
                hsl = slice(64 * e_, 64 * e_ + 64)
                tt('dve', bg2[hsl, 2 * p + e_, 0:nt], kk[hsl], egp[hsl], ALU.mult)
            ka = wlog
            tt('pool', ka, kk, a_, ALU.mult)
            tt('dve', kg32, kb2, eng, ALU.mult)
            yield
            ag32 = gc
            stt('dve', ag32, ka, -1.0, eng, ALU.mult, ALU.mult)
            cp('pool', kgT[:, p, 0:nt], kg32)
            for e_ in range(2):
                hsl = slice(64 * e_, 64 * e_ + 64)
                tt('dve', rg2[hsl, 2 * p + e_, 0:nt], r_p[hsl], eg[hsl], ALU.mult)
            cp('act', rvT[:, p, 0:nt], v_p)
            yield
            cp('pool', agT[:, p, 0:nt], ag32)
            for ci, (c0, C) in enumerate(B.chunks):
                ts('pool', atT[:, p, c0:c0 + C], ag32[:, c0:c0 + C], eglR[:, p, ci:ci + 1])
                ts('pool', ktT[:, p, c0:c0 + C], kg32[:, c0:c0 + C], eglR[:, p, ci:ci + 1])
            prod = egp
            stt('dve', prod, r_p, rkc[:, p:p + 1], kb2, ALU.mult, ALU.mult)
            yield
            psb_ = pgbank()
            mm(psb_[:, 0:nt], bones[:], prod)
            tt('dve', bonT[:, p, 0:nt], psb_[:, 0:nt], v_p, ALU.mult)

        run_pipelined((pair_gen(p) for p in range(8)), 2)
        wth = {}
        def zb_tile(ti):
            if ti % 4 == 0:
                wth[ti // 4] = wload(WG['zb'][:, :, ti * 128:ti * 128 + 512])
            ps = proj(wth[ti // 4], ti % 4, ncols)
            if ti % 2 == 0:
                zr, z1, z2 = cin, T[14], T[13]
            else:
                zr, z1, z2 = TX[0], TX[13], TX[12]
            mixed(ps, 25 + ti, z1[:, 0:nt], zr)
            yield
            sigmoid_to(z2[:, 0:nt], z1[:, 0:nt], z2[:, 0:nt])
            yield
            tt('dve', zbT[:, ti, 0:nt], z1[:, 0:nt], z2[:, 0:nt], ALU.mult)

        run_pipelined((zb_tile(ti) for ti in range(8)), 2)

    def rwkv_chunk(B, ci, c0, C):
        chk = CHK[0]
        bgT, agT, kgT, rgT, atT, ktT, rvT, bonT, zbT, ynT = BT
        cs = slice(c0, c0 + C)

        def hv3(ap2, w):
            return ap2.rearrange("p (h c) -> p h c", c=w)
        import os as _os3
        _kv = int(_os3.environ.get('KVAR', '3'))
        for k_, (src, dst) in enumerate(((rvT, Vtm), (atT, atl), (ktT, ktlR))[:_kv]):
            pk = pbbank()
            pkv = hv3(pk[0:C, :], 128)
            for p in range(8):
                tr(pkv[:, p, :], src[:, p, cs], identb[:])
            cp('act' if k_ == 0 else 'dve', dst[0:C], pkv)
        Vv = Vtm[0:C].rearrange("p a (e v) -> p (a e) v", e=2)
        chk(6.1)
        def half_gen(half):
            TS = RSET[half]
            heads = list(range(half * 8, half * 8 + 8))
            AakT_, AqbT_, AqkT_, RHS_, Urw_ = TS['AakT'], TS['AqbT'], TS['AqkT'], TS['RHS'], TS['Urw']

            def hop(T_, h):
                if T_ is bg2 or T_ is rg2:
                    return T_[:, h, cs]
                return T_[:, h // 2, cs]

            def score(lt, rt, mask, dst):
                ps_ = pgbank()
                psv = hv3(ps_[0:C, 0:8 * C], C)
                for hi, h in enumerate(heads):
                    mm(psv[:, hi, :], hop(lt, h), hop(rt, h))
                tt('dve', dst, psv, mask[0:C, 0:C].unsqueeze(1).to_broadcast([C, 8, C]), ALU.mult)
            N0 = TS['Nm'][0][0:C, :, 0:C]
            NT0 = TS['NTm'][0][0:C, :, 0:C]
            score(bg2, agT, m_st, N0)
            score(agT, bg2, m_stT, NT0)
            yield
            score(kgT, bg2, m_stT, AakT_[0:C, :, 0:C])
            score(agT, rg2, m_inT, AqbT_[0:C, :, 0:C])
            score(kgT, rg2, m_inT, AqkT_[0:C, :, 0:C])
            yield
            res = {}
            for _ in tri_inverse_gen(N0, NT0, C, res, TS['Nm'], TS['NTm'], TS['RT']):
                yield
            RTf = res['RT']
            p1 = pgbank()
            p1v = hv3(p1[0:C, 0:8 * 64], 64)
            for hi, h in enumerate(heads):
                p = h // 2
                mm(p1v[:, hi, :], hop(bg2, h), Mrb[:, p, :], start=True, stop=False)
                mm(p1v[:, hi, :], AakT_[0:C, hi, 0:C], Vv[:, h, :], start=False, stop=True)
            cp('act', RHS_[0:C], p1v)
            yield
            p2 = pgbank()
            p2v = hv3(p2[0:C, 0:8 * 64], 64)
            for hi, h in enumerate(heads):
                mm(p2v[:, hi, :], RTf[:, hi, :], RHS_[0:C, hi, :])
            cp('act', Urw_[0:C], p2v)
            yield
            p3 = pgbank()
            p3v = hv3(p3[0:C, 0:8 * 64], 64)
            for hi, h in enumerate(heads):
                p = h // 2
                mm(p3v[:, hi, :], hop(rg2, h), Mrb[:, p, :], start=True, stop=False)
                mm(p3v[:, hi, :], AqbT_[0:C, hi, 0:C], Urw_[0:C, hi, :], start=False, stop=False)
                mm(p3v[:, hi, :], AqkT_[0:C, hi, 0:C], Vv[:, h, :], start=False, stop=True)
            ysv = ysq[0:C].rearrange("p a (b v) -> p (a b) v", b=2)
            Ysb = tmpA[0:C]
            cp('act', Ysb, p3v)
            rsum(yst[0:C, :, 0:1], Ysb)
            act(ysv, Ysb, AF.Square)
            rsum(yst[0:C, :, 1:2], ysv)
            ts('dve', yst[0:C, :, 0:1], yst[0:C, :, 0:1], 1.0 / 64)
            tt('dve', yst[0:C, :, 2:3], yst[0:C, :, 0:1], yst[0:C, :, 0:1], ALU.mult)
            stt('dve', yst[0:C, :, 1:2], yst[0:C, :, 1:2], 1.0 / 64, yst[0:C, :, 2:3], ALU.mult, ALU.subtract)
            ts('dve', yst[0:C, :, 1:2], yst[0:C, :, 1:2], GN_EPS, op0=ALU.add)
            act(yst[0:C, :, 3:4], yst[0:C, :, 1:2], AF.Ln)
            act(yst[0:C, :, 3:4], yst[0:C, :, 3:4], AF.Exp, scale=-0.5)
            tt('dve', ysv, Ysb, yst[0:C, :, 0:1].to_broadcast([C, 8, 64]), ALU.subtract)
            ynv8 = ynb[0:C, 0:4, :].rearrange("p a (b v) -> p (a b) v", b=2)
            tt('dve', ynv8, ysv, yst[0:C, :, 3:4].to_broadcast([C, 8, 64]), ALU.mult)
            po = pbbank()
            pov = hv3(po[:, 0:4 * C], C)
            for a in range(4):
                tr(pov[:, a, :], ynb[0:C, a, :], identb[0:C, 0:C])
            cp('act', ynT[:, half * 4:half * 4 + 4, cs], pov)
            yield
            pA = pgbank()
            pAv = hv3(pA[:, 0:4 * 64], 64)
            pBk = pgbank()
            pBv = hv3(pBk[:, 0:4 * 64], 64)
            for a in range(4):
                p = half * 4 + a
                for e, pv in ((0, pAv), (1, pBv)):
                    hi = 2 * a + e
                    h = 2 * p + e
                    mm(pv[:, a, :], atl[0:C, p, :], Urw_[0:C, hi, :], start=True, stop=False)
                    mm(pv[:, a, :], ktlR[0:C, p, :], Vv[:, h, :], start=False, stop=True)
            ps_ = slice(half * 4, half * 4 + 4)
            for e, pv in ((0, pAv), (1, pBv)):
                rows = slice(64 * e, 64 * e + 64)
                tt('dve', Mr[rows, ps_, :], Mr[rows, ps_, :],
                   eglR[rows, ps_, ci:ci + 1].to_broadcast([64, 4, 64]), ALU.mult)
                tt('dve', Mr[rows, ps_, :], Mr[rows, ps_, :], pv[rows], ALU.add)
            cp('act', Mrb[:, ps_, :], Mr[:, ps_, :])

        run_pipelined((half_gen(hf) for hf in range(2)), 2)

    def out_stage(B):
        nt = B.ntok
        bonT, zbT, ynT = BT[7], BT[8], BT[9]
        for p in range(8):
            ob_t = T[p % 2]
            ts('dve', ob_t[:, 0:nt], ynT[:, p, 0:nt], gnwc[:, p:p + 1], gnbc[:, p:p + 1], op0=ALU.mult, op1=ALU.add)
            tt('pool', ob_t[:, 0:nt], ob_t[:, 0:nt], bonT[:, p, 0:nt], ALU.add)
            tt('dve', obT[:, p, 0:nt], ob_t[:, 0:nt], zbT[:, p, 0:nt], ALU.mult)
        wgh = {}

        def gate_tile(bi, src, gcol, ti):
            if ti % 4 == 0:
                wgh[(bi, ti // 4)] = (wload(WG[gcol][:, :, ti * 128:ti * 128 + 512]),
                                      wload(Wo[bi][:, :, ti * 128:ti * 128 + 512]))
            wg, wo = wgh[(bi, ti // 4)]
            psg = proj(wg, ti % 4, nt)
            G1 = T[1] if ti % 2 == 0 else T[3]
            G2 = T[2] if ti % 2 == 0 else T[4]
            pbr = pgbank()
            for ec in range(8):
                mm(pbr[:, 0:nt], wo[:, ec, (ti % 4) * 128:(ti % 4 + 1) * 128], src[:, ec, 0:nt], start=(ec == 0), stop=(ec == 7))
            yield
            sigmoid_to(G1[:, 0:nt], psg[:, 0:nt], G1[:, 0:nt])
            yield
            if bi == 0:
                tt('dve', MG[:, ti, 0:nt], pbr[:, 0:nt], G1[:, 0:nt], ALU.mult)
            else:
                tt('dve', G2[:, 0:nt], pbr[:, 0:nt], G1[:, 0:nt], ALU.mult)
                tt('dve', mgT[:, ti, 0:nt], G2[:, 0:nt], MG[:, ti, 0:nt], ALU.add)

        for bi, (src, gcol) in enumerate(((oaT, 'ga'), (obT, 'gb'))):
            run_pipelined((gate_tile(bi, src, gcol, ti) for ti in range(8)), 2)
        wos = [wload(Wo[2][:, :, 0:512]), wload(Wo[2][:, :, 512:1024])]
        for ti, (r0, n) in enumerate(B.tiles):
            load_x_tile(B, r0, n, hrow)
            for half in range(2):
                px = ppbank()
                for dc in range(8):
                    mm(px[0:n, :], mgT[:, dc, r0:r0 + n], wos[half][:, dc, :], start=(dc == 0), stop=(dc == 7))
                tt('dve', xrow[0:n, half * 512:(half + 1) * 512], px[0:n, :], hrow[0:n, half * 512:(half + 1) * 512], ALU.add)
            rms_rows(xrow, n, hrow, lnfbc)
            if B.kind == 'p':
                if B.idx == 0 and r0 == 0:
                    dma('sp', yp[0:n - NMETA, :], hrow[NMETA:n, :], 'out')
                else:
                    s0 = r0 + 256 * B.idx - (NMETA if B.idx == 0 else 0)
                    dma('sp', yp[s0:s0 + n, :], hrow[0:n, :], 'out')
            else:
                dma('sp', ys[0:n, :], hrow[0:n, :], 'out')

    def load_gdn_state(s):
        dma('sp', Sg[:], sg_in[s].rearrange("h k v -> k h v"), 'st')
        cp('pool', Sgb[:], Sg[:])

    def store_gdn_state(dst):
        dma('sp', dst.rearrange("h k v -> k h v"), Sg[:], 'out')

    def load_rwkv_state(s):
        dma('sp', strw, sr_in[s].rearrange("h v k -> v h k"), 'st')
        for p in range(8):
            pt = pgbank()
            tr(pt[:, 0:64], strw[:, 2 * p:2 * p + 2, :].rearrange("p a b -> p (a b)"), ident[0:64, 0:64])
            cp('act', Mr[:, p, :], pt[:, 0:64])
        cp('pool', Mrb[:], Mr[:])

    def store_rwkv_state(dst):
        for p in range(8):
            pt = pgbank()
            tr(pt[0:64, 0:128], Mr[:, p, :], ident[:])
            cp('act', strw[:, 2 * p:2 * p + 2, :].rearrange("p a b -> p (a b)"), pt[0:64, 0:128])
        dma('sp', dst.rearrange("h v k -> v h k"), strw, 'out')

    def store_conv(B, dst):
        n3 = 3 * B.nseg
        for q in range(4):
            for j in range(6):
                ft = q * 6 + j
                pt = pgbank()
                tr(pt[0:n3, 0:128], hist[:, ft, 0:n3], ident[:])
                cp('act' if j % 2 == 0 else 'dve', cvt[0:n3, j * 128:(j + 1) * 128], pt[0:n3, 0:128])
            dma('sp', dst[:, q * 768:(q + 1) * 768], cvt[0:n3, :], 'out')

    def load_conv_hist(B):
        n3 = 3 * B.nseg
        for q in range(4):
            dma('sp', cvt[0:n3, :], sc_in[:, q * 768:(q + 1) * 768], 'st')
            for j in range(6):
                ft = q * 6 + j
                pt = pgbank()
                tr(pt[:, 0:n3], cvt[0:n3, j * 128:(j + 1) * 128], ident[0:n3, 0:n3])
                cp('act' if j % 2 == 0 else 'dve', hist[:, ft, 0:n3], pt[:, 0:n3])

    import os as _os
    KSTOP = float(_os.environ.get('KSTOP', '999'))

    class _Stop(Exception):
        pass

    def chk(n):
        if n > KSTOP:
            raise _Stop()

    CHK[0] = chk

    def main_prog():
      memset('pool', Sg[:], 0.0)
      memset('pool', Sgb[:], 0.0)
      memset('pool', Mr[:], 0.0)
      memset('pool', Mrb[:], 0.0)
      for B in blocks:
        chk(2)
        front(B)
        if B.kind == 's':
            load_conv_hist(B)
        chk(3)
        gdn_prep(B)
        ci = 0
        for seg in B.segs:
            if B.kind == 's':
                load_gdn_state(seg['sid'])
            for (c0, C) in seg['chunks']:
                chk(4)
                gdn_chunk(B, ci, c0, C)
                ci += 1
            if B.kind == 's':
                store_gdn_state(ngs[seg['sid']])
        if B.kind == 'p' and B.last:
            store_gdn_state(ngp)
        if B.last:
            store_conv(B, ncp if B.kind == 'p' else ncs)
        chk(5)
        rwkv_prep(B)
        ci = 0
        for seg in B.segs:
            if B.kind == 's':
                load_rwkv_state(seg['sid'])
            for (c0, C) in seg['chunks']:
                chk(6)
                rwkv_chunk(B, ci, c0, C)
                ci += 1
            if B.kind == 's':
                store_rwkv_state(nrs[seg['sid']])
        if B.kind == 'p' and B.last:
            store_rwkv_state(nrp)
        chk(7)
        out_stage(B)

    try:
        main_prog()
    except _Stop:
        pass
    if _os.environ.get('KDBG'):
        dbg = dout("dbg", [10, 128, 8, NCMAX])
        for i in range(9):
            for p in range(8):
                cp('dve', T[0][:, 0:NCMAX], BT[i][:, p, :])
                dma('sp', dbg[i, :, p, :], T[0][:, 0:NCMAX], 'out')

    S.emit(st)
    st.close()
    return nc


_NC_CACHE = {}


def make_in_maps(cfg, inputs, ncores):
    f = lambda a: np.ascontiguousarray(np.asarray(a, dtype=np.float32))
    ns = cfg.ns
    shared = {
        "meta": f(inputs["meta_tokens"]),
        "ln1_w": f(inputs["ln1_w"]).reshape(D),
        "w_in": f(inputs["w_in"]).reshape(D, DIN),
        "conv_w": f(inputs["gdn_conv_w"]).reshape(4, 3072),
        "a_log": f(inputs["gdn_a_log"]).reshape(8, 1),
        "dt_bias": f(inputs["gdn_dt_bias"]).reshape(8, 1),
        "gnorm_w": f(inputs["gdn_norm_w"]).reshape(128, 1),
        "w_out_a": f(inputs["w_out_a"]).reshape(D, D),
        "mu": f(inputs["rwkv_mu"]).reshape(33, 128),
        "w0": f(inputs["rwkv_w0"]).reshape(8, 128),
        "w2": f(inputs["rwkv_w2"]).reshape(64, D),
        "a0": f(inputs["rwkv_a0"]).reshape(8, 128),
        "a2": f(inputs["rwkv_a2"]).reshape(64, D),
        "k_k": f(inputs["rwkv_k_k"]).reshape(8, 128),
        "k_a": f(inputs["rwkv_k_a"]).reshape(8, 128),
        "r_k": f(inputs["rwkv_r_k"]).reshape(8, 128),
        "gn_w": f(inputs["rwkv_gn_w"]).reshape(8, 128),
        "gn_b": f(inputs["rwkv_gn_b"]).reshape(8, 128),
        "w_out_b": f(inputs["w_out_b"]).reshape(D, D),
        "w_out": f(inputs["w_out"]).reshape(D, D),
        "lnf_w": f(inputs["lnf_w"]).reshape(D),
    }
    maps = []
    for c in range(ncores):
        m = dict(shared)
        m["xp"] = f(inputs["x_prompt"][c])
        if ns > 0:
            sl = slice(c * ns, (c + 1) * ns)
            m["xs"] = f(inputs["x_sample"][sl]).reshape(ns * 4, D)
            m["sg"] = f(inputs["state_gdn"][0, sl])
            m["sc"] = f(inputs["state_gdn_conv"][0, sl]).reshape(ns * 3, 3072)
            m["sr"] = f(inputs["state_rwkv"][0, sl])
            m["ss"] = f(inputs["state_shift"][0, sl])
        maps.append(m)
    return maps


def gather(cfg, results, ncores):
    ns = cfg.ns
    cat = lambda k, shp: np.concatenate([np.asarray(r[k], dtype=np.float32).reshape(shp) for r in results], axis=0)
    y_prompt = cat("yp", (1, cfg.seq, D))
    ngp = cat("ngp", (1, 8, 128, 128))[None]
    ncp = cat("ncp", (1, 3, 3072))[None]
    nrp = cat("nrp", (1, 16, 64, 64))[None]
    nsp = cat("nsp", (1, D))[None]
    y_sample = cat("ys", (ns, 4, D))
    ngs = cat("ngs", (ns, 8, 128, 128))[None]
    ncs = cat("ncs", (ns, 3, 3072))[None]
    nrs = cat("nrs", (ns, 16, 64, 64))[None]
    nss = cat("nss", (ns, D))[None]
    return (y_prompt, y_sample, ngp, ncp, nrp, nsp, ngs, ncs, nrs, nss)


def kernel(**inputs):
    cfg = Cfg(8, 16)
    if 'nc' not in _NC_CACHE:
        _NC_CACHE['nc'] = build(cfg)
    nc = _NC_CACHE['nc']
    maps = make_in_maps(cfg, inputs, 8)
    res = run_bass_kernel_spmd(nc, maps, core_ids=list(range(8)))
    return gather(cfg, res.results, 8)
```

```python
import numpy as np
from contextlib import ExitStack
import concourse.bass as bass
import concourse.mybir as mybir
from concourse.bass_utils import run_bass_kernel_spmd

F32 = mybir.dt.float32
BF16 = mybir.dt.bfloat16
AF = mybir.ActivationFunctionType
ALU = mybir.AluOpType
AX = mybir.AxisListType

ENGS = ['sp', 'act', 'dve', 'pe', 'pool']


class Op:
    __slots__ = ('eng', 'fn', 'idx', 'pos', 'deps', 'dma_sem', 'dma_ord', 'waits', 'signal', 'semval', 'vc')


class Sched:
    DMA_POOL = {'sp': 32, 'pool': 8, 'act': 8}

    def __init__(self, nc):
        self.nc = nc
        self.ops = []
        self.eng_ops = {e: [] for e in ENGS}
        self.acc = {}
        self.dma_count = {}
        self.dma_keys = []
        self.dma_rr = {}
        self.dma_last = {}

    @staticmethod
    def _box(a):
        t = a.tensor
        name = t.name
        shape = list(t.shape)
        isdram = type(t).__name__.startswith('DRam')
        off = int(a.offset)
        if isdram:
            lo = off
            hi = off
            for st, cnt in a.ap:
                if cnt > 1:
                    if st >= 0:
                        hi += st * (cnt - 1)
                    else:
                        lo += st * (cnt - 1)
            return name, (0, 0, lo, hi)
        row = 1
        for s in shape[1:]:
            row *= s
        p0 = off // row
        f0 = off % row
        pe = 0
        fe = 0
        for st, cnt in a.ap:
            if cnt <= 1 or st == 0:
                continue
            if st % row == 0:
                pe += (st // row) * (cnt - 1)
            else:
                fe += st * (cnt - 1)
        p1 = p0 + pe
        f1 = f0 + fe
        if type(t).__name__.startswith('PSum'):
            eb = 1024 if 'bfloat16' in str(t.dtype) else 512
            f0 = (f0 // eb) * eb
            f1 = (f1 // eb) * eb + eb - 1
            p0 = (p0 // 32) * 32
            p1 = (p1 // 32) * 32 + 31
        return name, (p0, p1, f0, f1)

    @staticmethod
    def _ovl(a, b):
        return not (a[1] < b[0] or b[1] < a[0] or a[3] < b[2] or b[3] < a[2])

    @staticmethod
    def _covers(a, b):
        return a[0] <= b[0] and a[1] >= b[1] and a[2] <= b[2] and a[3] >= b[3]

    def add(self, eng, fn, reads=(), writes=(), dma=None):
        op = Op()
        op.eng = eng
        op.fn = fn
        op.idx = len(self.ops)
        op.deps = set()
        op.dma_sem = dma
        op.dma_ord = None
        op.signal = False
        for a in reads:
            name, box = self._box(a)
            lst = self.acc.setdefault(name, [])
            for rec in lst:
                if rec[2] and self._ovl(rec[0], box):
                    op.deps.add(rec[1])
            lst.append([box, op.idx, False])
        for a in writes:
            name, box = self._box(a)
            lst = self.acc.setdefault(name, [])
            keep = []
            for rec in lst:
                if rec[1] == op.idx:
                    keep.append(rec)
                    continue
                if self._ovl(rec[0], box):
                    op.deps.add(rec[1])
                    if self._covers(box, rec[0]):
                        continue
                keep.append(rec)
            keep.append([box, op.idx, True])
            self.acc[name] = keep
        op.deps.discard(op.idx)
        if eng == 'pe' and dma is None:
            op.deps = set(d for d in op.deps if not (self.ops[d].eng == 'pe' and self.ops[d].dma_sem is None))
        if dma is not None:
            npool = self.DMA_POOL.get(eng, 4)
            self.dma_rr[eng] = self.dma_rr.get(eng, 0) + 1
            dma = '%s%d' % (eng, self.dma_rr[eng] % npool)
            op.dma_sem = dma
            if dma not in self.dma_count:
                self.dma_count[dma] = 0
                self.dma_keys.append(dma)
            else:
                op.deps.add(self.dma_last[dma])
            self.dma_count[dma] += 1
            op.dma_ord = self.dma_count[dma]
            self.dma_last[dma] = op.idx
        op.pos = len(self.eng_ops[eng]) + 1
        self.eng_ops[eng].append(op)
        self.ops.append(op)
        return op

    def finalize(self):
        ops = self.ops
        K = {e: {} for e in ENGS}
        dma_seen = {k: 0 for k in self.dma_keys}
        for op in ops:
            e = op.eng
            need = {}
            for d in op.deps:
                dop = ops[d]
                if dop.dma_sem is not None:
                    ch = 'dma:' + dop.dma_sem
                    cnt = dop.dma_ord
                else:
                    ch = dop.eng
                    cnt = dop.pos
                if ch not in need or need[ch][0] < cnt:
                    need[ch] = (cnt, dop)
            waits = []
            Ke = K[e]
            for ch, (cnt, dop) in sorted(need.items(), key=lambda kv: -kv[1][1].idx):
                if Ke.get(ch, 0) >= cnt:
                    continue
                waits.append((ch, cnt))
                Ke[ch] = cnt
                for c2, v2 in dop.vc.items():
                    if Ke.get(c2, 0) < v2:
                        Ke[c2] = v2
            op.waits = waits
            vc = dict(Ke)
            if op.dma_sem is not None:
                dma_seen[op.dma_sem] += 1
                vc['dma:' + op.dma_sem] = op.dma_ord
            else:
                vc[e] = op.pos
            op.vc = vc
        for op in ops:
            for ch, cnt in op.waits:
                if not ch.startswith('dma:'):
                    self.eng_ops[ch][cnt - 1].signal = True
        for e in ENGS:
            last = [o for o in self.eng_ops[e] if o.dma_sem is None]
            if last:
                last[-1].signal = True
        self.final_counts = {}
        for e in ENGS:
            c = 0
            for o in self.eng_ops[e]:
                if o.dma_sem is None and o.signal:
                    c += 1
                o.semval = c
            self.final_counts[e] = c

    def emit(self, stack):
        nc = self.nc
        self.finalize()
        sems = {}
        for e in ENGS:
            sems[e] = stack.enter_context(nc.semaphore('s_' + e))
        for k in self.dma_keys:
            sems['dma:' + k] = stack.enter_context(nc.semaphore('d_' + k))
        block = stack.enter_context(nc.Block())
        sched = self

        def run(engname, eng):
            for op in sched.eng_ops[engname]:
                for ch, cnt in op.waits:
                    if ch.startswith('dma:'):
                        eng.wait_ge(sems[ch], 16 * cnt)
                    else:
                        eng.wait_ge(sems[ch], sched.eng_ops[ch][cnt - 1].semval)
                ins = op.fn(eng)
                if op.dma_sem is not None:
                    ins.then_inc(sems['dma:' + op.dma_sem], 16)
                elif op.signal:
                    ins.then_inc(sems[engname], 1)
            if engname == 'sp':
                for k in sched.dma_keys:
                    eng.wait_ge(sems['dma:' + k], 16 * sched.dma_count[k])
                for e2 in ENGS:
                    if e2 != 'sp' and sched.final_counts[e2] > 0:
                        eng.wait_ge(sems[e2], sched.final_counts[e2])

        @block.sync
        def _(eng):
            run('sp', eng)

        @block.scalar
        def _(eng):
            run('act', eng)

        @block.vector
        def _(eng):
            run('dve', eng)

        @block.tensor
        def _(eng):
            run('pe', eng)

        @block.gpsimd
        def _(eng):
            run('pool', eng)


D = 1024
DIN = 10384
NMETA = 16
EPS = 1e-6
GN_EPS = 64 * 1e-5
OFF_A = 3072
NEG = -30000.0
NCMAX = 272
WCOLS = 10368
C_Q, C_K, C_V, C_Z, C_WDAD, C_RW, C_ZB, C_GA, C_GB = 0, 1024, 2048, 3072, 4096, 4224, 7296, 8320, 9344


class Cfg:
    def __init__(self, npb=8, ns=16):
        self.npb = npb
        self.ns = ns
        self.seq = 256 * npb


def build(cfg):
    nc = bass.Bass("TRN2", target_bir_lowering=False)
    st = ExitStack()
    S = Sched(nc)
    NS = cfg.ns
    SEQ = cfg.seq

    def din(name, shape):
        return nc.dram_tensor(name, shape, F32, kind="ExternalInput").ap()

    def dout(name, shape):
        return nc.dram_tensor(name, shape, F32, kind="ExternalOutput").ap()

    xp = din("xp", [SEQ, D])
    xs = din("xs", [NS * 4, D])
    sg_in = din("sg", [NS, 8, 128, 128])
    sc_in = din("sc", [NS * 3, 3072])
    sr_in = din("sr", [NS, 16, 64, 64])
    ss_in = din("ss", [NS, D])
    meta = din("meta", [NMETA, D])
    ln1_w = din("ln1_w", [D])
    w_in = din("w_in", [D, DIN])
    conv_w = din("conv_w", [4, 3072])
    a_log = din("a_log", [8, 1])
    dt_bias = din("dt_bias", [8, 1])
    gnorm_w = din("gnorm_w", [128, 1])
    w_out_a = din("w_out_a", [D, D])
    mu_in = din("mu", [33, 128])
    w0_in = din("w0", [8, 128])
    w2_in = din("w2", [64, D])
    a0_in = din("a0", [8, 128])
    a2_in = din("a2", [64, D])
    kk_in = din("k_k", [8, 128])
    ka_in = din("k_a", [8, 128])
    rk_in = din("r_k", [8, 128])
    gnw_in = din("gn_w", [8, 128])
    gnb_in = din("gn_b", [8, 128])
    w_out_b = din("w_out_b", [D, D])
    w_out = din("w_out", [D, D])
    lnf_w = din("lnf_w", [D])

    yp = dout("yp", [SEQ, D])
    ys = dout("ys", [NS * 4, D])
    ngp = dout("ngp", [8, 128, 128])
    ncp = dout("ncp", [3, 3072])
    nrp = dout("nrp", [16, 64, 64])
    nsp = dout("nsp", [1, D])
    ngs = dout("ngs", [NS, 8, 128, 128])
    ncs = dout("ncs", [NS * 3, 3072])
    nrs = dout("nrs", [NS, 16, 64, 64])
    nss = dout("nss", [NS, D])

    WG = {}
    for _nm, _w in (('q', 1024), ('k', 1024), ('v', 1024), ('z', 1024), ('wdad', 128), ('rw', 3072),
                    ('zb', 1024), ('ga', 1024), ('gb', 1024)):
        WG[_nm] = nc.dram_tensor("Wb_" + _nm, [128, 8, _w], BF16, kind="Internal").ap()
    Wab = nc.dram_tensor("Wab", [128, 8, 16], BF16, kind="Internal").ap()
    Wo = [nc.dram_tensor("Wo%d" % _i, [128, 8, D], BF16, kind="Internal").ap() for _i in range(3)]

    def sb(name, shape, dt=F32):
        return st.enter_context(nc.sbuf_tensor("s_" + name, shape, dt))

    def psum(name, shape, dt=F32):
        return st.enter_context(nc.psum_tensor("p_" + name, shape, dt))

    def rw(aps):
        return [a for a in aps if a is not None and not isinstance(a, (int, float))]

    def mm(out, lhsT, rhs, start=True, stop=True):
        S.add('pe', lambda e: e.matmul(out, lhsT=lhsT, rhs=rhs, start=start, stop=stop),
              reads=[lhsT, rhs], writes=[out])

    def tr(out, in_, idn):
        S.add('pe', lambda e: e.transpose(out=out, in_=in_, identity=idn), reads=[in_, idn], writes=[out])

    def act(out, in_, func, bias=None, scale=None, accum=None):
        kw = {}
        if bias is not None:
            kw['bias'] = bias
        if scale is not None:
            kw['scale'] = scale
        if accum is not None:
            kw['accum_out'] = accum
        S.add('act', lambda e: e.activation(out=out, in_=in_, func=func, **kw),
              reads=rw([in_, bias, scale]), writes=rw([out, accum]))

    def tt(eng, out, in0, in1, op):
        S.add(eng, lambda e: e.tensor_tensor(out=out, in0=in0, in1=in1, op=op), reads=[in0, in1], writes=[out])

    def ts(eng, out, in0, s1, s2=None, op0=ALU.mult, op1=None):
        kw = {}
        if op1 is not None:
            kw['op1'] = op1
        S.add(eng, lambda e: e.tensor_scalar(out=out, in0=in0, scalar1=s1, scalar2=s2, op0=op0, **kw),
              reads=rw([in0, s1, s2]), writes=[out])

    def stt(eng, out, in0, scalar, in1, op0, op1):
        S.add(eng, lambda e: e.scalar_tensor_tensor(out=out, in0=in0, scalar=scalar, in1=in1, op0=op0, op1=op1),
              reads=rw([in0, scalar, in1]), writes=[out])

    def cp(eng, out, in_):
        if eng == 'act':
            S.add('act', lambda e: e.copy(out=out, in_=in_), reads=[in_], writes=[out])
        else:
            S.add(eng, lambda e: e.tensor_copy(out=out, in_=in_), reads=[in_], writes=[out])

    def memset(eng, out, val):
        S.add(eng, lambda e: e.memset(out, val), writes=[out])

    def dma(eng, out, in_, key):
        S.add(eng, lambda e: e.dma_start(out=out, in_=in_), reads=[in_], writes=[out], dma=key)

    def recip(out, in_):
        S.add('dve', lambda e: e.reciprocal(out=out, in_=in_), reads=[in_], writes=[out])

    def rsum(out, in_):
        S.add('dve', lambda e: e.reduce_sum(out=out, in_=in_, axis=AX.X), reads=[in_], writes=[out])

    def scan(out, d0, d1):
        S.add('dve', lambda e: e.tensor_tensor_scan(out=out, data0=d0, data1=d1, initial=0.0,
                                                    op0=ALU.mult, op1=ALU.add), reads=[d0, d1], writes=[out])

    def asel(out, in_, pattern, cmp, fill, base, cm):
        S.add('pool', lambda e: e.affine_select(out=out, in_=in_, pattern=pattern, compare_op=cmp, fill=fill,
                                                base=base, channel_multiplier=cm), reads=[in_], writes=[out])

    def sigmoid_to(out, in_, tmp, bias=None, scale=1.0):
        if bias is not None:
            act(tmp, in_, AF.Exp, bias=bias, scale=-scale)
        else:
            act(tmp, in_, AF.Exp, scale=-scale)
        act(tmp, tmp, AF.Ln, bias=1.0)
        act(out, tmp, AF.Exp, scale=-1.0)

    def run_pipelined(gens, depth):
        active = []
        it = iter(gens)
        done = False
        while True:
            while len(active) < depth and not done:
                try:
                    active.append(next(it))
                except StopIteration:
                    done = True
            if not active:
                break
            for g in list(active):
                try:
                    next(g)
                except StopIteration:
                    active.remove(g)

    def rsqrt_to(out, in_, eps, mult=1.0):
        act(out, in_, AF.Ln, bias=eps, scale=mult)
        act(out, out, AF.Exp, scale=-0.5)

    PP = psum("PP", [128, 2, 512])
    PB = [psum("PB%d" % i, [128, 1024], BF16) for i in range(2)]
    PG = psum("PG", [128, 4, 512])
    ctr = {'pp': 0, 'pb': 0, 'pg': 0}

    def ppbank():
        i = ctr['pp'] % 2
        ctr['pp'] += 1
        return PP[:, i, :]

    def pbbank():
        i = ctr['pb'] % 2
        ctr['pb'] += 1
        return PB[i]

    def pgbank():
        i = ctr['pg'] % 4
        ctr['pg'] += 1
        return PG[:, i, :]

    ident = sb("ident", [128, 128])
    identb = sb("identb", [128, 128], BF16)
    ones = sb("ones", [128, 128])
    bones = sb("bones", [128, 128])
    nmS = sb("nmS", [64, 64])
    nmTI = sb("nmTI", [64, 64])
    m_st = sb("m_st", [64, 64])
    m_stT = sb("m_stT", [64, 64])
    m_inT = sb("m_inT", [64, 64])
    Esel = sb("Esel", [8, 8, 64])
    memset('pool', ident[:], 0.0)
    asel(ident[:], ident[:], [[-1, 128]], ALU.not_equal, 1.0, 0, 1)
    cp('pool', identb[:], ident[:])
    memset('pool', ones[:], 1.0)
    memset('pool', bones[:], 0.0)
    memset('pool', bones[0:64, 0:64], 1.0)
    memset('pool', bones[64:128, 64:128], 1.0)
    memset('pool', nmS[:], 0.0)
    asel(nmS[:], nmS[:], [[-1, 64]], ALU.is_gt, NEG, 0, 1)
    memset('pool', nmTI[:], 0.0)
    asel(nmTI[:], nmTI[:], [[1, 64]], ALU.is_ge, NEG, 0, -1)
    memset('pool', m_st[:], 1.0)
    asel(m_st[:], m_st[:], [[-1, 64]], ALU.is_gt, 0.0, 0, 1)
    memset('pool', m_stT[:], 1.0)
    asel(m_stT[:], m_stT[:], [[1, 64]], ALU.is_gt, 0.0, 0, -1)
    memset('pool', m_inT[:], 1.0)
    asel(m_inT[:], m_inT[:], [[1, 64]], ALU.is_ge, 0.0, 0, -1)
    memset('pool', Esel[:], 0.0)
    asel(Esel[:], Esel[:], [[-1, 8], [0, 64]], ALU.not_equal, 1.0, 0, 1)

    hrow = sb("hrow", [128, D])
    xrow = sb("xrow", [128, D])
    rstat = sb("rstat", [128, 4])
    memset('pool', hrow[:], 0.0)

    pstage = xrow[0:33, 0:128]

    def load_cols(name, src, ntile):
        t = sb(name, [128, ntile])
        dma('sp', pstage[0:ntile, :], src, 'c')
        pt = pgbank()
        tr(pt[:, 0:ntile], pstage[0:ntile, :], ident[0:ntile, 0:ntile])
        cp('act', t[:], pt[:, 0:ntile])
        return t

    mu = load_cols("mu", mu_in, 33)
    w0c = load_cols("w0c", w0_in, 8)
    a0c = load_cols("a0c", a0_in, 8)
    kkc = load_cols("kkc", kk_in, 8)
    kac = load_cols("kac", ka_in, 8)
    rkc = load_cols("rkc", rk_in, 8)
    gnwc = load_cols("gnwc", gnw_in, 8)
    gnbc = load_cols("gnbc", gnb_in, 8)
    nw0c = sb("nw0c", [128, 8])
    na0c = sb("na0c", [128, 8])
    ts('dve', nw0c[:], w0c[:], -1.0)
    ts('dve', na0c[:], a0c[:], -1.0)
    omka = sb("omka", [128, 8])
    ts('dve', omka[:], kac[:], -1.0, 1.0, op0=ALU.mult, op1=ALU.add)
    convw = sb("convw", [128, 24, 4])
    for q in range(3):
        dma('sp', xrow[0:4, :], conv_w[:, q * 1024:(q + 1) * 1024], 'c')
        for j in range(8):
            ft = q * 8 + j
            pt = pgbank()
            tr(pt[:, 0:4], xrow[0:4, j * 128:(j + 1) * 128], ident[0:4, 0:4])
            cp('act', convw[:, ft, :], pt[:, 0:4])
    ln1bc = sb("ln1bc", [128, D])
    lnfbc = sb("lnfbc", [128, D])
    dma('sp', ln1bc[:], ln1_w.partition_broadcast(128), 'c')
    dma('sp', lnfbc[:], lnf_w.partition_broadcast(128), 'c')
    alog = sb("alog", [8, 1])
    dtb = sb("dtb", [8, 1])
    nA = sb("nA", [8, 1])
    dma('sp', alog[:], a_log, 'c')
    dma('sp', dtb[:], dt_bias, 'c')
    act(nA[:], alog[:], AF.Exp)
    ts('dve', nA[:], nA[:], -1.0)
    gnw = sb("gnw", [128, 1])
    dma('sp', gnw[:], gnorm_w, 'c')
    wa2 = sb("wa2", [128, D])
    dma('sp', wa2[0:64, :], w2_in, 'c')
    dma('sp', wa2[64:128, :], a2_in, 'c')

    import os as _os2
    _KS = float(_os2.environ.get('KSTOP', '999'))
    w_v = w_in.rearrange("(dh dl) c -> dl dh c", dl=128)
    if _KS >= 1:
        for dh in range(8):
            dma('pool', Wab[:, dh, :], w_v[:, dh, OFF_A:OFF_A + 16], 'wcast')
        for nm, c0 in (('q', 0), ('k', 1024), ('v', 2048), ('z', 3088), ('wdad', 7184)):
            wd_ = WG[nm].shape[2]
            for dh in range(8):
                dma('pool', WG[nm][:, dh, :], w_v[:, dh, c0:c0 + wd_], 'wcast')
        for dh in range(8):
            for j in range(3):
                dma('pool', WG['rw'][:, dh, :].rearrange("d (p j f) -> d p j f", p=8, j=3)[:, :, j, :],
                    w_v[:, dh, 4112 + j * 1024:4112 + (j + 1) * 1024].rearrange("d (p f) -> d p f", p=8), 'wcast')
        for nm, c0 in (('zb', 7312), ('ga', 8336)):
            for dh in range(8):
                dma('pool', WG[nm][:, dh, :], w_v[:, dh, c0:c0 + 1024], 'wcast')
        for dh in range(8):
            dma('pool', Wo[0][:, dh, :], w_out_a.rearrange("(dh dl) c -> dl dh c", dl=128)[:, dh, :], 'wcast')
        for dh in range(8):
            dma('pool', WG['gb'][:, dh, :], w_v[:, dh, 9360:9360 + 1024], 'wcast')
        for i, wsrc in ((1, w_out_b), (2, w_out)):
            wv = wsrc.rearrange("(dh dl) c -> dl dh c", dl=128)
            for dh in range(8):
                dma('pool', Wo[i][:, dh, :], wv[:, dh, :], 'wcast')
    wab = sb("wab", [128, 8, 16], BF16)
    dma('sp', wab[:], Wab, 'c')

    NWB = 2
    wbuf = [sb("wbuf%d" % i, [128, 8, 512], BF16) for i in range(NWB)]
    wctr = [0]

    def wload(src):
        b = wbuf[wctr[0] % NWB]
        key = 'w%d' % (wctr[0] % NWB)
        wctr[0] += 1
        dma('sp', b[:, :, 0:src.shape[2]], src, key)
        return b

    hT = sb("hT", [128, 8, NCMAX], BF16)
    BT = [sb("BT%d" % i, [128, 8, NCMAX], BF16) for i in range(10)]
    oaT = sb("oaT", [128, 8, NCMAX], BF16)
    obT = BT[0]
    bg2 = sb("bg2", [128, 16, NCMAX], BF16)
    rg2 = sb("rg2", [128, 16, NCMAX], BF16)
    memset('pool', bg2[:], 0.0)
    memset('pool', rg2[:], 0.0)
    mgT = BT[9]
    MG = BT[7]

    Sg = sb("Sg", [128, 8, 128])
    Sgb = sb("Sgb", [128, 8, 128], BF16)
    Mr = sb("Mr", [128, 8, 64])
    Mrb = sb("Mrb", [128, 8, 64], BF16)
    NSQ = max(NS, 1)
    hist = sb("hist", [128, 24, 3 * NSQ])
    pbh = sb("pbh", [128, 33])
    scanm = sb("scanm", [128, NCMAX])

    cin = sb("cin", [128, NCMAX + 4])
    T = [sb("T%d" % i, [128, NCMAX + 4]) for i in range(15)]
    TX = [sb("TX%d" % i, [128, NCMAX + 4]) for i in range(14)]
    g8 = {n: sb("g8" + n, [8, NCMAX]) for n in ('g', 'gc', 'ngc', 'egc', 'beta', 'beg', 'etail')}
    NCHMAX = max(5, NS)
    eglD = sb("eglD", [8, NCHMAX, 8])
    eglbc = sb("eglbc", [128, NCHMAX, 8])
    eglR = sb("eglR", [128, 8, NCHMAX])

    def ctile(name, shape, dt=F32):
        return sb(name, [64] + shape, dt)
    sc24 = ctile("sc24", [24])
    kbg = ctile("kbg", [8, 128], BF16)
    ktl = ctile("ktl", [8, 128], BF16)
    vbt = ctile("vbt", [8, 128], BF16)
    Vtm, atl, ktlR = kbg, ktl, vbt
    tmpA = ctile("tmpA", [8, 64])
    DmS = ctile("DmS", [8, 64])
    DmT = ctile("DmT", [8, 64])
    RHS = ctile("RHS", [8, 64], BF16)
    Nm = [ctile("Nm%d" % i, [8, 64], BF16) for i in range(2)]
    NTm = [ctile("NTm%d" % i, [8, 64], BF16) for i in range(2)]
    RT = [ctile("RT%d" % i, [8, 64], BF16) for i in range(2)]
    AqkT = ctile("AqkT", [8, 64], BF16)
    AakT = ctile("AakT", [8, 64], BF16)
    AqbT = ctile("AqbT", [8, 64], BF16)
    nwT = sb("nwT", [128, 8, 64], BF16)
    vnew = ctile("vnew", [8, 128], BF16)
    osq = ctile("osq", [4, 128])
    ysq = osq
    ost = ctile("ost", [8, 4])
    yst = ost
    onb = ctile("onb", [8, 128], BF16)
    ynb = onb
    Urw = ctile("Urw", [8, 64], BF16)
    Nm2 = [ctile("Nm2_%d" % i, [8, 64], BF16) for i in range(2)]
    NTm2 = [ctile("NTm2_%d" % i, [8, 64], BF16) for i in range(2)]
    RT2 = [ctile("RT2_%d" % i, [8, 64], BF16) for i in range(2)]
    RHS2 = ctile("RHS2", [8, 64], BF16)
    Urw2 = ctile("Urw2", [8, 64], BF16)
    RSET = [dict(Nm=Nm, NTm=NTm, RT=RT, RHS=RHS, Urw=Urw, AakT=AakT, AqbT=AqbT, AqkT=AqkT),
            dict(Nm=Nm2, NTm=NTm2, RT=RT2, RHS=RHS2, Urw=Urw2, AakT=vnew[:, :, 0:64], AqbT=vnew[:, :, 64:128],
                 AqkT=nwT[0:64, :, :])]
    strw = xrow[0:64, :].rearrange("p (h k) -> p h k", k=64)
    cvt = xrow[0:64, 0:768]

    class Blk:
        pass

    blocks = []
    for b in range(cfg.npb):
        B = Blk()
        B.kind = 'p'
        B.idx = b
        B.ntok = 256 + (NMETA if b == 0 else 0)
        B.ncols = B.ntok
        B.nseg = 1
        B.L = B.ntok
        B.first = (b == 0)
        B.last = (b == cfg.npb - 1)
        chunks = []
        c = 0
        if b == 0:
            chunks.append((0, NMETA))
            c = NMETA
        while c < B.ntok:
            chunks.append((c, 64))
            c += 64
        B.segs = [dict(sid=0, chunks=chunks)]
        B.chunks = chunks
        B.groups = ([(0, 1, NMETA, 0), (NMETA, 4, 64, 1)] if b == 0 else [(0, 4, 64, 0)])
        tiles = []
        r = 0
        while r < B.ntok:
            n = min(128, B.ntok - r)
            tiles.append((r, n))
            r += n
        B.tiles = tiles
        blocks.append(B)
    if NS > 0:
        B = Blk()
        B.kind = 's'
        B.idx = 0
        B.ntok = 4 * NS
        B.ncols = 4 * NS + NS
        B.nseg = NS
        B.L = 4
        B.first = True
        B.last = True
        B.segs = [dict(sid=s, chunks=[(4 * s, 4)]) for s in range(NS)]
        B.chunks = [(4 * s, 4) for s in range(NS)]
        B.groups = [(0, NS, 4, 0)]
        B.tiles = [(0, B.ntok)]
        blocks.append(B)

    def load_x_tile(B, r0, n, dst):
        if B.kind == 'p':
            if B.idx == 0 and r0 == 0:
                dma('sp', dst[0:NMETA, :], meta, 'x')
                dma('sp', dst[NMETA:n, :], xp[0:n - NMETA, :], 'x')
            else:
                s0 = r0 + 256 * B.idx - (NMETA if B.idx == 0 else 0)
                dma('sp', dst[0:n, :], xp[s0:s0 + n, :], 'x')
        else:
            dma('sp', dst[0:n, :], xs[0:n, :], 'x')

    def rms_rows(src, n, dst, wbc):
        act(dst[0:n, :], src[0:n, :], AF.Square, accum=rstat[0:n, 0:1])
        rsqrt_to(rstat[0:n, 3:4], rstat[0:n, 0:1], EPS, 1.0 / D)
        stt('dve', dst[0:n, :], src[0:n, :], rstat[0:n, 3:4], wbc[0:n, :], ALU.mult, ALU.mult)

    def front(B):
        for ti, (r0, n) in enumerate(B.tiles):
            load_x_tile(B, r0, n, xrow)
            rms_rows(xrow, n, hrow, ln1bc)
            if B.kind == 's':
                dma('sp', hrow[64:64 + NS, :], ss_in, 'x')
                for s in range(NS):
                    dma('sp', nss[s:s + 1, :], hrow[4 * s + 3:4 * s + 4, :], 'out')
            elif B.last and ti == len(B.tiles) - 1:
                dma('sp', nsp, hrow[n - 1:n, :], 'out')
            for dc in range(8):
                pt = ppbank()
                if B.kind == 's':
                    tr(pt[:, 0:64 + NS], hrow[0:64 + NS, dc * 128:(dc + 1) * 128], ident[0:64 + NS, 0:64 + NS])
                    cp('act', hT[:, dc, 0:B.ntok], pt[:, 0:B.ntok])
                    cp('act', hT[:, dc, B.ntok:B.ntok + NS], pt[:, 64:64 + NS])
                else:
                    tr(pt[:, 0:n], hrow[0:n, dc * 128:(dc + 1) * 128], ident[0:n, 0:n])
                    cp('act' if dc % 2 == 0 else 'dve', hT[:, dc, r0:r0 + n], pt[:, 0:n])

    def proj(wt, ti, ncols, M=128, wcol0=None):
        ps = ppbank()
        c0 = ti * 128 if wcol0 is None else wcol0
        for dc in range(8):
            mm(ps[0:M, 0:ncols], wt[:, dc, c0:c0 + M], hT[:, dc, 0:ncols], start=(dc == 0), stop=(dc == 7))
        return ps

    def tokview(B, ap2d):
        return ap2d.rearrange("p (s l) -> p s l", l=B.L)

    CHK = [None]

    def gdn_prep(B):
        nt = B.ntok
        qT, kT, qgT, vT, zsT = BT[0], BT[1], BT[2], BT[3], BT[4]
        memset('pool', scanm[:, 0:nt], 1.0)
        for (c0, C) in B.chunks:
            memset('pool', scanm[:, c0:c0 + 1], 0.0)
        psa = proj(wab, 0, nt, M=8, wcol0=0)
        act(T[0][0:8, 0:nt], psa[0:8, 0:nt], AF.Exp, bias=dtb[:])
        act(T[0][0:8, 0:nt], T[0][0:8, 0:nt], AF.Ln, bias=1.0)
        ts('dve', g8['g'][:, 0:nt], T[0][0:8, 0:nt], nA[:])
        psb = proj(wab, 0, nt, M=8, wcol0=8)
        sigmoid_to(g8['beta'][:, 0:nt], psb[0:8, 0:nt], T[0][0:8, 0:nt])
        scan(g8['gc'][:, 0:nt], scanm[0:8, 0:nt], g8['g'][:, 0:nt])
        ts('dve', g8['ngc'][:, 0:nt], g8['gc'][:, 0:nt], -1.0)
        act(g8['egc'][:, 0:nt], g8['gc'][:, 0:nt], AF.Exp)
        tt('dve', g8['beg'][:, 0:nt], g8['beta'][:, 0:nt], g8['egc'][:, 0:nt], ALU.mult)
        for ci, (c0, C) in enumerate(B.chunks):
            last = c0 + C - 1
            ts('dve', g8['etail'][:, c0:c0 + C], g8['ngc'][:, c0:c0 + C], g8['gc'][:, last:last + 1], op0=ALU.add)
            ts('dve', eglD[:, ci, :], ident[0:8, 0:8], g8['egc'][:, last:last + 1])
        act(g8['etail'][:, 0:nt], g8['etail'][:, 0:nt], AF.Exp)
        nch = len(B.chunks)
        pe_ = pgbank()
        mm(pe_[:, 0:nch * 8], ones[0:8, :], eglD[:, 0:nch, :].rearrange("p c h -> p (c h)"))
        cp('act', eglbc[:, 0:nch, :].rearrange("p c h -> p (c h)"), pe_[:, 0:nch * 8])
        wth = {}

        def qkv_tile(gi, gname, ti):
            ft = gi * 8 + ti
            if ti % 4 == 0:
                wth[(gi, ti // 4)] = wload(WG[gname][:, :, ti * 128:ti * 128 + 512])
            wt = wth[(gi, ti // 4)]
            ps = proj(wt, ti % 4, nt)
            if ft % 2 == 0:
                cinx, A0, A1, A2, A3 = cin, T[0], T[1], T[2], T[3]
            else:
                cinx, A0, A1, A2, A3 = T[4], T[5], T[6], T[7], T[8]
            cv = cinx[:, 0:B.nseg * (B.L + 3)].rearrange("p (s l) -> p s l", l=B.L + 3)
            cp('act', cv[:, :, 3:3 + B.L], tokview(B, ps[:, 0:nt]))
            hv = hist[:, ft, 0:3 * B.nseg].rearrange("p (s l) -> p s l", l=3)
            if B.first and B.kind == 'p':
                memset('pool', cv[:, :, 0:3], 0.0)
            else:
                cp('pool', cv[:, :, 0:3], hv)
            yield
            acc = tokview(B, A0[:, 0:nt])
            ts('dve', acc, cv[:, :, 0:B.L], convw[:, ft, 0:1])
            for i in range(1, 4):
                stt('dve', acc, cv[:, :, i:i + B.L], convw[:, ft, i:i + 1], acc, ALU.mult, ALU.add)
            cp('pool', hv, cv[:, :, B.L:B.L + 3])
            yield
            sigmoid_to(A1[:, 0:nt], A0[:, 0:nt], A1[:, 0:nt])
            yield
            if gname == 'v':
                tt('dve', vT[:, ti, 0:nt], A0[:, 0:nt], A1[:, 0:nt], ALU.mult)
                return
            tt('dve', A1[:, 0:nt], A0[:, 0:nt], A1[:, 0:nt], ALU.mult)
            act(A2[:, 0:nt], A1[:, 0:nt], AF.Square)
            yield
            pn = pgbank()
            mm(pn[:, 0:nt], ones[:], A2[:, 0:nt])
            act(A3[:, 0:nt], pn[:, 0:nt], AF.Ln, bias=1e-6)
            act(A3[:, 0:nt], A3[:, 0:nt], AF.Exp, scale=-0.5)
            yield
            if gname == 'k':
                tt('dve', kT[:, ti, 0:nt], A1[:, 0:nt], A3[:, 0:nt], ALU.mult)
            else:
                stt('dve', qT[:, ti, 0:nt], A1[:, 0:nt], 128.0 ** -0.5, A3[:, 0:nt], ALU.mult, ALU.mult)
                ts('dve', A3[0:8, 0:nt], g8['egc'][:, 0:nt], ident[0:8, ti:ti + 1])
                yield
                pq = pgbank()
                mm(pq[:, 0:nt], ones[0:8, :], A3[0:8, 0:nt])
                tt('dve', qgT[:, ti, 0:nt], qT[:, ti, 0:nt], pq[:, 0:nt], ALU.mult)

        run_pipelined((qkv_tile(gi, gname, ti) for gi, gname in enumerate(('q', 'k', 'v')) for ti in range(8)), 2)
        def z_tile(ti):
            if ti % 4 == 0:
                wth[('z', ti // 4)] = wload(WG['z'][:, :, ti * 128:ti * 128 + 512])
            ps = proj(wth[('z', ti // 4)], ti % 4, nt)
            zt = T[ti % 2]
            yield
            sigmoid_to(zt[:, 0:nt], ps[:, 0:nt], zt[:, 0:nt])
            yield
            tt('dve', zsT[:, ti, 0:nt], ps[:, 0:nt], zt[:, 0:nt], ALU.mult)

        run_pipelined((z_tile(ti) for ti in range(8)), 2)

    def tri_inverse_gen(N0, NT0, C, res, Nm_=None, NTm_=None, RT_=None, nh=8):
        Nm_ = Nm if Nm_ is None else Nm_
        NTm_ = NTm if NTm_ is None else NTm_
        RT_ = RT if RT_ is None else RT_
        cur = 0
        Icb = ident[0:C, 0:C].unsqueeze(1).to_broadcast([C, nh, C])
        tt('dve', RT_[0][0:C, 0:nh, 0:C], NT0, Icb, ALU.add)
        P, PT = N0, NT0
        nlev = 0
        while (1 << (nlev + 1)) < C:
            nlev += 1
        for m in range(1, nlev + 1):
            pP = pgbank()
            pPv = pP[0:C, 0:nh * C].rearrange("p (h c) -> p h c", c=C)
            for h in range(nh):
                mm(pPv[:, h, :], PT[:, h, :], P[:, h, :])
            newP = Nm_[m % 2][0:C, 0:nh, 0:C]
            cp('act', newP, pPv)
            newPT = None
            if m < nlev:
                pT = pgbank()
                pTv = pT[0:C, 0:nh * C].rearrange("p (h c) -> p h c", c=C)
                for h in range(nh):
                    mm(pTv[:, h, :], P[:, h, :], PT[:, h, :])
                newPT = NTm_[m % 2][0:C, 0:nh, 0:C]
                cp('act', newPT, pTv)
            yield
            pR = pgbank()
            pRv = pR[0:C, 0:nh * C].rearrange("p (h c) -> p h c", c=C)
            old = RT_[cur][0:C, 0:nh, 0:C]
            for h in range(nh):
                mm(pRv[:, h, :], newP[:, h, :], old[:, h, :])
            cur ^= 1
            tt('dve', RT_[cur][0:C, 0:nh, 0:C], pRv, old, ALU.add)
            P = newP
            PT = newPT
            yield
        res['RT'] = RT_[cur][0:C, 0:nh, 0:C]

    def tri_inverse(N0, NT0, C):
        res = {}
        for _ in tri_inverse_gen(N0, NT0, C, res):
            pass
        return res['RT']

    def gdn_chunk(B, ci, c0, C):
        qT, kT, qgT, vT, zsT = BT[0], BT[1], BT[2], BT[3], BT[4]
        cs = slice(c0, c0 + C)
        p1 = pgbank()
        for i, nm in enumerate(('beta', 'beg', 'etail')):
            tr(p1[0:C, 8 * i:8 * i + 8], g8[nm][:, cs], ident[0:8, 0:8])
        cp('act', sc24[0:C, :], p1[0:C, 0:24])

        def scb(i, w):
            return sc24[0:C, 8 * i:8 * i + 8].unsqueeze(2).to_broadcast([C, 8, w])

        def hv3(ap2, w):
            return ap2.rearrange("p (h c) -> p h c", c=w)
        pk = pbbank()
        pkv = hv3(pk[0:C, :], 128)
        for h in range(8):
            tr(pkv[:, h, :], kT[:, h, cs], identb[:])
        tt('dve', kbg[0:C], pkv, scb(1, 128), ALU.mult)
        tt('dve', ktl[0:C], pkv, scb(2, 128), ALU.mult)
        pv = pbbank()
        pvv = hv3(pv[0:C, :], 128)
        for h in range(8):
            tr(pvv[:, h, :], vT[:, h, cs], identb[:])
        tt('dve', vbt[0:C], pvv, scb(0, 128), ALU.mult)
        pd = pgbank()
        pdv = hv3(pd[0:C, 0:8 * C], C)
        for h in range(8):
            mm(pdv[:, h, :], g8['gc'][:, cs], Esel[:, h, 0:C], start=True, stop=False)
            mm(pdv[:, h, :], Esel[:, h, 0:C], g8['ngc'][:, cs], start=False, stop=True)
        tt('dve', DmS[0:C, :, 0:C], pdv, nmS[0:C, 0:C].unsqueeze(1).to_broadcast([C, 8, C]), ALU.add)
        act(DmS[0:C, :, 0:C], DmS[0:C, :, 0:C], AF.Exp)
        pdT = pgbank()
        pdTv = hv3(pdT[0:C, 0:8 * C], C)
        for h in range(8):
            mm(pdTv[:, h, :], Esel[:, h, 0:C], g8['gc'][:, cs], start=True, stop=False)
            mm(pdTv[:, h, :], g8['ngc'][:, cs], Esel[:, h, 0:C], start=False, stop=True)
        tt('dve', DmT[0:C, :, 0:C], pdTv, nmTI[0:C, 0:C].unsqueeze(1).to_broadcast([C, 8, C]), ALU.add)
        act(DmT[0:C, :, 0:C], DmT[0:C, :, 0:C], AF.Exp)
        pg_ = pgbank()
        pgv = hv3(pg_[0:C, 0:8 * C], C)
        for h in range(8):
            mm(pgv[:, h, :], kT[:, h, cs], kT[:, h, cs])
        stt('dve', tmpA[0:C, :, 0:C], pgv, -1.0, DmS[0:C, :, 0:C], ALU.mult, ALU.mult)
        N0 = Nm[0][0:C, :, 0:C]
        tt('dve', N0, tmpA[0:C, :, 0:C], scb(0, C), ALU.mult)
        pq = pgbank()
        pqv = hv3(pq[0:C, 0:8 * C], C)
        for h in range(8):
            mm(pqv[:, h, :], kT[:, h, cs], qT[:, h, cs])
        tt('dve', AqkT[0:C, :, 0:C], pqv, DmT[0:C, :, 0:C], ALU.mult)
        pn = pbbank()
        pnv = hv3(pn[0:C, 0:8 * C], C)
        for h in range(8):
            tr(pnv[:, h, :], N0[:, h, :], identb[0:C, 0:C])
        NT0 = NTm[0][0:C, :, 0:C]
        cp('act', NT0, pnv)
        RTf = tri_inverse(N0, NT0, C)
        pw = pgbank()
        pwv = hv3(pw[:, 0:8 * C], C)
        for h in range(8):
            mm(pwv[:, h, :], kbg[0:C, h, :], RTf[:, h, :])
        ts('dve', nwT[:, :, 0:C], pwv, -1.0)
        for half in range(2):
            p2 = pgbank()
            p2v = hv3(p2[0:C, :], 128)
            for hh in range(4):
                h = half * 4 + hh
                mm(p2v[:, hh, :], RTf[:, h, :], vbt[0:C, h, :], start=True, stop=False)
                mm(p2v[:, hh, :], nwT[:, h, 0:C], Sgb[:, h, :], start=False, stop=True)
            cp('act', vnew[0:C, half * 4:half * 4 + 4, :], p2v)
        for half in range(2):
            p3 = pgbank()
            p3v = hv3(p3[0:C, :], 128)
            for hh in range(4):
                h = half * 4 + hh
                mm(p3v[:, hh, :], qgT[:, h, cs], Sgb[:, h, :], start=True, stop=False)
                mm(p3v[:, hh, :], AqkT[0:C, h, 0:C], vnew[0:C, h, :], start=False, stop=True)
            hs = slice(half * 4, half * 4 + 4)
            act(osq[0:C], p3v, AF.Square)
            rsum(ost[0:C, hs, 0:1], osq[0:C])
            rsqrt_to(ost[0:C, hs, 3:4], ost[0:C, hs, 0:1], EPS, 1.0 / 128)
            tt('dve', onb[0:C, hs, :], p3v, ost[0:C, hs, 3:4].to_broadcast([C, 4, 128]), ALU.mult)
        po = pbbank()
        pov = hv3(po[:, 0:8 * C], C)
        for h in range(8):
            tr(pov[:, h, :], onb[0:C, h, :], identb[0:C, 0:C])
        stt('dve', oaT[:, :, cs], pov, gnw[:, 0:1], zsT[:, :, cs], ALU.mult, ALU.mult)
        for half in range(2):
            p4 = pgbank()
            p4v = hv3(p4[:, :], 128)
            for hh in range(4):
                h = half * 4 + hh
                mm(p4v[:, hh, :], ktl[0:C, h, :], vnew[0:C, h, :])
            for hh in range(4):
                h = half * 4 + hh
                stt('dve', Sg[:, h, :], Sg[:, h, :], eglbc[:, ci, h:h + 1], p4v[:, hh, :], ALU.mult, ALU.add)
            cp('act', Sgb[:, half * 4:half * 4 + 4, :], Sg[:, half * 4:half * 4 + 4, :])

    def rwkv_prep(B):
        nt = B.ntok
        ncols = B.ncols
        bgT, agT, kgT, rgT, atT, ktT, rvT, bonT, zbT = BT[0], BT[1], BT[2], BT[3], BT[4], BT[5], BT[6], BT[7], BT[8]

        def mixed(ps, fidx, dst, rawbuf=cin):
            raw = rawbuf[:, 0:B.nseg * (B.L + 1)].rearrange("p (s l) -> p s l", l=B.L + 1)
            cp('act', raw[:, :, 1:1 + B.L], tokview(B, ps[:, 0:nt]))
            if B.kind == 'p':
                if B.first:
                    memset('pool', raw[:, :, 0:1], 0.0)
                else:
                    cp('pool', raw[:, 0, 0:1], pbh[:, fidx:fidx + 1])
                cp('pool', pbh[:, fidx:fidx + 1], raw[:, 0, B.L:B.L + 1])
            else:
                cp('act', raw[:, :, 0:1], ps[:, nt:nt + B.nseg].unsqueeze(2))
            dv = tokview(B, dst)
            tt('dve', dv, raw[:, :, 0:B.L], raw[:, :, 1:1 + B.L], ALU.subtract)
            stt('dve', dv, dv, mu[:, fidx:fidx + 1], raw[:, :, 1:1 + B.L], ALU.mult, ALU.add)

        wt = wload(WG['wdad'])
        ps = proj(wt, 0, ncols)
        mixed(ps, 24, T[0][:, 0:nt])
        sigmoid_to(T[1][0:64, 0:nt], T[0][0:64, 0:nt], T[1][0:64, 0:nt], scale=2.0)
        ts('dve', T[1][0:64, 0:nt], T[1][0:64, 0:nt], 2.0, -1.0, op0=ALU.mult, op1=ALU.add)
        def pair_gen(p):
            if p % 2 == 0:
                rawb = cin
                TT = {i: T[i] for i in range(2, 15)}
            else:
                rawb = TX[0]
                TT = {i: TX[i - 1] for i in range(2, 15)}
            Tr, Tk, Tv = TT[2], TT[3], TT[4]
            wt = wload(WG['rw'][:, :, p * 384:(p + 1) * 384])
            for j, dst in enumerate((Tr, Tk, Tv)):
                ps = proj(wt, j, ncols)
                mixed(ps, 8 * j + p, dst[:, 0:nt], rawb)
                yield
            r_p, k_p, v_p = Tr[:, 0:nt], Tk[:, 0:nt], Tv[:, 0:nt]
            wlog, gc, eg, eng, egp = (TT[i][:, 0:nt] for i in (5, 6, 7, 8, 9))
            a_, kk, rn, kb2, kg32 = (TT[i][:, 0:nt] for i in (10, 11, 12, 13, 14))
            psw = pgbank()
            mm(psw[:, 0:nt], wa2[0:64, p * 128:(p + 1) * 128], T[1][0:64, 0:nt])
            sigmoid_to(wlog, psw[:, 0:nt], wlog, bias=nw0c[:, p:p + 1])
            psa = pgbank()
            mm(psa[:, 0:nt], wa2[64:128, p * 128:(p + 1) * 128], T[0][64:128, 0:nt])
            sigmoid_to(a_, psa[:, 0:nt], a_, bias=na0c[:, p:p + 1])
            act(kk, k_p, AF.Copy, scale=kkc[:, p:p + 1])
            act(rn, kk, AF.Square)
            yield
            scan(gc, scanm[:, 0:nt], wlog)
            pss = pgbank()
            mm(pss[:, 0:nt], bones[:], rn)
            act(rn, pss[:, 0:nt], AF.Ln, bias=1e-6)
            act(rn, rn, AF.Exp, scale=-0.5)
            yield
            WSC = float(np.exp(-0.5))
            act(eg, gc, AF.Exp, scale=-WSC)
            act(eng, gc, AF.Exp, scale=WSC)
            tt('pool', egp, gc, wlog, ALU.subtract)
            act(egp, egp, AF.Exp, scale=-WSC)
            tt('dve', kk, kk, rn, ALU.mult)
            ts('dve', kb2, a_, kac[:, p:p + 1], omka[:, p:p + 1], op0=ALU.mult, op1=ALU.add)
            tt('dve', kb2, kb2, k_p, ALU.mult)
            yield
            for (g0, gn, gC, gi0) in B.groups:
                cp('pool', eglR[:, p, gi0:gi0 + gn],
                   eg[:, g0:g0 + gn * gC].rearrange("p (n c) -> p n c", c=gC)[:, :, gC - 1])
            for e_ in range(2):
                hsl = slice(64 * e_, 64 * e_ + 64)
                tt('dve', bg2[hsl, 2 * p + e_, 0:nt], kk[hsl], egp[hsl], ALU.mult)
            ka = wlog
            tt('pool', ka, kk, a_, ALU.mult)
            tt('dve', kg32, kb2, eng, ALU.mult)
            yield
            ag32 = gc
            stt('dve', ag32, ka, -1.0, eng, ALU.mult, ALU.mult)
            cp('act', kgT[:, p, 0:nt], kg32)
            for e_ in range(2):
                hsl = slice(64 * e_, 64 * e_ + 64)
                tt('dve', rg2[hsl, 2 * p + e_, 0:nt], r_p[hsl], eg[hsl], ALU.mult)
            cp('act', rvT[:, p, 0:nt], v_p)
            yield
            cp('act', agT[:, p, 0:nt], ag32)
            for (g0, gn, gC, gi0) in B.groups:
                ebc = eglR[:, p, gi0:gi0 + gn].unsqueeze(2).to_broadcast([128, gn, gC])
                for src32, dstT in ((ag32, atT), (kg32, ktT)):
                    tt('dve', dstT[:, p, g0:g0 + gn * gC].rearrange("p (n c) -> p n c", c=gC),
                       src32[:, g0:g0 + gn * gC].rearrange("p (n c) -> p n c", c=gC), ebc, ALU.mult)
            prod = egp
            stt('dve', prod, r_p, rkc[:, p:p + 1], kb2, ALU.mult, ALU.mult)
            yield
            psb_ = pgbank()
            mm(psb_[:, 0:nt], bones[:], prod)
            tt('dve', bonT[:, p, 0:nt], psb_[:, 0:nt], v_p, ALU.mult)

        run_pipelined((pair_gen(p) for p in range(8)), 2)
        wth = {}
        def zb_tile(ti):
            if ti % 4 == 0:
                wth[ti // 4] = wload(WG['zb'][:, :, ti * 128:ti * 128 + 512])
            ps = proj(wth[ti // 4], ti % 4, ncols)
            if ti % 2 == 0:
                zr, z1, z2 = cin, T[14], T[13]
            else:
                zr, z1, z2 = TX[0], TX[13], TX[12]
            mixed(ps, 25 + ti, z1[:, 0:nt], zr)
            yield
            sigmoid_to(z2[:, 0:nt], z1[:, 0:nt], z2[:, 0:nt])
            yield
            tt('dve', zbT[:, ti, 0:nt], z1[:, 0:nt], z2[:, 0:nt], ALU.mult)

        run_pipelined((zb_tile(ti) for ti in range(8)), 2)

    def rwkv_chunk(B, ci, c0, C):
        chk = CHK[0]
        bgT, agT, kgT, rgT, atT, ktT, rvT, bonT, zbT, ynT = BT
        cs = slice(c0, c0 + C)

        def hv3(ap2, w):
            return ap2.rearrange("p (h c) -> p h c", c=w)
        import os as _os3
        _kv = int(_os3.environ.get('KVAR', '3'))
        for k_, (src, dst) in enumerate(((rvT, Vtm), (atT, atl), (ktT, ktlR))[:_kv]):
            pk = pbbank()
            pkv = hv3(pk[0:C, :], 128)
            for p in range(8):
                tr(pkv[:, p, :], src[:, p, cs], identb[:])
            cp('act' if k_ == 0 else 'dve', dst[0:C], pkv)
        Vv = Vtm[0:C].rearrange("p a (e v) -> p (a e) v", e=2)
        chk(6.1)
        def half_gen(half):
            TS = RSET[half]
            heads = list(range(half * 8, half * 8 + 8))
            AakT_, AqbT_, AqkT_, RHS_, Urw_ = TS['AakT'], TS['AqbT'], TS['AqkT'], TS['RHS'], TS['Urw']

            def hop(T_, h):
                if T_ is bg2 or T_ is rg2:
                    return T_[:, h, cs]
                return T_[:, h // 2, cs]

            def score(lt, rt, mask, dst):
                ps_ = pgbank()
                psv = hv3(ps_[0:C, 0:8 * C], C)
                for hi, h in enumerate(heads):
                    mm(psv[:, hi, :], hop(lt, h), hop(rt, h))
                tt('dve', dst, psv, mask[0:C, 0:C].unsqueeze(1).to_broadcast([C, 8, C]), ALU.mult)
            N0 = TS['Nm'][0][0:C, :, 0:C]
            NT0 = TS['NTm'][0][0:C, :, 0:C]
            score(bg2, agT, m_st, N0)
            score(agT, bg2, m_stT, NT0)
            yield
            score(kgT, bg2, m_stT, AakT_[0:C, :, 0:C])
            score(agT, rg2, m_inT, AqbT_[0:C, :, 0:C])
            score(kgT, rg2, m_inT, AqkT_[0:C, :, 0:C])
            yield
            res = {}
            for _ in tri_inverse_gen(N0, NT0, C, res, TS['Nm'], TS['NTm'], TS['RT']):
                yield
            RTf = res['RT']
            p1 = pgbank()
            p1v = hv3(p1[0:C, 0:8 * 64], 64)
            for hi, h in enumerate(heads):
                p = h // 2
                mm(p1v[:, hi, :], hop(bg2, h), Mrb[:, p, :], start=True, stop=False)
                mm(p1v[:, hi, :], AakT_[0:C, hi, 0:C], Vv[:, h, :], start=False, stop=True)
            cp('act', RHS_[0:C], p1v)
            yield
            p2 = pgbank()
            p2v = hv3(p2[0:C, 0:8 * 64], 64)
            for hi, h in enumerate(heads):
                mm(p2v[:, hi, :], RTf[:, hi, :], RHS_[0:C, hi, :])
            cp('act', Urw_[0:C], p2v)
            yield
            p3 = pgbank()
            p3v = hv3(p3[0:C, 0:8 * 64], 64)
            for hi, h in enumerate(heads):
                p = h // 2
                mm(p3v[:, hi, :], hop(rg2, h), Mrb[:, p, :], start=True, stop=False)
                mm(p3v[:, hi, :], AqbT_[0:C, hi, 0:C], Urw_[0:C, hi, :], start=False, stop=False)
                mm(p3v[:, hi, :], AqkT_[0:C, hi, 0:C], Vv[:, h, :], start=False, stop=True)
            ysv = ysq[0:C].rearrange("p a (b v) -> p (a b) v", b=2)
            Ysb = tmpA[0:C]
            cp('act', Ysb, p3v)
            rsum(yst[0:C, :, 0:1], Ysb)
            act(ysv, Ysb, AF.Square)
            rsum(yst[0:C, :, 1:2], ysv)
            ts('dve', yst[0:C, :, 0:1], yst[0:C, :, 0:1], 1.0 / 64)
            tt('dve', yst[0:C, :, 2:3], yst[0:C, :, 0:1], yst[0:C, :, 0:1], ALU.mult)
            stt('dve', yst[0:C, :, 1:2], yst[0:C, :, 1:2], 1.0 / 64, yst[0:C, :, 2:3], ALU.mult, ALU.subtract)
            ts('dve', yst[0:C, :, 1:2], yst[0:C, :, 1:2], GN_EPS, op0=ALU.add)
            act(yst[0:C, :, 3:4], yst[0:C, :, 1:2], AF.Ln)
            act(yst[0:C, :, 3:4], yst[0:C, :, 3:4], AF.Exp, scale=-0.5)
            tt('dve', ysv, Ysb, yst[0:C, :, 0:1].to_broadcast([C, 8, 64]), ALU.subtract)
            ynv8 = ynb[0:C, 0:4, :].rearrange("p a (b v) -> p (a b) v", b=2)
            tt('dve', ynv8, ysv, yst[0:C, :, 3:4].to_broadcast([C, 8, 64]), ALU.mult)
            po = pbbank()
            pov = hv3(po[:, 0:4 * C], C)
            for a in range(4):
                tr(pov[:, a, :], ynb[0:C, a, :], identb[0:C, 0:C])
            cp('act', ynT[:, half * 4:half * 4 + 4, cs], pov)
            yield
            pA = pgbank()
            pAv = hv3(pA[:, 0:4 * 64], 64)
            pBk = pgbank()
            pBv = hv3(pBk[:, 0:4 * 64], 64)
            for a in range(4):
                p = half * 4 + a
                for e, pv in ((0, pAv), (1, pBv)):
                    hi = 2 * a + e
                    h = 2 * p + e
                    mm(pv[:, a, :], atl[0:C, p, :], Urw_[0:C, hi, :], start=True, stop=False)
                    mm(pv[:, a, :], ktlR[0:C, p, :], Vv[:, h, :], start=False, stop=True)
            ps_ = slice(half * 4, half * 4 + 4)
            for e, pv in ((0, pAv), (1, pBv)):
                rows = slice(64 * e, 64 * e + 64)
                tt('dve', Mr[rows, ps_, :], Mr[rows, ps_, :],
                   eglR[rows, ps_, ci:ci + 1].to_broadcast([64, 4, 64]), ALU.mult)
                tt('dve', Mr[rows, ps_, :], Mr[rows, ps_, :], pv[rows], ALU.add)
            cp('act', Mrb[:, ps_, :], Mr[:, ps_, :])

        run_pipelined((half_gen(hf) for hf in range(2)), 2)

    def out_stage(B):
        nt = B.ntok
        bonT, zbT, ynT = BT[7], BT[8], BT[9]
        for p in range(8):
            ob_t = T[p % 2]
            ts('dve', ob_t[:, 0:nt], ynT[:, p, 0:nt], gnwc[:, p:p + 1], gnbc[:, p:p + 1], op0=ALU.mult, op1=ALU.add)
            tt('dve', ob_t[:, 0:nt], ob_t[:, 0:nt], bonT[:, p, 0:nt], ALU.add)
            tt('dve', obT[:, p, 0:nt], ob_t[:, 0:nt], zbT[:, p, 0:nt], ALU.mult)
        wgh = {}

        def gate_tile(bi, src, gcol, ti):
            if ti % 4 == 0:
                wgh[(bi, ti // 4)] = (wload(WG[gcol][:, :, ti * 128:ti * 128 + 512]),
                                      wload(Wo[bi][:, :, ti * 128:ti * 128 + 512]))
            wg, wo = wgh[(bi, ti // 4)]
            psg = proj(wg, ti % 4, nt)
            G1 = T[1] if ti % 2 == 0 else T[3]
            G2 = T[2] if ti % 2 == 0 else T[4]
            pbr = pgbank()
            for ec in range(8):
                mm(pbr[:, 0:nt], wo[:, ec, (ti % 4) * 128:(ti % 4 + 1) * 128], src[:, ec, 0:nt], start=(ec == 0), stop=(ec == 7))
            yield
            sigmoid_to(G1[:, 0:nt], psg[:, 0:nt], G1[:, 0:nt])
            yield
            if bi == 0:
                tt('dve', MG[:, ti, 0:nt], pbr[:, 0:nt], G1[:, 0:nt], ALU.mult)
            else:
                tt('dve', G2[:, 0:nt], pbr[:, 0:nt], G1[:, 0:nt], ALU.mult)
                tt('dve', mgT[:, ti, 0:nt], G2[:, 0:nt], MG[:, ti, 0:nt], ALU.add)

        for bi, (src, gcol) in enumerate(((oaT, 'ga'), (obT, 'gb'))):
            run_pipelined((gate_tile(bi, src, gcol, ti) for ti in range(8)), 2)
        wos = [wload(Wo[2][:, :, 0:512]), wload(Wo[2][:, :, 512:1024])]
        for ti, (r0, n) in enumerate(B.tiles):
            load_x_tile(B, r0, n, hrow)
            for half in range(2):
                px = ppbank()
                for dc in range(8):
                    mm(px[0:n, :], mgT[:, dc, r0:r0 + n], wos[half][:, dc, :], start=(dc == 0), stop=(dc == 7))
                tt('dve', xrow[0:n, half * 512:(half + 1) * 512], px[0:n, :], hrow[0:n, half * 512:(half + 1) * 512], ALU.add)
            rms_rows(xrow, n, hrow, lnfbc)
            if B.kind == 'p':
                if B.idx == 0 and r0 == 0:
                    dma('sp', yp[0:n - NMETA, :], hrow[NMETA:n, :], 'out')
                else:
                    s0 = r0 + 256 * B.idx - (NMETA if B.idx == 0 else 0)
                    dma('sp', yp[s0:s0 + n, :], hrow[0:n, :], 'out')
            else:
                dma('sp', ys[0:n, :], hrow[0:n, :], 'out')

    def load_gdn_state(s):
        dma('sp', Sg[:], sg_in[s].rearrange("h k v -> k h v"), 'st')
        cp('pool', Sgb[:], Sg[:])

    def store_gdn_state(dst):
        dma('sp', dst.rearrange("h k v -> k h v"), Sg[:], 'out')

    def load_rwkv_state(s):
        dma('sp', strw, sr_in[s].rearrange("h v k -> v h k"), 'st')
        for p in range(8):
            pt = pgbank()
            tr(pt[:, 0:64], strw[:, 2 * p:2 * p + 2, :].rearrange("p a b -> p (a b)"), ident[0:64, 0:64])
            cp('act', Mr[:, p, :], pt[:, 0:64])
        cp('pool', Mrb[:], Mr[:])

    def store_rwkv_state(dst):
        for p in range(8):
            pt = pgbank()
            tr(pt[0:64, 0:128], Mr[:, p, :], ident[:])
            cp('act', strw[:, 2 * p:2 * p + 2, :].rearrange("p a b -> p (a b)"), pt[0:64, 0:128])
        dma('sp', dst.rearrange("h v k -> v h k"), strw, 'out')

    def store_conv(B, dst):
        n3 = 3 * B.nseg
        for q in range(4):
            for j in range(6):
                ft = q * 6 + j
                pt = pgbank()
                tr(pt[0:n3, 0:128], hist[:, ft, 0:n3], ident[:])
                cp('act' if j % 2 == 0 else 'dve', cvt[0:n3, j * 128:(j + 1) * 128], pt[0:n3, 0:128])
            dma('sp', dst[:, q * 768:(q + 1) * 768], cvt[0:n3, :], 'out')

    def load_conv_hist(B):
        n3 = 3 * B.nseg
        for q in range(4):
            dma('sp', cvt[0:n3, :], sc_in[:, q * 768:(q + 1) * 768], 'st')
            for j in range(6):
                ft = q * 6 + j
                pt = pgbank()
                tr(pt[:, 0:n3], cvt[0:n3, j * 128:(j + 1) * 128], ident[0:n3, 0:n3])
                cp('act' if j % 2 == 0 else 'dve', hist[:, ft, 0:n3], pt[:, 0:n3])

    import os as _os
    KSTOP = float(_os.environ.get('KSTOP', '999'))

    class _Stop(Exception):
        pass

    def chk(n):
        if n > KSTOP:
            raise _Stop()

    CHK[0] = chk

    def main_prog():
      memset('pool', Sg[:], 0.0)
      memset('pool', Sgb[:], 0.0)
      memset('pool', Mr[:], 0.0)
      memset('pool', Mrb[:], 0.0)
      for B in blocks:
        chk(2)
        front(B)
        if B.kind == 's':
            load_conv_hist(B)
        chk(3)
        gdn_prep(B)
        ci = 0
        for seg in B.segs:
            if B.kind == 's':
                load_gdn_state(seg['sid'])
            for (c0, C) in seg['chunks']:
                chk(4)
                gdn_chunk(B, ci, c0, C)
                ci += 1
            if B.kind == 's':
                store_gdn_state(ngs[seg['sid']])
        if B.kind == 'p' and B.last:
            store_gdn_state(ngp)
        if B.last:
            store_conv(B, ncp if B.kind == 'p' else ncs)
        chk(5)
        rwkv_prep(B)
        ci = 0
        for seg in B.segs:
            if B.kind == 's':
                load_rwkv_state(seg['sid'])
            for (c0, C) in seg['chunks']:
                chk(6)
                rwkv_chunk(B, ci, c0, C)
                ci += 1
            if B.kind == 's':
                store_rwkv_state(nrs[seg['sid']])
        if B.kind == 'p' and B.last:
            store_rwkv_state(nrp)
        chk(7)
        out_stage(B)

    try:
        main_prog()
    except _Stop:
        pass
    if _os.environ.get('KDBG'):
        dbg = dout("dbg", [10, 128, 8, NCMAX])
        for i in range(9):
            for p in range(8):
                cp('dve', T[0][:, 0:NCMAX], BT[i][:, p, :])
                dma('sp', dbg[i, :, p, :], T[0][:, 0:NCMAX], 'out')

    S.emit(st)
    st.close()
    return nc


_NC_CACHE = {}


def make_in_maps(cfg, inputs, ncores):
    f = lambda a: np.ascontiguousarray(np.asarray(a, dtype=np.float32))
    ns = cfg.ns
    shared = {
        "meta": f(inputs["meta_tokens"]),
        "ln1_w": f(inputs["ln1_w"]).reshape(D),
        "w_in": f(inputs["w_in"]).reshape(D, DIN),
        "conv_w": f(inputs["gdn_conv_w"]).reshape(4, 3072),
        "a_log": f(inputs["gdn_a_log"]).reshape(8, 1),
        "dt_bias": f(inputs["gdn_dt_bias"]).reshape(8, 1),
        "gnorm_w": f(inputs["gdn_norm_w"]).reshape(128, 1),
        "w_out_a": f(inputs["w_out_a"]).reshape(D, D),
        "mu": f(inputs["rwkv_mu"]).reshape(33, 128),
        "w0": f(inputs["rwkv_w0"]).reshape(8, 128),
        "w2": f(inputs["rwkv_w2"]).reshape(64, D),
        "a0": f(inputs["rwkv_a0"]).reshape(8, 128),
        "a2": f(inputs["rwkv_a2"]).reshape(64, D),
        "k_k": f(inputs["rwkv_k_k"]).reshape(8, 128),
        "k_a": f(inputs["rwkv_k_a"]).reshape(8, 128),
        "r_k": f(inputs["rwkv_r_k"]).reshape(8, 128),
        "gn_w": f(inputs["rwkv_gn_w"]).reshape(8, 128),
        "gn_b": f(inputs["rwkv_gn_b"]).reshape(8, 128),
        "w_out_b": f(inputs["w_out_b"]).reshape(D, D),
        "w_out": f(inputs["w_out"]).reshape(D, D),
        "lnf_w": f(inputs["lnf_w"]).reshape(D),
    }
    maps = []
    for c in range(ncores):
        m = dict(shared)
        m["xp"] = f(inputs["x_prompt"][c])
        if ns > 0:
            sl = slice(c * ns, (c + 1) * ns)
            m["xs"] = f(inputs["x_sample"][sl]).reshape(ns * 4, D)
            m["sg"] = f(inputs["state_gdn"][0, sl])
            m["sc"] = f(inputs["state_gdn_conv"][0, sl]).reshape(ns * 3, 3072)
            m["sr"] = f(inputs["state_rwkv"][0, sl])
            m["ss"] = f(inputs["state_shift"][0, sl])
        maps.append(m)
    return maps


def gather(cfg, results, ncores):
    ns = cfg.ns
    cat = lambda k, shp: np.concatenate([np.asarray(r[k], dtype=np.float32).reshape(shp) for r in results], axis=0)
    y_prompt = cat("yp", (1, cfg.seq, D))
    ngp = cat("ngp", (1, 8, 128, 128))[None]
    ncp = cat("ncp", (1, 3, 3072))[None]
    nrp = cat("nrp", (1, 16, 64, 64))[None]
    nsp = cat("nsp", (1, D))[None]
    y_sample = cat("ys", (ns, 4, D))
    ngs = cat("ngs", (ns, 8, 128, 128))[None]
    ncs = cat("ncs", (ns, 3, 3072))[None]
    nrs = cat("nrs", (ns, 16, 64, 64))[None]
    nss = cat("nss", (ns, D))[None]
    return (y_prompt, y_sample, ngp, ncp, nrp, nsp, ngs, ncs, nrs, nss)


def kernel(**inputs):
    cfg = Cfg(8, 16)
    if 'nc' not in _NC_CACHE:
        _NC_CACHE['nc'] = build(cfg)
    nc = _NC_CACHE['nc']
    maps = make_in_maps(cfg, inputs, 8)
    res = run_bass_kernel_spmd(nc, maps, core_ids=list(range(8)))
    return gather(cfg, res.results, 8)
```
